# Optimizing a Trainium2 kernel written in Bass

```python
import math
import jax
import jax.numpy as jnp
from jax import lax
import numpy as np

D_MODEL = 1024
BATCH = 4
SEQ = 4096
DEPTH = 2

MEM_LEN = 256
EPS = 1e-6

MLA_HEADS = 4
MLA_Q_RANK = 256
MLA_KV_RANK = 128
MLA_NOPE = 128
MLA_ROPE = 64
MLA_V = 128
MLA_QK = MLA_NOPE + MLA_ROPE
ROPE_THETA = 10000.0
Q_BLOCK = 128

HG_HEADS = 4
HG_DK = 128
HG_DV = 128
HG_KW = HG_HEADS * HG_DK
HG_VW = HG_HEADS * HG_DV
HG_CHUNK = 64

S5_WIDTH = 512
S5_GROUP = 16
S5_GROUPS = S5_WIDTH // S5_GROUP
S5_STATE = 64

RW_HEADS = 8
RW_HEAD = 64
RW_WIDTH = RW_HEADS * RW_HEAD
RW_DECAY_LORA = 64
RW_AAA_LORA = 64
RW_GATE_LORA = 128
RW_MV_LORA = 32
RW_GN_EPS = 64e-5
RW_COLS = 3 * RW_WIDTH + RW_DECAY_LORA + RW_AAA_LORA + RW_GATE_LORA

N_BRANCH = 4
BRANCH_WIDTH = 512

X_HEADS = 4
X_HEAD_DIM = D_MODEL // X_HEADS

D_FF = 2816
CONV_W = 3

COL_SIZES = (MLA_Q_RANK, MLA_KV_RANK, MLA_ROPE,
             HG_KW, HG_KW, HG_VW, HG_VW,
             S5_WIDTH,
             RW_COLS,
             N_BRANCH * D_MODEL)
P_IN = MLA_Q_RANK + MLA_KV_RANK + MLA_ROPE + 2 * HG_KW + 2 * HG_VW + S5_WIDTH + RW_COLS + N_BRANCH * D_MODEL
RW_SIZES = (RW_WIDTH, RW_WIDTH, RW_WIDTH, RW_DECAY_LORA, RW_AAA_LORA, RW_GATE_LORA)

kernel_name = "hybrid_gated_mla_hgrn2_s5_rwkv7_block"


def split_cols(t, sizes):
    out, start = [], 0
    for n in sizes:
        out.append(t[..., start:start + n])
        start += n
    return out


def rmsnorm(x, g, eps=EPS):
    xf = x.astype(jnp.float32)
    y = xf * lax.rsqrt(jnp.mean(xf * xf, axis=-1, keepdims=True) + eps)
    return (y * g.astype(jnp.float32)).astype(x.dtype)


def rope_tables(positions):
    half = MLA_ROPE // 2
    inv_freq = ROPE_THETA ** (-jnp.arange(half, dtype=jnp.float32) / half)
    ang = positions.astype(jnp.float32)[..., None] * inv_freq
    return jnp.cos(ang), jnp.sin(ang)


def apply_rope(t, cos, sin):
    half = t.shape[-1] // 2
    t1, t2 = t[..., :half], t[..., half:]
    cos = cos.astype(t.dtype)
    sin = sin.astype(t.dtype)
    return jnp.concatenate([t1 * cos - t2 * sin, t1 * sin + t2 * cos], axis=-1)


def mla(cq, ckv, k_rope, cos, sin, q_norm, w_uq, kv_norm, w_ukv):
    B, S, _ = cq.shape
    H = MLA_HEADS
    q = (rmsnorm(cq, q_norm) @ w_uq).reshape(B, S, H, MLA_QK)
    kv = (rmsnorm(ckv, kv_norm) @ w_ukv).reshape(B, S, H, MLA_NOPE + MLA_V)
    q = jnp.concatenate([q[..., :MLA_NOPE],
                         apply_rope(q[..., MLA_NOPE:], cos[:, :, None], sin[:, :, None])], axis=-1)
    k_pe = apply_rope(k_rope, cos, sin)
    k = jnp.concatenate([kv[..., :MLA_NOPE],
                         jnp.broadcast_to(k_pe[:, :, None, :], (B, S, H, MLA_ROPE))], axis=-1)
    v = kv[..., MLA_NOPE:]
    q, k, v = (t.transpose(0, 2, 1, 3) for t in (q, k, v))
    nb = S // Q_BLOCK
    q_blocks = q.reshape(B, H, nb, Q_BLOCK, MLA_QK).transpose(2, 0, 1, 3, 4)
    k_pos = jnp.arange(S)
    scale = MLA_QK ** -0.5

    def one_block(args):
        qb, i = args
        s = jnp.einsum('bhqd,bhkd->bhqk', qb, k).astype(jnp.float32) * scale
        q_pos = i * Q_BLOCK + jnp.arange(Q_BLOCK)
        s = jnp.where(k_pos[None, :] <= q_pos[:, None], s, -jnp.inf)
        p = jax.nn.softmax(s, axis=-1).astype(v.dtype)
        return jnp.einsum('bhqk,bhkd->bhqd', p, v)

    o = lax.map(one_block, (q_blocks, jnp.arange(nb)))
    return o.transpose(1, 0, 3, 2, 4).reshape(B, S, H * MLA_V)


def hgrn2(qc, fc, ic, gc, lb, o_norm):
    B, S, _ = qc.shape
    H, C = HG_HEADS, HG_CHUNK
    f = lb + (1.0 - lb) * jax.nn.sigmoid(fc.astype(jnp.float32))
    logf = jnp.log(f)
    k = 1.0 - f

    def chunks(t, d):
        return t.astype(jnp.float32).reshape(B, S // C, C, H, d).transpose(1, 0, 3, 2, 4)

    xs = (chunks(qc, HG_DK), chunks(k, HG_DK), chunks(ic, HG_DV), chunks(logf, HG_DK))
    causal = jnp.tril(jnp.ones((C, C), dtype=bool))

    def step(state, inp):
        q, kk, v, lf = inp
        b = jnp.cumsum(lf, axis=2)
        diff = jnp.where(causal[:, :, None], b[:, :, :, None, :] - b[:, :, None, :, :], -jnp.inf)
        att = jnp.einsum('bhtk,bhsk,bhtsk->bhts', q, kk, jnp.exp(diff))
        o = (jnp.einsum('bhts,bhsv->bhtv', att, v)
             + jnp.einsum('bhtk,bhkv->bhtv', q * jnp.exp(b), state))
        b_last = b[:, :, -1:, :]
        state = (state * jnp.exp(b_last)[:, :, 0, :, None]
                 + jnp.einsum('bhsk,bhsv->bhkv', kk * jnp.exp(b_last - b), v))
        return state, o

    s0 = jnp.zeros((B, H, HG_DK, HG_DV), jnp.float32)
    _, o = lax.scan(step, s0, xs)
    o = o.transpose(1, 0, 3, 2, 4).reshape(B, S, H, HG_DV)
    o = o * lax.rsqrt(jnp.mean(o * o, axis=-1, keepdims=True) + EPS) * o_norm.astype(jnp.float32).reshape(H, HG_DV)
    o = o.reshape(B, S, HG_VW) * jax.nn.sigmoid(gc.astype(jnp.float32))
    return o.astype(qc.dtype)


def s5(u, A_re, A_im, log_step, B_re, B_im, C_re, C_im, D, w_glu, b_glu):
    Bn, S, _ = u.shape
    G, N = S5_GROUPS, S5_STATE
    f32 = jnp.float32
    uf = u.astype(f32).reshape(Bn, S, G, S5_GROUP)
    a_re = jnp.minimum(A_re.astype(f32), -1e-4)
    a_im = A_im.astype(f32)
    dt = jnp.exp(log_step.astype(f32))[:, None]
    mag = jnp.exp(dt * a_re)
    ab_re = mag * jnp.cos(dt * a_im)
    ab_im = mag * jnp.sin(dt * a_im)
    den = a_re * a_re + a_im * a_im
    z_re = ((ab_re - 1.0) * a_re + ab_im * a_im) / den
    z_im = (ab_im * a_re - (ab_re - 1.0) * a_im) / den
    Br, Bi = B_re.astype(f32), B_im.astype(f32)
    bb_re = z_re[..., None] * Br - z_im[..., None] * Bi
    bb_im = z_re[..., None] * Bi + z_im[..., None] * Br
    bu_re = jnp.einsum('bsgc,gnc->bsgn', uf, bb_re)
    bu_im = jnp.einsum('bsgc,gnc->bsgn', uf, bb_im)
    a_re_s = jnp.broadcast_to(ab_re[None, None], (Bn, S, G, N))
    a_im_s = jnp.broadcast_to(ab_im[None, None], (Bn, S, G, N))

    def combine(e1, e2):
        a1r, a1i, b1r, b1i = e1
        a2r, a2i, b2r, b2i = e2
        return (a2r * a1r - a2i * a1i,
                a2r * a1i + a2i * a1r,
                a2r * b1r - a2i * b1i + b2r,
                a2r * b1i + a2i * b1r + b2i)

    _, _, x_re, x_im = lax.associative_scan(combine, (a_re_s, a_im_s, bu_re, bu_im), axis=1)
    y = (jnp.einsum('bsgn,gcn->bsgc', x_re, C_re.astype(f32))
         - jnp.einsum('bsgn,gcn->bsgc', x_im, C_im.astype(f32)))
    y = y.reshape(Bn, S, S5_WIDTH) + D.astype(f32) * u.astype(f32)
    z = jax.nn.gelu(y)
    out = z * jax.nn.sigmoid(z @ w_glu.astype(f32) + b_glu.astype(f32))
    return out.astype(u.dtype)


def token_shift_mix(t, mu):
    prev = jnp.pad(t, ((0, 0), (1, 0), (0, 0)))[:, :-1]
    return t + (prev - t) * mu


def rwkv7(m, h, v_first, vres, w0, w_up, a0, a_up, g_up, k_k, k_a, r_k, ln_w, ln_b):
    B, S, _ = m.shape
    f32 = jnp.float32
    r, k, v, wd, ad, gd = split_cols(m, RW_SIZES)
    w_log = -jax.nn.softplus(-(w0 + jnp.tanh(wd) @ w_up)) - 0.5
    decay = jnp.exp(-jnp.exp(w_log.astype(f32)))
    a = jax.nn.sigmoid(a0 + ad @ a_up)
    g = jax.nn.sigmoid(gd) @ g_up
    if vres is not None:
        v_down, v_up, v_bias = vres
        v = v + (v_first - v) * jax.nn.sigmoid(v_bias + (h @ v_down) @ v_up)
    v_out = v

    def heads(t):
        return t.astype(f32).reshape(B, S, RW_HEADS, RW_HEAD)

    kk = heads(k * k_k)
    kk = kk * lax.rsqrt(jnp.sum(kk * kk, axis=-1, keepdims=True) + 1e-12)
    a_h = heads(a)
    k_a_h = k_a.astype(f32).reshape(RW_HEADS, RW_HEAD)
    k_h = heads(k) * (1.0 + (a_h - 1.0) * k_a_h)
    r_h, v_h, w_h = heads(r), heads(v), heads(decay)

    def step(st, inp):
        r_t, k_t, v_t, w_t, kk_t, a_t = inp
        sa = jnp.einsum('bhvk,bhk->bhv', st, -kk_t)
        st = (st * w_t[:, :, None, :] + sa[..., None] * (kk_t * a_t)[:, :, None, :]
              + v_t[..., None] * k_t[:, :, None, :])
        return st, jnp.einsum('bhvk,bhk->bhv', st, r_t)

    xs = tuple(t.transpose(1, 0, 2, 3) for t in (r_h, k_h, v_h, w_h, kk, a_h))
    s0 = jnp.zeros((B, RW_HEADS, RW_HEAD, RW_HEAD), f32)
    _, y = lax.scan(step, s0, xs)
    y = y.transpose(1, 0, 2, 3)
    mu = jnp.mean(y, axis=-1, keepdims=True)
    var = jnp.mean((y - mu) ** 2, axis=-1, keepdims=True)
    y = ((y - mu) * lax.rsqrt(var + RW_GN_EPS)).reshape(B, S, RW_WIDTH) * ln_w.astype(f32) + ln_b.astype(f32)
    bonus = jnp.sum(r_h * k_h * r_k.astype(f32), axis=-1, keepdims=True) * v_h
    y = (y + bonus.reshape(B, S, RW_WIDTH)) * g.astype(f32)
    return y.astype(m.dtype), v_out


def cross_attn(hq, hm, w_q, w_kv, w_o):
    B, S, D = hq.shape
    M = hm.shape[1]
    q = (hq @ w_q).reshape(B, S, X_HEADS, X_HEAD_DIM)
    kv = (hm @ w_kv).reshape(B, M, 2, X_HEADS, X_HEAD_DIM)
    k, v = kv[:, :, 0], kv[:, :, 1]
    s = jnp.einsum('bshd,bmhd->bhsm', q, k).astype(jnp.float32) * (X_HEAD_DIM ** -0.5)
    p = jax.nn.softmax(s, axis=-1).astype(v.dtype)
    o = jnp.einsum('bhsm,bmhd->bshd', p, v).reshape(B, S, D)
    return o @ w_o


def conv_ffn(h, w_gate_up, conv_w, conv_b, w_down):
    gu = h @ w_gate_up
    gate, up = gu[..., :D_FF], gu[..., D_FF:]
    gate = lax.conv_general_dilated(gate, conv_w[:, None, :].astype(gate.dtype), window_strides=(1,),
                                    padding=[(CONV_W - 1, 0)], dimension_numbers=('NWC', 'WIO', 'NWC'),
                                    feature_group_count=D_FF) + conv_b
    return (jax.nn.silu(gate) * up) @ w_down


def setup_inputs(seed: int = 0) -> dict:
    key = jax.random.key(seed)
    ks = list(jax.random.split(key, 64))

    def nk():
        return ks.pop()

    def nrm(shape, scale):
        return scale * jax.random.normal(nk(), shape, jnp.float32)

    def gain(shape):
        return 1.0 + nrm(shape, 0.02)

    L, D = DEPTH, D_MODEL
    x = nrm((BATCH, SEQ, D), 1.0)
    mem = nrm((BATCH, MEM_LEN, D), 1.0)
    offsets = jax.random.randint(nk(), (BATCH, 1), 0, 1024, dtype=jnp.int32)
    positions = offsets + jnp.arange(SEQ, dtype=jnp.int32)[None, :]
    n_idx = jnp.arange(S5_STATE, dtype=jnp.float32)
    return {
        "x": x,
        "mem": mem,
        "positions": positions,
        "norm_mix": gain((L, D)),
        "w_in": nrm((L, D, P_IN), D ** -0.5),
        "mla_q_norm": gain((L, MLA_Q_RANK)),
        "mla_w_uq": nrm((L, MLA_Q_RANK, MLA_HEADS * MLA_QK), MLA_Q_RANK ** -0.5),
        "mla_kv_norm": gain((L, MLA_KV_RANK)),
        "mla_w_ukv": nrm((L, MLA_KV_RANK, MLA_HEADS * (MLA_NOPE + MLA_V)), MLA_KV_RANK ** -0.5),
        "hgrn_lb_logits": nrm((L, HG_KW), 0.1),
        "hgrn_o_norm": gain((L, HG_VW)),
        "s5_A_re": -0.5 + nrm((L, S5_GROUPS, S5_STATE), 0.01),
        "s5_A_im": math.pi * n_idx + nrm((L, S5_GROUPS, S5_STATE), 0.01),
        "s5_log_step": jax.random.uniform(nk(), (L, S5_GROUPS), jnp.float32, math.log(1e-3), math.log(1e-1)),
        "s5_B_re": nrm((L, S5_GROUPS, S5_STATE, S5_GROUP), S5_GROUP ** -0.5),
        "s5_B_im": nrm((L, S5_GROUPS, S5_STATE, S5_GROUP), S5_GROUP ** -0.5),
        "s5_C_re": nrm((L, S5_GROUPS, S5_GROUP, S5_STATE), 0.25),
        "s5_C_im": nrm((L, S5_GROUPS, S5_GROUP, S5_STATE), 0.25),
        "s5_D": nrm((L, S5_WIDTH), 1.0),
        "s5_w_glu": nrm((L, S5_WIDTH, S5_WIDTH), S5_WIDTH ** -0.5),
        "s5_b_glu": nrm((L, S5_WIDTH), 0.01),
        "rwkv_mu": jax.random.uniform(nk(), (L, RW_COLS), jnp.float32),
        "rwkv_w0": jax.random.uniform(nk(), (L, RW_WIDTH), jnp.float32, -5.0, 1.0),
        "rwkv_w_up": nrm((L, RW_DECAY_LORA, RW_WIDTH), 0.5 * RW_DECAY_LORA ** -0.5),
        "rwkv_a0": nrm((L, RW_WIDTH), 0.1),
        "rwkv_a_up": nrm((L, RW_AAA_LORA, RW_WIDTH), 0.5 * RW_AAA_LORA ** -0.5),
        "rwkv_g_up": nrm((L, RW_GATE_LORA, RW_WIDTH), RW_GATE_LORA ** -0.5),
        "rwkv_k_k": 0.85 + nrm((L, RW_WIDTH), 0.02),
        "rwkv_k_a": gain((L, RW_WIDTH)),
        "rwkv_r_k": nrm((L, RW_HEADS, RW_HEAD), 0.1),
        "rwkv_ln_w": gain((L, RW_WIDTH)),
        "rwkv_ln_b": nrm((L, RW_WIDTH), 0.01),
        "rwkv_vres_down": nrm((L - 1, D, RW_MV_LORA), D ** -0.5),
        "rwkv_vres_up": nrm((L - 1, RW_MV_LORA, RW_WIDTH), RW_MV_LORA ** -0.5),
        "rwkv_vres_bias": nrm((L - 1, RW_WIDTH), 0.1),
        "w_branch_mla": nrm((L, BRANCH_WIDTH, D), BRANCH_WIDTH ** -0.5),
        "w_branch_hgrn": nrm((L, BRANCH_WIDTH, D), BRANCH_WIDTH ** -0.5),
        "w_branch_s5": nrm((L, BRANCH_WIDTH, D), BRANCH_WIDTH ** -0.5),
        "w_branch_rwkv": nrm((L, BRANCH_WIDTH, D), BRANCH_WIDTH ** -0.5),
        "w_out": nrm((L, D, D), D ** -0.5),
        "norm_xq": gain((L, D)),
        "norm_xm": gain((L, D)),
        "xattn_w_q": nrm((L, D, D), D ** -0.5),
        "xattn_w_kv": nrm((L, D, 2 * D), D ** -0.5),
        "xattn_w_o": nrm((L, D, D), D ** -0.5),
        "norm_ffn": gain((L, D)),
        "ffn_w_gate_up": nrm((L, D, 2 * D_FF), D ** -0.5),
        "ffn_conv_w": nrm((L, CONV_W, D_FF), CONV_W ** -0.5),
        "ffn_conv_b": nrm((L, D_FF), 0.01),
        "ffn_w_down": nrm((L, D_FF, D), D_FF ** -0.5),
        "norm_final": gain((D,)),
    }


def reference(x, mem, positions, norm_mix, w_in, mla_q_norm, mla_w_uq, mla_kv_norm, mla_w_ukv,
              hgrn_lb_logits, hgrn_o_norm,
              s5_A_re, s5_A_im, s5_log_step, s5_B_re, s5_B_im, s5_C_re, s5_C_im, s5_D, s5_w_glu, s5_b_glu,
              rwkv_mu, rwkv_w0, rwkv_w_up, rwkv_a0, rwkv_a_up, rwkv_g_up, rwkv_k_k, rwkv_k_a, rwkv_r_k,
              rwkv_ln_w, rwkv_ln_b, rwkv_vres_down, rwkv_vres_up, rwkv_vres_bias,
              w_branch_mla, w_branch_hgrn, w_branch_s5, w_branch_rwkv, w_out,
              norm_xq, norm_xm, xattn_w_q, xattn_w_kv, xattn_w_o,
              norm_ffn, ffn_w_gate_up, ffn_conv_w, ffn_conv_b, ffn_w_down, norm_final):
    B, S, D = x.shape
    lb_p = jax.nn.softmax(hgrn_lb_logits.astype(jnp.float32), axis=0)
    lb_c = jnp.cumsum(lb_p, axis=0)
    lower_bounds = lb_c - lb_c[0:1]
    cos, sin = rope_tables(positions)
    v_first = None
    for l in range(DEPTH):
        h = rmsnorm(x, norm_mix[l])
        p = h @ w_in[l]
        cq, ckv, krope, hq, hf, hi, hg, su, rw, gate_logits = split_cols(p, COL_SIZES)
        o_mla = mla(cq, ckv, krope, cos, sin, mla_q_norm[l], mla_w_uq[l], mla_kv_norm[l], mla_w_ukv[l])
        o_hg = hgrn2(hq, hf, hi, hg, lower_bounds[l], hgrn_o_norm[l])
        o_s5 = s5(su, s5_A_re[l], s5_A_im[l], s5_log_step[l], s5_B_re[l], s5_B_im[l],
                  s5_C_re[l], s5_C_im[l], s5_D[l], s5_w_glu[l], s5_b_glu[l])
        vres = None if l == 0 else (rwkv_vres_down[l - 1], rwkv_vres_up[l - 1], rwkv_vres_bias[l - 1])
        o_rw, v_l = rwkv7(token_shift_mix(rw, rwkv_mu[l]), h, v_first, vres,
                          rwkv_w0[l], rwkv_w_up[l], rwkv_a0[l], rwkv_a_up[l], rwkv_g_up[l],
                          rwkv_k_k[l], rwkv_k_a[l], rwkv_r_k[l], rwkv_ln_w[l], rwkv_ln_b[l])
        if l == 0:
            v_first = v_l
        gates = jax.nn.sigmoid(gate_logits).reshape(B, S, N_BRANCH, D)
        y = (gates[:, :, 0] * (o_mla @ w_branch_mla[l])
             + gates[:, :, 1] * (o_hg @ w_branch_hgrn[l])
             + gates[:, :, 2] * (o_s5 @ w_branch_s5[l])
             + gates[:, :, 3] * (o_rw @ w_branch_rwkv[l]))
        x = x + y @ w_out[l]
        x = x + cross_attn(rmsnorm(x, norm_xq[l]), rmsnorm(mem, norm_xm[l]),
                           xattn_w_q[l], xattn_w_kv[l], xattn_w_o[l])
        x = x + conv_ffn(rmsnorm(x, norm_ffn[l]), ffn_w_gate_up[l], ffn_conv_w[l], ffn_conv_b[l], ffn_w_down[l])
    return rmsnorm(x, norm_final)
```

```python
import numpy as np
import concourse.bass as bass
import concourse.mybir as mybir
from concourse.bass_utils import run_bass_kernel_spmd
from contextlib import ExitStack

F32 = mybir.dt.float32
BF16 = mybir.dt.bfloat16
I32 = mybir.dt.int32
AF = mybir.ActivationFunctionType
ALU = mybir.AluOpType

T = 4096
D = 1024
TB = 512
NB = T // TB
DEPTH = 2
EPS = 1e-6
P_IN = 8896
NMIX = 4800
D_FF = 2816


class Buf:
    __slots__ = ("lw", "rd")

    def __init__(self):
        self.lw = None
        self.rd = {}


class V:
    def __init__(self, ap, buf):
        self.ap = ap
        self.buf = buf

    def __getitem__(self, k):
        return V(self.ap[k], self.buf)

    def re(self, pat, **kw):
        return V(self.ap.rearrange(pat, **kw), self.buf)

    def bc(self, shape):
        return V(self.ap.to_broadcast(list(shape)), self.buf)

    def pbc(self, n):
        return V(self.ap.partition_broadcast(n), self.buf)

    def us(self, axis):
        return V(self.ap.unsqueeze(axis), self.buf)

    def bitcast(self, dt):
        return V(self.ap.bitcast(dt), self.buf)

    def fresh(self):
        return V(self.ap, Buf())

    @property
    def shape(self):
        return self.ap.shape


OUT_KEYS = ("out", "accum_out")
ENGS = ("pe", "act", "dve", "pool", "sp")
EPOCH = 30000
NDMA = 56
SAME_ENGINE_SYNC = True
NDMA_HW = 40


class EngProxy:
    def __init__(self, prog, eng):
        self.prog = prog
        self.eng = eng

    def __getattr__(self, name):
        prog, eng = self.prog, self.eng

        def call(**kw):
            reads, writes = [], []
            kk = {}
            for k, v in kw.items():
                if isinstance(v, V):
                    (writes if k in OUT_KEYS else reads).append(v.buf)
                    kk[k] = v.ap
                else:
                    kk[k] = v
            if name == "dma_start":
                return prog.dma(eng, kk, reads, writes)
            if name == "memset":
                ap = kk.pop("out")
                val = kk.pop("value")
                return prog.op(eng, lambda e: e.memset(ap, val), reads, writes)
            if name == "transpose":
                o, i, idn = kk["out"], kk["in_"], kk["identity"]
                return prog.op(eng, lambda e: e.transpose(o, i, idn), reads, writes)
            return prog.op(eng, lambda e: getattr(e, name)(**kk), reads, writes)

        return call


class Prog:
    def __init__(self, nc, outer):
        self.nc = nc
        self.outer = outer
        self.ph = ExitStack()
        self.q = {e: [] for e in ENGS}
        self.seq = {e: 0 for e in ENGS}
        self.wm = {e: {} for e in ENGS}
        self.needed = {e: set() for e in ENGS}
        self.inc = {e: 0 for e in ENGS}
        self.rank = {e: {} for e in ENGS}
        self.esems = {e: [] for e in ENGS}
        self.dsems = [outer.enter_context(nc.semaphore(f"sd{j}")) for j in range(NDMA)]
        self.dma_cnt = [0] * NDMA
        self.dma_rr = 0
        self.dma_rr_sw = 0
        self.pe = EngProxy(self, "pe")
        self.act = EngProxy(self, "act")
        self.dve = EngProxy(self, "dve")
        self.pool = EngProxy(self, "pool")
        self.sp = EngProxy(self, "sp")
        self.same_engine_sync = SAME_ENGINE_SYNC
        self.nt = 0
        self.nphase = 0

    def tile(self, shape, dtype, name=None):
        self.nt += 1
        t = self.ph.enter_context(self.nc.sbuf_tensor(f"t{self.nt}", list(shape), dtype))
        return V(t[:], Buf())

    def psum(self, shape, dtype=F32):
        self.nt += 1
        t = self.ph.enter_context(self.nc.psum_tensor(f"p{self.nt}", list(shape), dtype))
        return V(t[:], Buf())

    def dram(self, name, shape, dtype, kind="Internal"):
        t = self.nc.dram_tensor(name, list(shape), dtype, kind=kind)
        return V(t.ap(), Buf())

    def _deps(self, reads, writes):
        deps = {}

        def add(k, v):
            if deps.get(k, 0) < v:
                deps[k] = v

        for b in reads:
            if b.lw is not None:
                add(*b.lw)
        for b in writes:
            if b.lw is not None:
                add(*b.lw)
            for k, v in b.rd.items():
                add(k, v)
        return deps

    def _waits(self, eng, deps, force=False):
        waits = []
        wm = self.wm[eng]
        for k, v in deps.items():
            if k == eng and not force and (eng == "pe" or not self.same_engine_sync):
                continue
            if wm.get(k, 0) >= v:
                continue
            wm[k] = v
            waits.append((k, v))
            if not isinstance(k, tuple):
                self.needed[k].add(v)
        return waits

    def _mark(self, tok, reads, writes):
        k, v = tok
        for b in reads:
            if b.rd.get(k, 0) < v:
                b.rd[k] = v
        for b in writes:
            b.lw = tok
            b.rd = {}

    def op(self, eng, fn, reads, writes):
        deps = self._deps(reads, writes)
        waits = self._waits(eng, deps)
        self.seq[eng] += 1
        tok = (eng, self.seq[eng])
        self.q[eng].append([waits, fn, tok, None])
        self._mark(tok, reads, writes)
        return tok

    def dma(self, eng, kk, reads, writes):
        deps = self._deps(reads, writes)
        if eng == "pool":
            i = NDMA_HW + self.dma_rr_sw
            self.dma_rr_sw = (self.dma_rr_sw + 1) % (NDMA - NDMA_HW)
        else:
            i = self.dma_rr
            self.dma_rr = (self.dma_rr + 1) % NDMA_HW
        key = ("d", i)
        prev = self.dma_cnt[i]
        if prev > 0 and deps.get(key, 0) < prev:
            deps[key] = prev
        waits = self._waits(eng, deps)
        self.dma_cnt[i] = prev + 16
        tok = (key, prev + 16)
        self.q[eng].append([waits, lambda e: e.dma_start(**kk), None, i])
        self._mark(tok, reads, writes)
        return tok

    def _semval(self, k, v):
        if isinstance(k, tuple):
            return self.dsems[k[1]], v
        r = self.rank[k][v] - 1
        j = r // EPOCH
        while len(self.esems[k]) <= j:
            self.esems[k].append(self.outer.enter_context(self.nc.semaphore(f"s{k}{len(self.esems[k])}")))
        return self.esems[k][j], (r % EPOCH) + 1

    def phase_end(self):
        deps = {e: self.seq[e] for e in ENGS if self.seq[e] > 0}
        for i in range(NDMA):
            if self.dma_cnt[i] > 0:
                deps[("d", i)] = self.dma_cnt[i]
        for e in ENGS:
            waits = self._waits(e, dict(deps), force=True)
            self.q[e].append([waits, None, None, None])
        for e in ENGS:
            for s in sorted(self.needed[e]):
                self.inc[e] += 1
                self.rank[e][s] = self.inc[e]
            self.needed[e] = set()
        with self.nc.Block() as block:
            engmap = {"pe": block.tensor, "act": block.scalar, "dve": block.vector,
                      "pool": block.gpsimd, "sp": block.sync}
            for e in ENGS:
                q = self.q[e]
                rk = self.rank[e]

                def body(eng, q=q, rk=rk, e=e):
                    for waits, fn, tok, di in q:
                        for k, v in waits:
                            s, val = self._semval(k, v)
                            eng.wait_ge(s, val)
                        if fn is None:
                            continue
                        ins = fn(eng)
                        if di is not None:
                            ins.then_inc(self.dsems[di], 16)
                        elif tok[1] in rk:
                            s, val = self._semval(e, tok[1])
                            ins.then_inc(s, 1)

                engmap[e](body)
        for e in ENGS:
            self.q[e] = []
            self.rank[e] = {}
        self.ph.close()
        self.ph = ExitStack()
        self.nphase += 1


class RR:
    def __init__(self, tiles):
        self.t = tiles
        self.i = 0

    def next(self):
        t = self.t[self.i % len(self.t)]
        self.i += 1
        return t


def tiles(P, n, shape, dtype):
    return RR([P.tile(shape, dtype) for _ in range(n)])


def psums(P, n, shape, dtype=F32):
    return RR([P.psum(shape, dtype) for _ in range(n)])


class Ctx:
    pass


def load_consts(P, C):
    ident = P.tile([128, 128], F32)
    P.sp.dma_start(out=ident, in_=C.ident_d.fresh())
    ones = P.tile([128, 128], F32)
    P.dve.memset(out=ones, value=1.0)
    return ident, ones


def rms_block(P, C, ones, xblk, nk, gain, out_bf, ps_pool, sq_pool, tmp_pool, n=TB, dim=None):
    dim = dim or nk * 128
    ps = ps_pool.next()
    for k in range(nk):
        sq = sq_pool.next()
        P.act.activation(out=sq[:, :n], in_=xblk[:, k, :], func=AF.Square)
        P.pe.matmul(out=ps[:, :n], lhsT=ones, rhs=sq[:, :n], start=(k == 0), stop=(k == nk - 1))
    rstd = tmp_pool.next()
    P.act.activation(out=rstd[:, :n], in_=ps[:, :n], func=AF.Ln, scale=1.0 / dim, bias=C.eps_t[:, 0:1])
    P.act.activation(out=rstd[:, :n], in_=rstd[:, :n], func=AF.Exp, scale=-0.5)
    for k in range(nk):
        P.dve.scalar_tensor_tensor(out=out_bf[:, k, :], in0=xblk[:, k, :], scalar=gain[:, k:k + 1],
                                   in1=rstd[:, :n], op0=ALU.mult, op1=ALU.mult)
    return rstd


def load_w(P, wt, W, r0, nrows, c0, ncols):
    nk = (nrows + 127) // 128
    if nrows % 128 == 0:
        P.pool.dma_start(out=wt[:, :nk, :ncols],
                         in_=W[r0:r0 + nrows, c0:c0 + ncols].re("(k p) c -> p k c", p=128).fresh())
    else:
        assert nk == 1
        P.pool.dma_start(out=wt[:nrows, 0, :ncols], in_=W[r0:r0 + nrows, c0:c0 + ncols].fresh())


def linear(P, W, K, c0, ncols, rhs, blocks, evac, wpool, pspool, n=TB, r0=0):
    nk = (K + 127) // 128
    for g0 in range(0, ncols, 512):
        gw = min(512, ncols - g0)
        wt = wpool.next()
        load_w(P, wt, W, r0, K, c0 + g0, gw)
        for j in blocks:
            for cb in range(0, gw, 128):
                cs = min(128, gw - cb)
                ps = pspool.next()
                for kc in range(nk):
                    ksz = min(128, K - kc * 128)
                    P.pe.matmul(out=ps[:cs, :n], lhsT=wt[:ksz, kc, cb:cb + cs], rhs=rhs(kc, j),
                                start=(kc == 0), stop=(kc == nk - 1))
                evac(g0 + cb, cs, j, ps)


def stage_x_in(P, C):
    ident, ones = load_consts(P, C)
    xin = tiles(P, 2, [128, D], F32)
    st = tiles(P, 2, [128, 8, 128], F32)
    pp = psums(P, 4, [128, 4, 128], F32)
    for i in range(T // 128):
        xt = xin.next()
        P.sp.dma_start(out=xt, in_=C.x[i * 128:(i + 1) * 128, :].fresh())
        s = st.next()
        for half in range(2):
            ps = pp.next()
            for q in range(4):
                k = half * 4 + q
                P.pe.transpose(out=ps[:, q, :], in_=xt[:, k * 128:(k + 1) * 128], identity=ident)
            if half == 0:
                P.act.copy(out=s[:, 0:4, :], in_=ps)
            else:
                P.dve.tensor_copy(out=s[:, 4:8, :], in_=ps)
        P.act.dma_start(out=C.xT.re("k p t -> p k t")[:, :, i * 128:(i + 1) * 128].fresh(), in_=s)
    P.phase_end()


def stage_sweep(P, C, l):
    ident, ones = load_consts(P, C)
    hT = [P.tile([128, 8, TB], BF16) for _ in range(NB)]
    g = P.tile([128, 8], F32)
    P.sp.dma_start(out=g, in_=C.w["norm_mix"][l].re("(k p) -> p k", p=128).fresh())
    xb = tiles(P, 2, [128, 8, TB], F32)
    sq = tiles(P, 2, [128, TB], F32)
    tmp = tiles(P, 2, [128, TB], F32)
    pp = psums(P, 2, [128, TB])
    for j in range(NB):
        x = xb.next()
        P.sp.dma_start(out=x, in_=C.xT.re("k p t -> p k t")[:, :, j * TB:(j + 1) * TB].fresh())
        rms_block(P, C, ones, x, 8, g, hT[j], pp, sq, tmp)
    wpool = tiles(P, 3, [128, 8, 512], BF16)
    pp2 = psums(P, 4, [128, TB])
    stf = tiles(P, 3, [128, TB], F32)
    stb = tiles(P, 3, [128, TB], BF16)
    W = C.w["w_in"][l]
    cnt = [0]

    def evac_mix(coff, cs, j, ps):
        s = stf.next()
        if cnt[0] % 2 == 0:
            P.act.copy(out=s[:cs], in_=ps[:cs])
        else:
            P.dve.tensor_copy(out=s[:cs], in_=ps[:cs])
        cnt[0] += 1
        P.act.dma_start(out=C.pT[coff:coff + cs, j * TB:(j + 1) * TB].fresh(), in_=s[:cs])

    def evac_gate(coff, cs, j, ps):
        s = stb.next()
        P.act.activation(out=s[:cs], in_=ps[:cs], func=AF.Sigmoid)
        P.act.dma_start(out=C.gT[coff:coff + cs, j * TB:(j + 1) * TB].fresh(), in_=s[:cs])

    rhs = lambda kc, j: hT[j][:, kc, :]
    linear(P, W, D, 0, NMIX, rhs, range(NB), evac_mix, wpool, pp2)
    linear(P, W, D, NMIX, 4096, rhs, range(NB), evac_gate, wpool, pp2)
    if l >= 1:
        def evac_vd(coff, cs, j, ps):
            s = stf.next()
            P.act.copy(out=s[:cs], in_=ps[:cs])
            P.act.dma_start(out=C.pT[NMIX + coff:NMIX + coff + cs, j * TB:(j + 1) * TB].fresh(), in_=s[:cs])
        linear(P, C.w["rwkv_vres_down"][l - 1], D, 0, 32, rhs, range(NB), evac_vd, wpool, pp2)
    P.phase_end()


def stage_rope(P, C):
    posi = P.tile([128, T], I32)
    P.sp.dma_start(out=posi, in_=C.pos[0:1, :].pbc(128).fresh())
    posf = P.tile([128, T], F32)
    P.dve.tensor_copy(out=posf, in_=posi)
    cv = P.tile([128, 2], F32)
    P.sp.dma_start(out=cv, in_=C.ropec_d.fresh())
    tp = tiles(P, 2, [128, TB], F32)
    ti = tiles(P, 2, [128, TB], I32)
    tq = tiles(P, 2, [128, TB], F32)
    to = tiles(P, 2, [128, TB], F32)
    for j in range(NB):
        blk = slice(j * TB, (j + 1) * TB)
        for which in range(2):
            u = tp.next()
            P.dve.tensor_scalar(out=u, in0=posf[:, blk], scalar1=cv[:, 0:1], scalar2=None, op0=ALU.mult)
            P.dve.tensor_scalar(out=u, in0=u, scalar1=0.15915494309189535, scalar2=(0.25 if which == 0 else 0.0),
                                op0=ALU.mult, op1=ALU.add)
            ui = ti.next()
            P.dve.tensor_copy(out=ui, in_=u)
            uf = tq.next()
            P.dve.tensor_copy(out=uf, in_=ui)
            P.dve.tensor_tensor(out=u, in0=u, in1=uf, op=ALU.subtract)
            P.dve.tensor_scalar(out=uf, in0=u, scalar1=0.5, scalar2=None, op0=ALU.is_gt)
            P.dve.tensor_tensor(out=u, in0=u, in1=uf, op=ALU.subtract)
            P.dve.tensor_scalar(out=uf, in0=u, scalar1=-0.5, scalar2=None, op0=ALU.is_lt)
            P.dve.tensor_tensor(out=u, in0=u, in1=uf, op=ALU.add)
            o = to.next()
            P.act.activation(out=o, in_=u, func=AF.Sin, scale=6.28318)
            if which == 1:
                P.dve.tensor_scalar(out=o, in0=o, scalar1=cv[:, 1:2], scalar2=None, op0=ALU.mult)
            P.act.dma_start(out=(C.cosT if which == 0 else C.sinT)[:, blk].fresh(), in_=o)
    P.phase_end()


def stage_mla(P, C, l):
    ident, ones = load_consts(P, C)
    onesb = P.tile([128, 128], BF16)
    P.dve.memset(out=onesb, value=1.0)
    tri = P.tile([128, 128], BF16)
    P.pool.dma_start(out=tri, in_=C.tri_d.fresh())
    gq = P.tile([128, 2], F32)
    P.sp.dma_start(out=gq, in_=C.w["mla_q_norm"][l].re("(k p) -> p k", p=128).fresh())
    gkv = P.tile([128, 1], F32)
    P.sp.dma_start(out=gkv, in_=C.w["mla_kv_norm"][l].re("(k p) -> p k", p=128).fresh())
    cqn = P.tile([128, 2, T], BF16)
    ckvn = P.tile([128, 1, T], BF16)
    Krot = P.tile([128, T], BF16)
    Qn = P.tile([128, 4, T], BF16)
    Kn = P.tile([128, 4, T], BF16)
    Vt = P.tile([128, 32, 512], BF16)
    Qrot = P.tile([128, 2, T], BF16)
    ps_all = [P.psum([128, TB]) for _ in range(8)]
    gen = RR(ps_all[0:4])
    opool = RR(ps_all[4:6])
    dpool = RR(ps_all[6:8])
    xq = tiles(P, 2, [128, 2, TB], F32)
    xkv = tiles(P, 2, [128, 1, TB], F32)
    sq = tiles(P, 2, [128, TB], F32)
    tmp = tiles(P, 4, [128, TB], F32)
    csp = tiles(P, 4, [128, TB], F32)
    krp = tiles(P, 4, [128, TB], F32)
    wuq = C.w["mla_w_uq"][l]
    wukv = C.w["mla_w_ukv"][l]
    wqn = P.tile([128, 2, 512], BF16)
    wqr = P.tile([128, 2, 2, 128], BF16)
    wqs = P.tile([128, 2, 2, 128], BF16)
    wk = P.tile([128, 512], BF16)
    wv = P.tile([128, 512], BF16)
    for h in range(4):
        P.pool.dma_start(out=wqn[:, :, h * 128:(h + 1) * 128],
                         in_=wuq[:, h * 192:h * 192 + 128].re("(k p) c -> p k c", p=128).fresh())
        hp, h2 = h // 2, h % 2
        P.pool.dma_start(out=wqr[:, :, hp, h2 * 64:h2 * 64 + 64],
                         in_=wuq[:, h * 192 + 128:h * 192 + 192].re("(k p) c -> p k c", p=128).fresh())
        P.pool.dma_start(out=wqs[:, :, hp, h2 * 64:h2 * 64 + 32],
                         in_=wuq[:, h * 192 + 160:h * 192 + 192].re("(k p) c -> p k c", p=128).fresh())
        P.pool.dma_start(out=wqs[:, :, hp, h2 * 64 + 32:h2 * 64 + 64],
                         in_=wuq[:, h * 192 + 128:h * 192 + 160].re("(k p) c -> p k c", p=128).fresh())
        P.pool.dma_start(out=wk[:, h * 128:(h + 1) * 128], in_=wukv[:, h * 256:h * 256 + 128].fresh())
        P.pool.dma_start(out=wv[:, h * 128:(h + 1) * 128], in_=wukv[:, h * 256 + 128:h * 256 + 256].fresh())
    for j in range(NB):
        blk = slice(j * TB, (j + 1) * TB)
        cq = xq.next()
        P.sp.dma_start(out=cq, in_=C.pT[0:256, blk].re("(k p) t -> p k t", p=128).fresh())
        ckv = xkv.next()
        P.sp.dma_start(out=ckv[:, 0, :], in_=C.pT[256:384, blk].fresh())
        rms_block(P, C, ones, cq, 2, gq, cqn[:, :, blk], gen, sq, tmp)
        rms_block(P, C, ones, ckv, 1, gkv, ckvn[:, :, blk], gen, sq, tmp)
        cos = csp.next()
        P.sp.dma_start(out=cos, in_=C.cosT[:, blk].fresh())
        sin = csp.next()
        P.sp.dma_start(out=sin, in_=C.sinT[:, blk].fresh())
        kr = krp.next()
        ks = krp.next()
        for h2 in range(2):
            P.sp.dma_start(out=kr[h2 * 64:h2 * 64 + 64], in_=C.pT[384:448, blk].fresh())
            P.sp.dma_start(out=ks[h2 * 64:h2 * 64 + 32], in_=C.pT[416:448, blk].fresh())
            P.sp.dma_start(out=ks[h2 * 64 + 32:h2 * 64 + 64], in_=C.pT[384:416, blk].fresh())
        P.dve.tensor_tensor(out=kr, in0=kr, in1=cos, op=ALU.mult)
        P.pool.tensor_tensor(out=ks, in0=ks, in1=sin, op=ALU.mult)
        P.dve.tensor_tensor(out=Krot[:, blk], in0=kr, in1=ks, op=ALU.add)
        for h in range(4):
            ps = gen.next()
            for kc in range(2):
                P.pe.matmul(out=ps, lhsT=wqn[:, kc, h * 128:(h + 1) * 128], rhs=cqn[:, kc, blk],
                            start=(kc == 0), stop=(kc == 1))
            P.act.copy(out=Qn[:, h, blk], in_=ps)
            ps = gen.next()
            P.pe.matmul(out=ps, lhsT=wk[:, h * 128:(h + 1) * 128], rhs=ckvn[:, 0, blk], start=True, stop=True)
            P.dve.tensor_copy(out=Kn[:, h, blk], in_=ps)
        for hp in range(2):
            ps = gen.next()
            ps2 = gen.next()
            for kc in range(2):
                P.pe.matmul(out=ps, lhsT=wqr[:, kc, hp, :], rhs=cqn[:, kc, blk], start=(kc == 0), stop=(kc == 1))
            for kc in range(2):
                P.pe.matmul(out=ps2, lhsT=wqs[:, kc, hp, :], rhs=cqn[:, kc, blk], start=(kc == 0), stop=(kc == 1))
            t1 = tmp.next()
            t2 = tmp.next()
            P.dve.tensor_tensor(out=t1, in0=ps, in1=cos, op=ALU.mult)
            P.dve.tensor_tensor(out=t2, in0=ps2, in1=sin, op=ALU.mult)
            P.pool.tensor_tensor(out=Qrot[:, hp, blk], in0=t1, in1=t2, op=ALU.add)
        for i in range(4):
            tt = j * 4 + i
            ps = gen.next()
            P.pe.matmul(out=ps, lhsT=ckvn[:, 0, tt * 128:(tt + 1) * 128], rhs=wv, start=True, stop=True)
            if i % 2 == 0:
                P.act.copy(out=Vt[:, tt, :], in_=ps)
            else:
                P.dve.tensor_copy(out=Vt[:, tt, :], in_=ps)
    scale = 192.0 ** -0.5
    epool = tiles(P, 5, [128, TB], BF16)
    stb = tiles(P, 2, [128, TB], BF16)
    for h in range(4):
        hp, sub = h // 2, h % 2
        for j in range(NB):
            ops_ = opool.next()
            dps = dpool.next()
            nkt = 4 * j + 4

            def score(kt):
                r = kt - 4 * j
                q0 = 128 * r if r > 0 else 0
                qs = slice(j * TB + q0, (j + 1) * TB)
                ks_ = slice(kt * 128, (kt + 1) * 128)
                sp_ = gen.next()
                P.pe.matmul(out=sp_[:, q0:], lhsT=Kn[:, h, ks_], rhs=Qn[:, h, qs], start=True, stop=False)
                P.pe.matmul(out=sp_[:, q0:], lhsT=Krot[64 * sub:64 * sub + 64, ks_],
                            rhs=Qrot[64 * sub:64 * sub + 64, hp, qs], start=False, stop=True)
                e = epool.next()
                P.act.activation(out=e[:, q0:], in_=sp_[:, q0:], func=AF.Exp, scale=scale)
                if r >= 0:
                    P.dve.tensor_tensor(out=e[:, q0:q0 + 128], in0=e[:, q0:q0 + 128], in1=tri, op=ALU.mult)
                return (kt, e, q0)

            def pv(item):
                kt, e, q0 = item
                P.pe.matmul(out=ops_[:, q0:], lhsT=Vt[:, kt, h * 128:(h + 1) * 128], rhs=e[:, q0:],
                            start=(kt == 0), stop=(kt == nkt - 1))
                P.pe.matmul(out=dps[:, q0:], lhsT=onesb, rhs=e[:, q0:], start=(kt == 0), stop=(kt == nkt - 1))

            pend = []
            for kt in range(nkt):
                pend.append(score(kt))
                if len(pend) > 2:
                    pv(pend.pop(0))
            while pend:
                pv(pend.pop(0))
            rden = tmp.next()
            P.dve.reciprocal(out=rden, in_=dps)
            ob = stb.next()
            P.dve.tensor_tensor(out=ob, in0=ops_, in1=rden, op=ALU.mult)
            P.act.dma_start(out=C.oT[0, h * 128:(h + 1) * 128, j * TB:(j + 1) * TB].fresh(), in_=ob)
    P.phase_end()


def stage_hgrn(P, C, l):
    ident, ones = load_consts(P, C)
    identb = P.tile([128, 128], BF16)
    P.dve.tensor_copy(out=identb, in_=ident)
    cm = P.tile([64, 64], F32)
    P.sp.dma_start(out=cm, in_=C.tri_d[0:64, 0:64].fresh())
    mask = P.tile([128, T], BF16)
    P.dve.memset(out=mask, value=1.0)
    P.dve.memset(out=mask.re("p (c s) -> p c s", s=64)[:, :, 0:1], value=0.0)
    lb = P.tile([128, 4], F32)
    oml = P.tile([128, 4], F32)
    if l == 0:
        P.dve.memset(out=lb, value=0.0)
    else:
        z0 = P.tile([128, 4], F32)
        P.sp.dma_start(out=z0, in_=C.w["hgrn_lb_logits"][0].re("(h p) -> p h", p=128).fresh())
        P.sp.dma_start(out=lb, in_=C.w["hgrn_lb_logits"][1].re("(h p) -> p h", p=128).fresh())
        P.dve.tensor_tensor(out=lb, in0=lb, in1=z0, op=ALU.subtract)
        P.act.activation(out=lb, in_=lb, func=AF.Sigmoid)
    P.dve.tensor_scalar(out=oml, in0=lb, scalar1=-1.0, scalar2=1.0, op0=ALU.mult, op1=ALU.add)
    onorm = P.tile([128, 4], F32)
    P.sp.dma_start(out=onorm, in_=C.w["hgrn_o_norm"][l].re("(h p) -> p h", p=128).fresh())
    qf = P.tile([128, T], F32)
    ff = P.tile([128, T], F32)
    bb = P.tile([128, T], F32)
    eb = P.tile([128, T], F32)
    ktf = qf
    NH = 2
    Qt = [P.tile([128, T], BF16) for _ in range(NH)]
    Kt = [P.tile([128, T], BF16) for _ in range(NH)]
    Kh = [P.tile([128, T], BF16) for _ in range(NH)]
    Ib = [P.tile([128, T], BF16) for _ in range(NH)]
    oacc = [P.tile([128, T], F32) for _ in range(NH)]
    ebl = [P.tile([128, 64], F32) for _ in range(NH)]
    S = [P.tile([128, 128], F32) for _ in range(NH)]
    Sb = [P.tile([128, 128], BF16) for _ in range(NH)]
    p_tr = psums(P, 2, [64, 2, 128], BF16)
    p_at = psums(P, 2, [64, 64])
    p_o = psums(P, 2, [128, 64])
    p_kv = psums(P, 1, [128, 128])
    p_gen = psums(P, 1, [128, TB])
    vk = tiles(P, 8, [64, 2, 128], BF16)
    att = tiles(P, 8, [64, 64], BF16)
    sq = tiles(P, 2, [128, TB], F32)
    tmp = tiles(P, 2, [128, TB], F32)
    stb = tiles(P, 2, [128, TB], BF16)
    for h0 in range(0, 4, NH):
        for a in range(NH):
            h = h0 + a
            P.sp.dma_start(out=qf, in_=C.pT[1472 + h * 128:1472 + (h + 1) * 128, :].fresh())
            P.act.copy(out=Ib[a], in_=qf)
            P.sp.dma_start(out=ff, in_=C.pT[960 + h * 128:960 + (h + 1) * 128, :].fresh())
            P.act.activation(out=ff, in_=ff, func=AF.Sigmoid)
            P.dve.tensor_scalar(out=ff, in0=ff, scalar1=oml[:, h:h + 1], scalar2=lb[:, h:h + 1],
                                op0=ALU.mult, op1=ALU.add)
            P.act.activation(out=eb, in_=ff, func=AF.Ln)
            P.dve.tensor_tensor_scan(out=bb, data0=mask, data1=eb, initial=0.0, op0=ALU.mult, op1=ALU.add)
            P.dve.tensor_scalar(out=ff, in0=ff, scalar1=-1.0, scalar2=1.0, op0=ALU.mult, op1=ALU.add)
            P.act.activation(out=eb, in_=bb, func=AF.Exp)
            P.sp.dma_start(out=qf, in_=C.pT[448 + h * 128:448 + (h + 1) * 128, :].fresh())
            P.dve.tensor_tensor(out=Qt[a], in0=qf, in1=eb, op=ALU.mult)
            P.act.activation(out=ktf, in_=bb, func=AF.Exp, scale=-1.0)
            P.pool.tensor_tensor(out=ktf, in0=ktf, in1=ff, op=ALU.mult)
            P.act.copy(out=Kt[a], in_=ktf)
            P.act.copy(out=ebl[a], in_=eb.re("p (c s) -> p c s", s=64)[:, :, 63])
            P.dve.tensor_tensor(out=Kh[a].re("p (c s) -> p c s", s=64), in0=ktf.re("p (c s) -> p c s", s=64),
                                in1=ebl[a].us(2).bc([128, 64, 64]), op=ALU.mult)
        def hg_I(c):
            ch = slice(c * 64, (c + 1) * 64)
            vks, ats = [], []
            for a in range(NH):
                pt = p_tr.next()
                P.pe.transpose(out=pt[:, 0, :], in_=Ib[a][:, ch], identity=identb)
                P.pe.transpose(out=pt[:, 1, :], in_=Kh[a][:, ch], identity=identb)
                v = vk.next()
                if a % 2 == 0:
                    P.act.copy(out=v, in_=pt)
                else:
                    P.dve.tensor_copy(out=v, in_=pt)
                vks.append(v)
            for a in range(NH):
                pa = p_at.next()
                P.pe.matmul(out=pa, lhsT=Kt[a][:, ch], rhs=Qt[a][:, ch], start=True, stop=True)
                at = att.next()
                P.dve.tensor_tensor(out=at, in0=pa, in1=cm, op=ALU.mult)
                ats.append(at)
            return (c, vks, ats)

        def hg_II(item):
            c, vks, ats = item
            ch = slice(c * 64, (c + 1) * 64)
            for a in range(NH):
                po = p_o.next()
                P.pe.matmul(out=po, lhsT=vks[a][:, 0, :], rhs=ats[a], start=True, stop=(c == 0))
                if c > 0:
                    P.pe.matmul(out=po, lhsT=Sb[a], rhs=Qt[a][:, ch], start=False, stop=True)
                P.act.copy(out=oacc[a][:, ch], in_=po)
            for a in range(NH):
                pk = p_kv.next()
                P.pe.matmul(out=pk, lhsT=vks[a][:, 1, :], rhs=vks[a][:, 0, :], start=True, stop=True)
                if c == 0:
                    P.dve.tensor_copy(out=Sb[a], in_=pk)
                    P.dve.tensor_copy(out=S[a], in_=pk)
                else:
                    P.dve.scalar_tensor_tensor(out=Sb[a], in0=S[a], scalar=ebl[a][:, c:c + 1], in1=pk,
                                               op0=ALU.mult, op1=ALU.add)
                    P.dve.scalar_tensor_tensor(out=S[a], in0=S[a], scalar=ebl[a][:, c:c + 1], in1=pk,
                                               op0=ALU.mult, op1=ALU.add)

        pend = []
        for c in range(T // 64):
            pend.append(hg_I(c))
            if len(pend) > 2:
                hg_II(pend.pop(0))
        while pend:
            hg_II(pend.pop(0))
        for a in range(NH):
            h = h0 + a
            P.sp.dma_start(out=qf, in_=C.pT[1984 + h * 128:1984 + (h + 1) * 128, :].fresh())
            P.act.activation(out=qf, in_=qf, func=AF.Sigmoid)
            for j in range(NB):
                blk = slice(j * TB, (j + 1) * TB)
                s_ = sq.next()
                P.act.activation(out=s_, in_=oacc[a][:, blk], func=AF.Square)
                ps = p_gen.next()
                P.pe.matmul(out=ps, lhsT=ones, rhs=s_, start=True, stop=True)
                r = tmp.next()
                P.act.activation(out=r, in_=ps, func=AF.Ln, scale=1.0 / 128, bias=C.eps_t[:, 0:1])
                P.act.activation(out=r, in_=r, func=AF.Exp, scale=-0.5)
                P.dve.scalar_tensor_tensor(out=r, in0=oacc[a][:, blk], scalar=onorm[:, h:h + 1], in1=r,
                                           op0=ALU.mult, op1=ALU.mult)
                ob = stb.next()
                P.pool.tensor_tensor(out=ob, in0=r, in1=qf[:, blk], op=ALU.mult)
                P.act.dma_start(out=C.oT[1, h * 128:(h + 1) * 128, blk].fresh(), in_=ob)
    P.phase_end()


def frac_sin(P, u, o, ui, uf):
    P.dve.tensor_copy(out=ui, in_=u)
    P.dve.tensor_copy(out=uf, in_=ui)
    P.dve.tensor_tensor(out=u, in0=u, in1=uf, op=ALU.subtract)
    P.dve.tensor_scalar(out=uf, in0=u, scalar1=0.5, scalar2=None, op0=ALU.is_gt)
    P.dve.tensor_tensor(out=u, in0=u, in1=uf, op=ALU.subtract)
    P.dve.tensor_scalar(out=uf, in0=u, scalar1=-0.5, scalar2=None, op0=ALU.is_lt)
    P.dve.tensor_tensor(out=u, in0=u, in1=uf, op=ALU.add)
    P.act.activation(out=o, in_=u, func=AF.Sin, scale=6.28318)


def stage_s5(P, C, l):
    ident, ones = load_consts(P, C)
    w = C.w
    def ld(shape, src):
        t = P.tile(shape, F32)
        P.sp.dma_start(out=t, in_=src.fresh())
        return t
    Are = ld([128, 16], w["s5_A_re"][l].re("(t g) n -> (g n) t", g=2))
    Aim = ld([128, 16], w["s5_A_im"][l].re("(t g) n -> (g n) t", g=2))
    ls = P.tile([128, 16], F32)
    lsv = w["s5_log_step"][l].re("(t g) -> g t", g=2)
    for g2 in range(2):
        P.sp.dma_start(out=ls[g2 * 64:(g2 + 1) * 64, :], in_=lsv[g2:g2 + 1, :].bc([64, 16]).fresh())
    Bre = ld([128, 16, 16], w["s5_B_re"][l].re("(t g) n c -> (g n) t c", g=2))
    Bim = ld([128, 16, 16], w["s5_B_im"][l].re("(t g) n c -> (g n) t c", g=2))
    Cre = ld([128, 4, 64], w["s5_C_re"][l].re("(u g) c n -> (g c) u n", g=8))
    Cim = ld([128, 4, 64], w["s5_C_im"][l].re("(u g) c n -> (g c) u n", g=8))
    Dv = ld([128, 4], w["s5_D"][l].re("(u p) -> p u", p=128))
    bglu = ld([128, 4], w["s5_b_glu"][l].re("(u p) -> p u", p=128))
    mB = ld([128, 4, 128], C.maskB_d.re("r p c -> p r c"))
    mC = ld([128, 4, 128], C.maskC_d.re("r p c -> p r c"))
    wglu = P.tile([128, 4, 512], BF16)
    load_w(P, wglu, w["s5_w_glu"][l], 0, 512, 0, 512)
    sm = [P.tile([128, 16], F32) for _ in range(12)]
    smi = P.tile([128, 16], I32)
    dt, mag, th, cs, sn, abre, abim, zre, zim, t0, t1, t2 = sm
    P.act.activation(out=dt, in_=ls, func=AF.Exp)
    P.dve.tensor_scalar(out=Are, in0=Are, scalar1=-1e-4, scalar2=None, op0=ALU.min)
    P.dve.tensor_tensor(out=t0, in0=dt, in1=Are, op=ALU.mult)
    P.act.activation(out=mag, in_=t0, func=AF.Exp)
    P.dve.tensor_tensor(out=th, in0=dt, in1=Aim, op=ALU.mult)
    P.dve.tensor_scalar(out=t0, in0=th, scalar1=0.15915494309189535, scalar2=0.25, op0=ALU.mult, op1=ALU.add)
    frac_sin(P, t0, cs, smi, t1)
    P.dve.tensor_scalar(out=t0, in0=th, scalar1=0.15915494309189535, scalar2=None, op0=ALU.mult)
    frac_sin(P, t0, sn, smi, t1)
    P.dve.tensor_tensor(out=abre, in0=mag, in1=cs, op=ALU.mult)
    P.dve.tensor_tensor(out=abim, in0=mag, in1=sn, op=ALU.mult)
    P.dve.tensor_tensor(out=t0, in0=Are, in1=Are, op=ALU.mult)
    P.dve.tensor_tensor(out=t1, in0=Aim, in1=Aim, op=ALU.mult)
    P.dve.tensor_tensor(out=t0, in0=t0, in1=t1, op=ALU.add)
    P.dve.reciprocal(out=t0, in_=t0)
    P.dve.tensor_scalar(out=t1, in0=abre, scalar1=-1.0, scalar2=None, op0=ALU.add)
    P.dve.tensor_tensor(out=zre, in0=t1, in1=Are, op=ALU.mult)
    P.dve.tensor_tensor(out=t2, in0=abim, in1=Aim, op=ALU.mult)
    P.dve.tensor_tensor(out=zre, in0=zre, in1=t2, op=ALU.add)
    P.dve.tensor_tensor(out=zre, in0=zre, in1=t0, op=ALU.mult)
    P.dve.tensor_tensor(out=zim, in0=abim, in1=Are, op=ALU.mult)
    P.dve.tensor_tensor(out=t2, in0=t1, in1=Aim, op=ALU.mult)
    P.dve.tensor_tensor(out=zim, in0=zim, in1=t2, op=ALU.subtract)
    P.dve.tensor_tensor(out=zim, in0=zim, in1=t0, op=ALU.mult)
    bbre = P.tile([128, 16, 16], F32)
    bbim = P.tile([128, 16, 16], F32)
    tb = P.tile([128, 16, 16], F32)
    zr_b = zre.us(2).bc([128, 16, 16])
    zi_b = zim.us(2).bc([128, 16, 16])
    P.dve.tensor_tensor(out=bbre, in0=Bre, in1=zr_b, op=ALU.mult)
    P.dve.tensor_tensor(out=tb, in0=Bim, in1=zi_b, op=ALU.mult)
    P.dve.tensor_tensor(out=bbre, in0=bbre, in1=tb, op=ALU.subtract)
    P.dve.tensor_tensor(out=bbim, in0=Bim, in1=zr_b, op=ALU.mult)
    P.dve.tensor_tensor(out=tb, in0=Bre, in1=zi_b, op=ALU.mult)
    P.dve.tensor_tensor(out=bbim, in0=bbim, in1=tb, op=ALU.add)
    LB = P.tile([128, 16, 2, 128], BF16)
    LC = P.tile([128, 16, 2, 128], BF16)
    xt = tiles(P, 3, [128, 128], F32)
    ptr = psums(P, 1, [128, 128])
    for st in range(16):
        pr, ut = st % 4, st // 4
        for ri, src in ((0, bbre), (1, bbim)):
            x = xt.next()
            P.dve.tensor_tensor(out=x.re("p (g c) -> p g c", c=16), in0=mB[:, pr, :].re("p (g c) -> p g c", c=16),
                                in1=src[:, st, :].us(1).bc([128, 8, 16]), op=ALU.mult)
            ps = ptr.next()
            P.pe.transpose(out=ps, in_=x, identity=ident)
            P.act.copy(out=LB[:, st, ri, :], in_=ps)
        for ri, src in ((0, Cre), (1, Cim)):
            x = xt.next()
            P.dve.tensor_tensor(out=x.re("p (g n) -> p g n", n=64), in0=mC[:, pr, :].re("p (g n) -> p g n", n=64),
                                in1=src[:, ut, :].us(1).bc([128, 2, 64]), op=ALU.mult)
            ps = ptr.next()
            P.pe.transpose(out=ps, in_=x, identity=ident)
            P.act.activation(out=LC[:, st, ri, :], in_=ps, func=AF.Copy, scale=(1.0 if ri == 0 else -1.0))
    Ec = P.tile([128, 16, TB], F32)
    Es = P.tile([128, 16, TB], F32)
    tt = tiles(P, 4, [128, 256], F32)
    for st in range(16):
        P.act.copy(out=Ec[:, st, 0:1], in_=cs[:, st:st + 1])
        P.act.copy(out=Es[:, st, 0:1], in_=sn[:, st:st + 1])
        n = 1
        while n < TB:
            cre = Ec[:, st, n - 1:n]
            cim = Es[:, st, n - 1:n]
            a = tt.next()
            b = tt.next()
            P.dve.tensor_scalar(out=a[:, :n], in0=Es[:, st, 0:n], scalar1=cim, scalar2=None, op0=ALU.mult)
            P.dve.tensor_scalar(out=b[:, :n], in0=Ec[:, st, 0:n], scalar1=cim, scalar2=None, op0=ALU.mult)
            P.dve.scalar_tensor_tensor(out=Ec[:, st, n:2 * n], in0=Ec[:, st, 0:n], scalar=cre, in1=a[:, :n],
                                       op0=ALU.mult, op1=ALU.subtract)
            P.dve.scalar_tensor_tensor(out=Es[:, st, n:2 * n], in0=Es[:, st, 0:n], scalar=cre, in1=b[:, :n],
                                       op0=ALU.mult, op1=ALU.add)
            n *= 2
    car = P.tile([128, 16, 2], F32)
    P.dve.memset(out=car, value=0.0)
    ufp = tiles(P, 2, [128, 4, TB], F32)
    ubp = tiles(P, 2, [128, 4, TB], BF16)
    pb = psums(P, 4, [128, TB])
    py = psums(P, 2, [128, TB])
    pg = psums(P, 1, [128, TB])
    wk = tiles(P, 20, [128, TB], F32)
    wk2 = tiles(P, 5, [128, TB], F32)
    xb = tiles(P, 6, [128, TB], BF16)
    zfp = tiles(P, 1, [128, 4, TB], F32)
    zbp = tiles(P, 1, [128, 4, TB], BF16)
    stb = tiles(P, 2, [128, TB], BF16)
    for j in range(NB):
        blk = slice(j * TB, (j + 1) * TB)
        uf = ufp.next()
        P.sp.dma_start(out=uf, in_=C.pT[2496:3008, blk].re("(u p) t -> p u t", p=128).fresh())
        ub = ubp.next()
        P.act.copy(out=ub, in_=uf)
        zf = zfp.next()
        zb = zbp.next()
        ypss = {}

        def stA(st):
            ut = st // 4
            bre = pb.next()
            bim = pb.next()
            P.pe.matmul(out=bre, lhsT=LB[:, st, 0, :], rhs=ub[:, ut, :], start=True, stop=True)
            P.pe.matmul(out=bim, lhsT=LB[:, st, 1, :], rhs=ub[:, ut, :], start=True, stop=True)
            cos = Ec[:, st, :]
            sin = Es[:, st, :]
            t1_, t2_, t3_, t4_, wre, wim = [wk.next() for _ in range(6)]
            P.dve.tensor_tensor(out=t1_, in0=bre, in1=cos, op=ALU.mult)
            P.dve.tensor_tensor(out=t2_, in0=bim, in1=sin, op=ALU.mult)
            P.pool.tensor_tensor(out=wre, in0=t1_, in1=t2_, op=ALU.add)
            P.dve.tensor_tensor(out=t3_, in0=bim, in1=cos, op=ALU.mult)
            P.dve.tensor_tensor(out=t4_, in0=bre, in1=sin, op=ALU.mult)
            P.pool.tensor_tensor(out=wim, in0=t3_, in1=t4_, op=ALU.subtract)
            return dict(st=st, wre=wre, wim=wim, cos=cos, sin=sin)

        def stB(d):
            st = d["st"]
            zr, zi, a1, a2, a3, a4 = [wk.next() for _ in range(6)]
            rho = mag[:, st:st + 1].bc([128, TB])
            P.dve.tensor_tensor_scan(out=zr, data0=rho, data1=d["wre"], initial=car[:, st, 0:1], op0=ALU.mult, op1=ALU.add)
            P.dve.tensor_tensor_scan(out=zi, data0=rho, data1=d["wim"], initial=car[:, st, 1:2], op0=ALU.mult, op1=ALU.add)
            P.pool.tensor_tensor(out=a1, in0=zr, in1=d["cos"], op=ALU.mult)
            P.pool.tensor_tensor(out=a2, in0=zi, in1=d["sin"], op=ALU.mult)
            P.pool.tensor_tensor(out=a3, in0=zr, in1=d["sin"], op=ALU.mult)
            P.pool.tensor_tensor(out=a4, in0=zi, in1=d["cos"], op=ALU.mult)
            d.update(a1=a1, a2=a2, a3=a3, a4=a4)
            return d

        def stC(d):
            st = d["st"]
            ut, pr = st // 4, st % 4
            xre = xb.next()
            xim = xb.next()
            P.dve.tensor_tensor(out=xre, in0=d["a1"], in1=d["a2"], op=ALU.subtract)
            P.dve.tensor_tensor(out=xim, in0=d["a3"], in1=d["a4"], op=ALU.add)
            P.dve.tensor_tensor(out=car[:, st, 0:1], in0=d["a1"][:, TB - 1:TB], in1=d["a2"][:, TB - 1:TB], op=ALU.subtract)
            P.dve.tensor_tensor(out=car[:, st, 1:2], in0=d["a3"][:, TB - 1:TB], in1=d["a4"][:, TB - 1:TB], op=ALU.add)
            if pr == 0:
                ypss[ut] = py.next()
            yps = ypss[ut]
            P.pe.matmul(out=yps, lhsT=LC[:, st, 0, :], rhs=xre, start=(pr == 0), stop=False)
            P.pe.matmul(out=yps, lhsT=LC[:, st, 1, :], rhs=xim, start=False, stop=(pr == 3))
            if pr == 3:
                y, sq_, a_, in_, sg_ = [wk2.next() for _ in range(5)]
                P.dve.scalar_tensor_tensor(out=y, in0=uf[:, ut, :], scalar=Dv[:, ut:ut + 1], in1=yps, op0=ALU.mult, op1=ALU.add)
                P.act.activation(out=sq_, in_=y, func=AF.Square)
                P.dve.tensor_scalar(out=a_, in0=sq_, scalar1=0.044715, scalar2=1.0, op0=ALU.mult, op1=ALU.add)
                P.pool.tensor_tensor(out=in_, in0=a_, in1=y, op=ALU.mult)
                P.act.activation(out=sg_, in_=in_, func=AF.Sigmoid, scale=1.5957691216057308)
                P.dve.tensor_tensor(out=zf[:, ut, :], in0=y, in1=sg_, op=ALU.mult)
                P.act.copy(out=zb[:, ut, :], in_=zf[:, ut, :])

        qa, qb = [], []
        for st in range(16):
            qa.append(stA(st))
            if len(qa) > 1:
                qb.append(stB(qa.pop(0)))
            if len(qb) > 1:
                stC(qb.pop(0))
        while qa:
            qb.append(stB(qa.pop(0)))
            if len(qb) > 1:
                stC(qb.pop(0))
        while qb:
            stC(qb.pop(0))
        for cb in range(4):
            ps = pg.next()
            for kc in range(4):
                P.pe.matmul(out=ps, lhsT=wglu[:, kc, cb * 128:(cb + 1) * 128], rhs=zb[:, kc, :], start=(kc == 0), stop=(kc == 3))
            sg_ = wk2.next()
            P.act.activation(out=sg_, in_=ps, func=AF.Sigmoid, bias=bglu[:, cb:cb + 1])
            ob = stb.next()
            P.dve.tensor_tensor(out=ob, in0=zf[:, cb, :], in1=sg_, op=ALU.mult)
            P.act.dma_start(out=C.oT[2, cb * 128:(cb + 1) * 128, blk].fresh(), in_=ob)
    P.phase_end()


def stage_rwkv(P, C, l):
    ident, ones = load_consts(P, C)
    w = C.w
    identb = P.tile([128, 128], BF16)
    P.dve.tensor_copy(out=identb, in_=ident)

    def ld(shape, src, dt=F32):
        t = P.tile(shape, dt)
        P.sp.dma_start(out=t, in_=src.fresh())
        return t

    bones = ld([128, 128], C.bones_d)
    mask4 = ld([64, 256], C.mask4_d)
    mls = ld([64, 64], C.mls_d)
    tiny = P.tile([128, 1], F32)
    P.dve.memset(out=tiny, value=1e-12)
    gneps = P.tile([128, 1], F32)
    P.dve.memset(out=gneps, value=64e-5)
    rmask = P.tile([128, TB], BF16)
    P.dve.memset(out=rmask, value=1.0)
    P.dve.memset(out=rmask.re("p (c s) -> p c s", s=64)[:, :, 0:1], value=0.0)
    mu = w["rwkv_mu"][l]
    v4 = lambda a: a.re("(u p) -> p u", p=128)
    mu_r, mu_k, mu_v = ld([128, 4], v4(mu[0:512])), ld([128, 4], v4(mu[512:1024])), ld([128, 4], v4(mu[1024:1536]))
    mu_s = P.tile([128, 3], F32)
    P.sp.dma_start(out=mu_s[0:64, 0:1], in_=mu[1536:1600].re("(p o) -> p o", o=1).fresh())
    P.sp.dma_start(out=mu_s[0:64, 1:2], in_=mu[1600:1664].re("(p o) -> p o", o=1).fresh())
    P.sp.dma_start(out=mu_s[:, 2:3], in_=mu[1664:1792].re("(p o) -> p o", o=1).fresh())
    om = {}
    for nm, t in (("r", mu_r), ("k", mu_k), ("v", mu_v)):
        o = P.tile([128, 4], F32)
        P.dve.tensor_scalar(out=o, in0=t, scalar1=-1.0, scalar2=1.0, op0=ALU.mult, op1=ALU.add)
        om[nm] = o
    omu_s = P.tile([128, 3], F32)
    P.dve.memset(out=omu_s, value=0.0)
    P.dve.tensor_scalar(out=omu_s[0:64, 0:2], in0=mu_s[0:64, 0:2], scalar1=-1.0, scalar2=1.0, op0=ALU.mult, op1=ALU.add)
    P.dve.tensor_scalar(out=omu_s[:, 2:3], in0=mu_s[:, 2:3], scalar1=-1.0, scalar2=1.0, op0=ALU.mult, op1=ALU.add)
    w0 = ld([128, 4], v4(w["rwkv_w0"][l]))
    a0 = ld([128, 4], v4(w["rwkv_a0"][l]))
    kkp = ld([128, 4], v4(w["rwkv_k_k"][l]))
    kap = ld([128, 4], v4(w["rwkv_k_a"][l]))
    omka = P.tile([128, 4], F32)
    P.dve.tensor_scalar(out=omka, in0=kap, scalar1=-1.0, scalar2=1.0, op0=ALU.mult, op1=ALU.add)
    rkp = ld([128, 4], w["rwkv_r_k"][l].re("(u s) k -> (s k) u", s=2))
    lnw = ld([128, 4], v4(w["rwkv_ln_w"][l]))
    lnb = ld([128, 4], v4(w["rwkv_ln_b"][l]))
    wup = P.tile([128, 1, 512], BF16)
    load_w(P, wup, w["rwkv_w_up"][l], 0, 64, 0, 512)
    aup = P.tile([128, 1, 512], BF16)
    load_w(P, aup, w["rwkv_a_up"][l], 0, 64, 0, 512)
    gup = P.tile([128, 1, 512], BF16)
    load_w(P, gup, w["rwkv_g_up"][l], 0, 128, 0, 512)
    if l >= 1:
        vup = P.tile([128, 1, 512], BF16)
        load_w(P, vup, w["rwkv_vres_up"][l - 1], 0, 32, 0, 512)
        vbias = ld([128, 4], v4(w["rwkv_vres_bias"][l - 1]))
    big = lambda: P.tile([128, 4, TB], F32)
    r_, k_, v_, lw, lP, a_, kkn, kh, eP, t1 = [big() for _ in range(10)]
    enP = k_
    g_ = P.tile([128, 4, TB], BF16)
    t2 = lP
    raw = tiles(P, 1, [128, 4, TB + 1], F32)
    raws = tiles(P, 2, [128, TB + 1], F32)
    sm_f = tiles(P, 3, [128, TB], F32)
    sm_b = tiles(P, 3, [128, TB], BF16)
    QRY = P.tile([128, 4, 8, 2, 64], BF16)
    KEYz = P.tile([128, 4, 8, 2, 2, 64], BF16)
    P.dve.memset(out=KEYz, value=0.0)
    Az = P.tile([128, 4, 8, 2, 64], BF16)
    P.dve.memset(out=Az, value=0.0)
    KB = P.tile([128, 4, 8, 64], BF16)
    HAT = P.tile([128, 4, 8, 2, 64], BF16)
    Vb = P.tile([128, 4, TB], BF16)
    yacc = a_
    PC = P.tile([128, 4, 8], F32)
    S = [P.tile([128, 64], F32) for _ in range(4)]
    Sb = [P.tile([128, 64], BF16) for _ in range(4)]
    Sbz = [P.tile([128, 2, 64], BF16) for _ in range(4)]
    for u in range(4):
        P.dve.memset(out=Sbz[u], value=0.0)
    banks = psums(P, 8, [128, TB])
    gen = banks
    tmp_tm = tiles(P, 5, [64, 2, 3, 128], BF16)
    amp = tiles(P, 5, [64, 4, 256], BF16)
    lbp = tiles(P, 4, [64, 4, 64], BF16)
    rbp = tiles(P, 6, [64, 4, 64], BF16)
    xp = tiles(P, 8, [64, 4, 64], BF16)
    sqp = tiles(P, 6, [64, 2, 4, 64], BF16)
    id64 = P.tile([64, 64], BF16)
    P.dve.tensor_copy(out=id64, in_=ident[0:64, 0:64])
    stb = tiles(P, 2, [128, TB], BF16)
    stf = tiles(P, 2, [128, TB], F32)
    c4 = lambda t: t.re("p u (c s) -> p u c s", s=64)

    def load_raw(t, rows, j, np_=128, multi=True):
        lo = j * TB
        if multi:
            src = C.pT[rows[0]:rows[1], :].re("(u p) t -> p u t", p=128)
            if j == 0:
                P.dve.memset(out=t[:, :, 0:1], value=0.0)
                P.sp.dma_start(out=t[:, :, 1:], in_=src[:, :, 0:TB].fresh())
            else:
                P.sp.dma_start(out=t, in_=src[:, :, lo - 1:lo + TB].fresh())
        else:
            src = C.pT[rows[0]:rows[1], :]
            if j == 0:
                P.dve.memset(out=t[:np_, 0:1], value=0.0)
                P.sp.dma_start(out=t[:np_, 1:], in_=src[:, 0:TB].fresh())
            else:
                P.sp.dma_start(out=t[:np_], in_=src[:, lo - 1:lo + TB].fresh())

    RW0 = 3008
    for j in range(C.opts.get("rw_blocks", NB)):
        blk = slice(j * TB, (j + 1) * TB)
        for (dst, roff, m_, o_) in ((r_, 0, mu_r, om["r"]), (k_, 512, mu_k, om["k"]), (v_, 1024, mu_v, om["v"])):
            x = raw.next()
            load_raw(x, (RW0 + roff, RW0 + roff + 512), j)
            for u in range(4):
                P.pool.tensor_scalar(out=dst[:, u, :], in0=x[:, u, 1:], scalar1=o_[:, u:u + 1], scalar2=0.0, op0=ALU.mult, op1=ALU.add)
                P.dve.scalar_tensor_tensor(out=dst[:, u, :], in0=x[:, u, 0:TB], scalar=m_[:, u:u + 1], in1=dst[:, u, :],
                                           op0=ALU.mult, op1=ALU.add)
        sm = []
        for ci, (roff, np_) in enumerate(((1536, 64), (1600, 64), (1664, 128))):
            x = raws.next()
            load_raw(x, (RW0 + roff, RW0 + roff + np_), j, np_, multi=False)
            o = sm_f.next()
            P.pool.tensor_scalar(out=o[:np_], in0=x[:np_, 1:], scalar1=omu_s[:np_, ci:ci + 1], scalar2=0.0, op0=ALU.mult, op1=ALU.add)
            P.dve.scalar_tensor_tensor(out=o[:np_], in0=x[:np_, 0:TB], scalar=mu_s[:np_, ci:ci + 1], in1=o[:np_],
                                       op0=ALU.mult, op1=ALU.add)
            sm.append(o)
        wdm, adm, gdm = sm
        tw = sm_b.next()
        P.act.activation(out=tw[:64], in_=wdm[:64], func=AF.Tanh)
        adb = sm_b.next()
        P.act.copy(out=adb[:64], in_=adm[:64])
        sgd = sm_b.next()
        P.act.activation(out=sgd, in_=gdm, func=AF.Sigmoid)
        for u in range(4):
            cs_ = slice(u * 128, (u + 1) * 128)
            ps = gen.next()
            P.pe.matmul(out=ps, lhsT=wup[:64, 0, cs_], rhs=tw[:64], start=True, stop=True)
            P.act.activation(out=lw[:, u, :], in_=ps, func=AF.Sigmoid, bias=w0[:, u:u + 1])
            ps = gen.next()
            P.pe.matmul(out=ps, lhsT=aup[:64, 0, cs_], rhs=adb[:64], start=True, stop=True)
            P.act.activation(out=a_[:, u, :], in_=ps, func=AF.Sigmoid, bias=a0[:, u:u + 1])
            ps = gen.next()
            P.pe.matmul(out=ps, lhsT=gup[:, 0, cs_], rhs=sgd, start=True, stop=True)
            P.act.copy(out=g_[:, u, :], in_=ps)
        P.dve.tensor_scalar(out=lw, in0=lw, scalar1=-0.6065306597126334, scalar2=None, op0=ALU.mult)
        if l >= 1:
            hv = sm_f.next()
            P.sp.dma_start(out=hv[:32], in_=C.pT[NMIX:NMIX + 32, blk].fresh())
            hvb = sm_b.next()
            P.act.copy(out=hvb[:32], in_=hv[:32])
            vf = raw.next()
            P.sp.dma_start(out=vf[:, :, 0:TB], in_=C.vfT[:, blk].re("(u p) t -> p u t", p=128).fresh())
            for u in range(4):
                ps = gen.next()
                P.pe.matmul(out=ps, lhsT=vup[:32, 0, u * 128:(u + 1) * 128], rhs=hvb[:32], start=True, stop=True)
                sv = sm_f.next()
                P.act.activation(out=sv, in_=ps, func=AF.Sigmoid, bias=vbias[:, u:u + 1])
                P.pool.tensor_tensor(out=vf[:, u, 0:TB], in0=vf[:, u, 0:TB], in1=v_[:, u, :], op=ALU.subtract)
                P.dve.tensor_tensor(out=vf[:, u, 0:TB], in0=vf[:, u, 0:TB], in1=sv, op=ALU.mult)
                P.pool.tensor_tensor(out=v_[:, u, :], in0=v_[:, u, :], in1=vf[:, u, 0:TB], op=ALU.add)
        else:
            P.act.dma_start(out=C.vfT[:, blk].re("(u p) t -> p u t", p=128).fresh(), in_=v_)
        for u in range(4):
            P.dve.tensor_scalar(out=kkn[:, u, :], in0=k_[:, u, :], scalar1=kkp[:, u:u + 1], scalar2=None, op0=ALU.mult)
            sq_ = sm_f.next()
            P.act.activation(out=sq_, in_=kkn[:, u, :], func=AF.Square)
            ps = gen.next()
            P.pe.matmul(out=ps, lhsT=bones, rhs=sq_, start=True, stop=True)
            rs = sm_f.next()
            P.act.activation(out=rs, in_=ps, func=AF.Ln, bias=tiny[:, 0:1])
            P.act.activation(out=rs, in_=rs, func=AF.Exp, scale=-0.5)
            P.pool.tensor_tensor(out=kkn[:, u, :], in0=kkn[:, u, :], in1=rs, op=ALU.mult)
            P.dve.tensor_scalar(out=t1[:, u, :], in0=a_[:, u, :], scalar1=kap[:, u:u + 1], scalar2=omka[:, u:u + 1],
                                op0=ALU.mult, op1=ALU.add)
        P.pool.tensor_tensor(out=kh, in0=k_, in1=t1, op=ALU.mult)
        for u in range(4):
            P.dve.tensor_tensor_scan(out=lP[:, u, :], data0=rmask, data1=lw[:, u, :], initial=0.0, op0=ALU.mult, op1=ALU.add)
        P.act.activation(out=eP, in_=lP, func=AF.Exp)
        P.act.activation(out=enP, in_=lP, func=AF.Exp, scale=-1.0)
        P.pool.tensor_tensor(out=lw, in0=lP, in1=lw, op=ALU.subtract)
        P.act.activation(out=lw, in_=lw, func=AF.Exp)
        P.act.copy(out=PC, in_=c4(eP)[:, :, :, 63])
        P.dve.tensor_tensor(out=t1, in0=kkn, in1=a_, op=ALU.mult)
        P.dve.tensor_tensor(out=t1, in0=t1, in1=enP, op=ALU.mult)
        P.pool.tensor_tensor(out=t2, in0=kh, in1=enP, op=ALU.mult)
        for u in range(4):
            c3 = lambda t: t[:, u, :].re("p (c s) -> p c s", s=64)
            P.dve.scalar_tensor_tensor(out=QRY[:, u, :, 0, :], in0=c3(kkn), scalar=-1.0, in1=c3(lw), op0=ALU.mult, op1=ALU.mult)
            P.pool.tensor_tensor(out=QRY[:, u, :, 1, :], in0=c3(r_), in1=c3(eP), op=ALU.mult)
            P.act.copy(out=KB[:, u, :, :], in_=c3(t1))
            for sub in range(2):
                hs = slice(64 * sub, 64 * sub + 64)
                P.act.copy(out=KEYz[hs, u, :, sub, 0, :], in_=c3(t1)[hs])
                P.act.copy(out=KEYz[hs, u, :, sub, 1, :], in_=c3(t2)[hs])
                P.pool.tensor_copy(out=Az[hs, u, :, sub, :], in_=QRY[hs, u, :, 0, :])
            pcb = PC[:, u, :].us(2).bc([128, 8, 64])
            P.dve.tensor_tensor(out=HAT[:, u, :, 0, :], in0=c3(t1), in1=pcb, op=ALU.mult)
            P.pool.tensor_tensor(out=HAT[:, u, :, 1, :], in0=c3(t2), in1=pcb, op=ALU.mult)
        P.act.copy(out=Vb, in_=v_)
        hd = lambda g, hl: (2 * g + hl // 2, hl // 2, hl % 2)
        v4 = lambda bk, lo=0: bk[:64, lo:lo + 256].re("p (h c) -> p h c", h=4)
        store = {}

        def square(N, L):
            bk = banks.next()
            pn, pl2 = v4(bk, 0), v4(bk, 256)
            for hl in range(4):
                P.pe.matmul(out=pn[:, hl, :], lhsT=L[:, hl, :], rhs=N[:, hl, :], start=True, stop=True)
            for hl in range(4):
                P.pe.matmul(out=pl2[:, hl, :], lhsT=N[:, hl, :], rhs=L[:, hl, :], start=True, stop=True)
            NL = sqp.next()
            P.act.copy(out=NL, in_=bk[:64, :].re("p (a h c) -> p a h c", a=2, h=4))
            return NL

        def gen_I(cc, g):
            c0 = cc * 64
            bk = banks.next()
            pt = bk.bitcast(BF16)[:64, 0:768].re("p (a b c) -> p a b c", a=2, b=3)
            for ul in range(2):
                u = 2 * g + ul
                P.pe.transpose(out=pt[:, ul, 0, :], in_=HAT[:, u, cc, 0, :], identity=identb)
                P.pe.transpose(out=pt[:, ul, 1, :], in_=HAT[:, u, cc, 1, :], identity=identb)
                P.pe.transpose(out=pt[:, ul, 2, :], in_=Vb[:, u, c0:c0 + 64], identity=identb)
            tm = tmp_tm.next()
            P.act.copy(out=tm, in_=pt)
            b1, b2, bl = banks.next(), banks.next(), banks.next()
            v1 = b1[:64].re("p (h c) -> p h c", h=4)
            v2 = b2[:64].re("p (h c) -> p h c", h=4)
            vl = v4(bl)
            for hl in range(4):
                u, ul, sub = hd(g, hl)
                qry = QRY[:, u, cc, :, :].re("p a s -> p (a s)")
                P.pe.matmul(out=v1[:, hl, :], lhsT=KEYz[:, u, cc, sub, 0, :], rhs=qry, start=True, stop=True)
                P.pe.matmul(out=v2[:, hl, :], lhsT=KEYz[:, u, cc, sub, 1, :], rhs=qry, start=True, stop=True)
                P.pe.matmul(out=vl[:, hl, :], lhsT=Az[:, u, cc, sub, :], rhs=KB[:, u, cc, :], start=True, stop=True)
            am = amp.next()
            P.dve.tensor_tensor(out=am[:, :, 0:128], in0=v1, in1=mask4[:, 0:128].us(1).bc([64, 4, 128]), op=ALU.mult)
            P.dve.tensor_tensor(out=am[:, :, 128:256], in0=v2, in1=mask4[:, 128:256].us(1).bc([64, 4, 128]), op=ALU.mult)
            Lb = lbp.next()
            P.dve.tensor_tensor(out=Lb, in0=vl, in1=mls.us(1).bc([64, 4, 64]), op=ALU.mult)
            yield
            X = xp.next()
            P.pool.tensor_tensor(out=X, in0=am[:, :, 0:64], in1=id64.us(1).bc([64, 4, 64]), op=ALU.add)
            NL = square(am[:, :, 0:64], Lb)
            yield
            for k in range(5):
                N, L = NL[:, 0], NL[:, 1]
                bk = banks.next()
                pa = v4(bk)
                for hl in range(4):
                    P.pe.matmul(out=pa[:, hl, :], lhsT=L[:, hl, :], rhs=X[:, hl, :], start=True, stop=True)
                Xn = xp.next()
                P.dve.tensor_tensor(out=Xn, in0=pa, in1=X, op=ALU.add)
                X = Xn
                if k < 4:
                    NL = square(N, L)
                yield
            store[(cc, g)] = (tm, am, X)

        def gen_II(cc, g):
            c0 = cc * 64
            first_chunk = (j == 0 and cc == 0)
            tm, am, WT = store.pop((cc, g))
            bk = banks.next()
            pr_ = v4(bk)
            for hl in range(4):
                u, ul, sub = hd(g, hl)
                p0 = 64 * sub
                if not first_chunk:
                    P.pe.matmul(out=pr_[:, hl, :], lhsT=Az[:, u, cc, sub, :], rhs=Sb[u], start=True, stop=False)
                P.pe.matmul(out=pr_[:, hl, :], lhsT=am[:, hl, 128:192], rhs=tm[:, ul, 2, p0:p0 + 64],
                            start=first_chunk, stop=True)
            Rb = rbp.next()
            P.act.copy(out=Rb, in_=pr_)
            yield
            bk = banks.next()
            pu = v4(bk)
            for hl in range(4):
                P.pe.matmul(out=pu[:, hl, :], lhsT=WT[:, hl, :], rhs=Rb[:, hl, :], start=True, stop=True)
            Ub = rbp.next()
            P.dve.tensor_copy(out=Ub, in_=pu)
            yield
            bk = banks.next()
            py = bk[:, 0:128].re("p (u c) -> p u c", u=2)
            for hl in range(4):
                u, ul, sub = hd(g, hl)
                p0 = 64 * sub
                if not first_chunk:
                    P.pe.matmul(out=py[p0:p0 + 64, ul, :], lhsT=Sbz[u][:, sub, :], rhs=QRY[:, u, cc, 1, :],
                                start=True, stop=False)
                P.pe.matmul(out=py[p0:p0 + 64, ul, :], lhsT=Ub[:, hl, :], rhs=am[:, hl, 64:128],
                            start=first_chunk, stop=False)
                P.pe.matmul(out=py[p0:p0 + 64, ul, :], lhsT=tm[:, ul, 2, p0:p0 + 64], rhs=am[:, hl, 192:256],
                            start=False, stop=True)
            P.act.copy(out=yacc[:, 2 * g:2 * g + 2, c0:c0 + 64], in_=py)
            bk2 = banks.next()
            ps_ = bk2[:, 0:128].re("p (u c) -> p u c", u=2)
            for hl in range(4):
                u, ul, sub = hd(g, hl)
                p0 = 64 * sub
                P.pe.matmul(out=ps_[p0:p0 + 64, ul, :], lhsT=tm[:, ul, 0, p0:p0 + 64], rhs=Ub[:, hl, :], start=True, stop=False)
                P.pe.matmul(out=ps_[p0:p0 + 64, ul, :], lhsT=tm[:, ul, 1, p0:p0 + 64], rhs=tm[:, ul, 2, p0:p0 + 64],
                            start=False, stop=True)
            for ul in range(2):
                u = 2 * g + ul
                if first_chunk:
                    P.dve.tensor_copy(out=S[u], in_=ps_[:, ul, :])
                else:
                    P.dve.scalar_tensor_tensor(out=S[u], in0=S[u], scalar=PC[:, u, cc:cc + 1], in1=ps_[:, ul, :],
                                               op0=ALU.mult, op1=ALU.add)
                P.pool.tensor_copy(out=Sb[u], in_=S[u])
                P.pool.tensor_copy(out=Sbz[u][0:64, 0, :], in_=S[u][0:64])
                P.pool.tensor_copy(out=Sbz[u][64:128, 1, :], in_=S[u][64:128])
            yield

        NCH = C.opts.get("rw_chunks", 8)
        LA = 1
        pendI = [(cc, g) for cc in range(NCH) for g in range(2)]
        nextII = {0: 0, 1: 0}
        actI, actII = [], {}
        doneI = set()
        while pendI or actI or actII or nextII[0] < NCH or nextII[1] < NCH:
            while pendI and len(actI) < 2 * (LA + 1) and pendI[0][0] <= min(nextII[0], nextII[1]) + LA:
                key = pendI.pop(0)
                actI.append((key, gen_I(*key)))
            for g in range(2):
                if g not in actII and nextII[g] < NCH and (nextII[g], g) in doneI:
                    actII[g] = gen_II(nextII[g], g)
            for g in list(actII.keys()):
                try:
                    next(actII[g])
                except StopIteration:
                    del actII[g]
                    nextII[g] += 1
            for item in list(actI):
                try:
                    next(item[1])
                except StopIteration:
                    actI.remove(item)
                    doneI.add(item[0])
        for u in range(4):
            ps = gen.next()
            P.pe.matmul(out=ps, lhsT=bones, rhs=yacc[:, u, :], start=True, stop=True)
            yc = sm_f.next()
            P.dve.scalar_tensor_tensor(out=yc, in0=ps, scalar=-1.0 / 64, in1=yacc[:, u, :], op0=ALU.mult, op1=ALU.add)
            sq_ = sm_f.next()
            P.act.activation(out=sq_, in_=yc, func=AF.Square)
            ps = gen.next()
            P.pe.matmul(out=ps, lhsT=bones, rhs=sq_, start=True, stop=True)
            rs = sm_f.next()
            P.act.activation(out=rs, in_=ps, func=AF.Ln, scale=1.0 / 64, bias=gneps[:, 0:1])
            P.act.activation(out=rs, in_=rs, func=AF.Exp, scale=-0.5)
            P.pool.tensor_tensor(out=yc, in0=yc, in1=rs, op=ALU.mult)
            P.dve.tensor_scalar(out=yc, in0=yc, scalar1=lnw[:, u:u + 1], scalar2=lnb[:, u:u + 1], op0=ALU.mult, op1=ALU.add)
            rk = stf.next()
            P.dve.scalar_tensor_tensor(out=rk, in0=r_[:, u, :], scalar=rkp[:, u:u + 1], in1=kh[:, u, :], op0=ALU.mult, op1=ALU.mult)
            ps = gen.next()
            P.pe.matmul(out=ps, lhsT=bones, rhs=rk, start=True, stop=True)
            P.dve.tensor_tensor(out=rk, in0=ps, in1=v_[:, u, :], op=ALU.mult)
            P.pool.tensor_tensor(out=yc, in0=yc, in1=rk, op=ALU.add)
            ob = stb.next()
            P.dve.tensor_tensor(out=ob, in0=yc, in1=g_[:, u, :], op=ALU.mult)
            P.act.dma_start(out=C.oT[3, u * 128:(u + 1) * 128, blk].fresh(), in_=ob)
    P.phase_end()


def load_w_big(P, wt, W, nrows, ncols):
    for c0 in range(0, ncols, 512):
        cw = min(512, ncols - c0)
        P.pool.dma_start(out=wt[:, :, c0:c0 + cw], in_=W[0:nrows, c0:c0 + cw].re("(k p) c -> p k c", p=128).fresh())


class WBig:
    def __init__(self, P, W, nrows, ncols, order=None):
        self.nk = nrows // 128
        self.tiles = {}
        pieces = list(range((ncols + 511) // 512))
        for pi in (order or pieces):
            c0 = pi * 512
            cw = min(512, ncols - c0)
            t = P.tile([128, self.nk, 512], BF16)
            P.pool.dma_start(out=t[:, :, :cw], in_=W[0:nrows, c0:c0 + cw].re("(k p) c -> p k c", p=128).fresh())
            self.tiles[pi] = t

    def sl(self, kc, c0, cs):
        return self.tiles[c0 // 512][:, kc, (c0 % 512):(c0 % 512) + cs]


def stage_merge(P, C, l):
    ident, ones = load_consts(P, C)
    w = C.w
    Pm = []
    for nm in ("w_branch_mla", "w_branch_hgrn", "w_branch_s5", "w_branch_rwkv"):
        Pm.append(WBig(P, w[nm][l], 512, 1024))
    wout = WBig(P, w["w_out"][l], 1024, 1024)
    xbp = tiles(P, 2, [128, 8, TB], F32)
    omp = tiles(P, 8, [128, 4, TB], BF16)
    ybp = tiles(P, 2, [128, 8, TB], BF16)
    gp = tiles(P, 6, [128, TB], BF16)
    accp = tiles(P, 2, [128, TB], F32)
    tmpp = tiles(P, 3, [128, TB], F32)
    pp = psums(P, 6, [128, TB])
    xTv = C.xT.re("k p t -> p k t")
    for j in range(NB):
        blk = slice(j * TB, (j + 1) * TB)
        xb = xbp.next()
        P.sp.dma_start(out=xb, in_=xTv[:, :, blk].fresh())
        om = []
        for m in range(4):
            t = omp.next()
            P.sp.dma_start(out=t, in_=C.oT[m, :, blk].re("(k p) t -> p k t", p=128).fresh())
            om.append(t)
        yb = ybp.next()
        for cb in range(8):
            acc = accp.next()
            for m in range(4):
                ps = pp.next()
                for kc in range(4):
                    P.pe.matmul(out=ps, lhsT=Pm[m].sl(kc, cb * 128, 128), rhs=om[m][:, kc, :],
                                start=(kc == 0), stop=(kc == 3))
                gt = gp.next()
                P.sp.dma_start(out=gt, in_=C.gT[m * 1024 + cb * 128:m * 1024 + (cb + 1) * 128, blk].fresh())
                if m == 0:
                    P.dve.tensor_tensor(out=acc, in0=ps, in1=gt, op=ALU.mult)
                else:
                    tm_ = tmpp.next()
                    P.dve.tensor_tensor(out=tm_, in0=ps, in1=gt, op=ALU.mult)
                    if m < 3:
                        P.pool.tensor_tensor(out=acc, in0=acc, in1=tm_, op=ALU.add)
                    else:
                        P.pool.tensor_tensor(out=yb[:, cb, :], in0=acc, in1=tm_, op=ALU.add)
        for cb2 in range(8):
            ps = pp.next()
            for kc in range(8):
                P.pe.matmul(out=ps, lhsT=wout.sl(kc, cb2 * 128, 128), rhs=yb[:, kc, :],
                            start=(kc == 0), stop=(kc == 7))
            P.dve.tensor_tensor(out=xb[:, cb2, :], in0=ps, in1=xb[:, cb2, :], op=ALU.add)
        P.act.dma_start(out=xTv[:, :, blk].fresh(), in_=xb)
    P.phase_end()


def stage_xattn(P, C, l):
    ident, ones = load_consts(P, C)
    w = C.w
    onesb = P.tile([128, 128], BF16)
    P.dve.memset(out=onesb, value=1.0)
    gm = P.tile([128, 8], F32)
    P.sp.dma_start(out=gm, in_=w["norm_xm"][l].re("(k p) -> p k", p=128).fresh())
    gq = P.tile([128, 8], F32)
    P.sp.dma_start(out=gq, in_=w["norm_xq"][l].re("(k p) -> p k", p=128).fresh())
    pp = psums(P, 7, [128, TB])
    sq = tiles(P, 2, [128, TB], F32)
    tmp = tiles(P, 3, [128, TB], F32)
    memT = P.tile([128, 8, 256], F32)
    mt_in = tiles(P, 2, [128, D], F32)
    for mt in range(2):
        t = mt_in.next()
        P.sp.dma_start(out=t, in_=C.mem[mt * 128:(mt + 1) * 128, :].fresh())
        for k in range(8):
            ps = pp.next()
            P.pe.transpose(out=ps[:, 0:128], in_=t[:, k * 128:(k + 1) * 128], identity=ident)
            P.act.copy(out=memT[:, k, mt * 128:(mt + 1) * 128], in_=ps[:, 0:128])
    hm = P.tile([128, 8, 256], BF16)
    rms_block(P, C, ones, memT, 8, gm, hm, pp, sq, tmp, n=256)
    wkv = w["xattn_w_kv"][l]
    kT = P.tile([128, 8, 256], BF16)
    vt = P.tile([128, 2, 1024], BF16)
    wpool = tiles(P, 2, [128, 8, 512], BF16)

    def evac_k(coff, cs, j, ps):
        P.act.copy(out=kT[:, coff // 128, :], in_=ps[:, :256])

    linear(P, wkv, D, 0, 1024, lambda kc, j: hm[:, kc, :], [0], evac_k, wpool, pp, n=256)
    for half in range(2):
        wt = wpool.next()
        load_w(P, wt, wkv, 0, D, 1024 + half * 512, 512)
        for mt in range(2):
            ps = pp.next()
            for kc in range(8):
                P.pe.matmul(out=ps, lhsT=hm[:, kc, mt * 128:(mt + 1) * 128], rhs=wt[:, kc, :], start=(kc == 0), stop=(kc == 7))
            P.dve.tensor_copy(out=vt[:, mt, half * 512:(half + 1) * 512], in_=ps)
    wq = WBig(P, w["xattn_w_q"][l], 1024, 1024)
    wo = WBig(P, w["xattn_w_o"][l], 1024, 1024)
    xbp = tiles(P, 2, [128, 8, TB], F32)
    hqp = tiles(P, 2, [128, 8, TB], BF16)
    qp = tiles(P, 2, [128, 8, TB], BF16)
    op_ = tiles(P, 2, [128, 8, TB], BF16)
    ep = tiles(P, 4, [128, TB], BF16)
    xTv = C.xT.re("k p t -> p k t")
    for j in range(NB):
        blk = slice(j * TB, (j + 1) * TB)
        xb = xbp.next()
        P.sp.dma_start(out=xb, in_=xTv[:, :, blk].fresh())
        hq = hqp.next()
        rms_block(P, C, ones, xb, 8, gq, hq, pp, sq, tmp)
        q = qp.next()
        for cb in range(8):
            ps = pp.next()
            for kc in range(8):
                P.pe.matmul(out=ps, lhsT=wq.sl(kc, cb * 128, 128), rhs=hq[:, kc, :], start=(kc == 0), stop=(kc == 7))
            if cb % 2 == 0:
                P.act.copy(out=q[:, cb, :], in_=ps)
            else:
                P.dve.tensor_copy(out=q[:, cb, :], in_=ps)
        o = op_.next()
        for h in range(4):
            es = []
            for mt in range(2):
                sps = pp.next()
                for c2 in range(2):
                    P.pe.matmul(out=sps, lhsT=kT[:, 2 * h + c2, mt * 128:(mt + 1) * 128], rhs=q[:, 2 * h + c2, :],
                                start=(c2 == 0), stop=(c2 == 1))
                e = ep.next()
                P.act.activation(out=e, in_=sps, func=AF.Exp, scale=1.0 / 16)
                es.append(e)
            dps = pp.next()
            for mt in range(2):
                P.pe.matmul(out=dps, lhsT=onesb, rhs=es[mt], start=(mt == 0), stop=(mt == 1))
            rden = tmp.next()
            P.dve.reciprocal(out=rden, in_=dps)
            for dv in range(2):
                ops_ = pp.next()
                for mt in range(2):
                    P.pe.matmul(out=ops_, lhsT=vt[:, mt, h * 256 + dv * 128:h * 256 + (dv + 1) * 128], rhs=es[mt],
                                start=(mt == 0), stop=(mt == 1))
                P.dve.tensor_tensor(out=o[:, 2 * h + dv, :], in0=ops_, in1=rden, op=ALU.mult)
        for cb2 in range(8):
            ps = pp.next()
            for kc in range(8):
                P.pe.matmul(out=ps, lhsT=wo.sl(kc, cb2 * 128, 128), rhs=o[:, kc, :], start=(kc == 0), stop=(kc == 7))
            P.dve.tensor_tensor(out=xb[:, cb2, :], in0=ps, in1=xb[:, cb2, :], op=ALU.add)
        P.act.dma_start(out=xTv[:, :, blk].fresh(), in_=xb)
    P.phase_end()


def stage_ffn(P, C, l):
    ident, ones = load_consts(P, C)
    w = C.w
    NF = D_FF // 128
    TF = 512
    wgu = WBig(P, w["ffn_w_gate_up"][l], 1024, 2 * D_FF, order=[0, 5, 6, 1, 7, 2, 8, 3, 9, 4, 10])
    wd = WBig(P, w["ffn_w_down"][l], D_FF, 1024)
    cw = []
    for wi in range(3):
        t_ = P.tile([128, NF], F32)
        P.sp.dma_start(out=t_, in_=w["ffn_conv_w"][l][wi].re("(k p) -> p k", p=128).fresh())
        cw.append(t_)
    cbias = P.tile([128, NF], F32)
    P.sp.dma_start(out=cbias, in_=w["ffn_conv_b"][l].re("(k p) -> p k", p=128).fresh())
    gn = P.tile([128, 8], F32)
    P.sp.dma_start(out=gn, in_=w["norm_ffn"][l].re("(k p) -> p k", p=128).fresh())
    halo = P.tile([128, NF, 2], F32)
    P.dve.memset(out=halo, value=0.0)
    xbp = tiles(P, 1, [128, 8, TF], F32)
    hp = tiles(P, 2, [128, 8, TF], BF16)
    ap_ = tiles(P, 1, [128, NF, TF], BF16)
    gxp = tiles(P, 2, [128, TF + 2], F32)
    tp = tiles(P, 3, [128, TF], F32)
    rxp = tiles(P, 3, [128, TF], F32)
    sq = tiles(P, 1, [128, TB], F32)
    tmp = tiles(P, 1, [128, TB], F32)
    pp = psums(P, 7, [128, TB])
    xTv = C.xT.re("k p t -> p k t")
    for j in range(T // TF):
        blk = slice(j * TF, (j + 1) * TF)
        xb = xbp.next()
        P.sp.dma_start(out=xb, in_=xTv[:, :, blk].fresh())
        h = hp.next()
        rms_block(P, C, ones, xb, 8, gn, h, pp, sq, tmp, n=TF)
        a = ap_.next()
        for cb in range(NF):
            gps = pp.next()
            for kc in range(8):
                P.pe.matmul(out=gps[:, :TF], lhsT=wgu.sl(kc, cb * 128, 128), rhs=h[:, kc, :], start=(kc == 0), stop=(kc == 7))
            ups = pp.next()
            for kc in range(8):
                P.pe.matmul(out=ups[:, :TF], lhsT=wgu.sl(kc, D_FF + cb * 128, 128), rhs=h[:, kc, :],
                            start=(kc == 0), stop=(kc == 7))
            gx = gxp.next()
            P.act.copy(out=gx[:, 2:TF + 2], in_=gps[:, :TF])
            P.pool.tensor_copy(out=gx[:, 0:2], in_=halo[:, cb, :])
            P.pool.tensor_copy(out=halo[:, cb, :], in_=gx[:, TF:TF + 2])
            t = tp.next()
            P.dve.tensor_scalar(out=t, in0=gx[:, 2:TF + 2], scalar1=cw[2][:, cb:cb + 1], scalar2=cbias[:, cb:cb + 1],
                                op0=ALU.mult, op1=ALU.add)
            P.dve.scalar_tensor_tensor(out=t, in0=gx[:, 1:TF + 1], scalar=cw[1][:, cb:cb + 1], in1=t, op0=ALU.mult, op1=ALU.add)
            P.dve.scalar_tensor_tensor(out=t, in0=gx[:, 0:TF], scalar=cw[0][:, cb:cb + 1], in1=t, op0=ALU.mult, op1=ALU.add)
            sl = tp.next()
            P.act.activation(out=sl, in_=t, func=AF.Silu)
            P.dve.tensor_tensor(out=a[:, cb, :], in0=ups[:, :TF], in1=sl, op=ALU.mult)
        for cb2 in range(8):
            ps = pp.next()
            for kc in range(NF):
                P.pe.matmul(out=ps[:, :TF], lhsT=wd.sl(kc, cb2 * 128, 128), rhs=a[:, kc, :], start=(kc == 0), stop=(kc == NF - 1))
            rx = rxp.next()
            P.sp.dma_start(out=rx, in_=C.xT[cb2, :, blk].fresh())
            P.dve.tensor_tensor(out=rx, in0=ps[:, :TF], in1=rx, op=ALU.add)
            P.act.dma_start(out=C.xT[cb2, :, blk].fresh(), in_=rx)
    P.phase_end()


def stage_final(P, C):
    ident, ones = load_consts(P, C)
    gf = P.tile([128, 8], F32)
    P.sp.dma_start(out=gf, in_=C.w["norm_final"].re("(k p) -> p k", p=128).fresh())
    xbp = tiles(P, 2, [128, 8, TB], F32)
    hfp = tiles(P, 2, [128, 8, TB], F32)
    otp = tiles(P, 2, [128, D], F32)
    sq = tiles(P, 2, [128, TB], F32)
    tmp = tiles(P, 2, [128, TB], F32)
    pp = psums(P, 2, [128, TB])
    pt = psums(P, 4, [128, 4, 128])
    xTv = C.xT.re("k p t -> p k t")
    for j in range(NB):
        blk = slice(j * TB, (j + 1) * TB)
        xb = xbp.next()
        P.sp.dma_start(out=xb, in_=xTv[:, :, blk].fresh())
        hf = hfp.next()
        rms_block(P, C, ones, xb, 8, gf, hf, pp, sq, tmp)
        for tt in range(4):
            ot = otp.next()
            for half in range(2):
                ps = pt.next()
                for q in range(4):
                    k = half * 4 + q
                    P.pe.transpose(out=ps[:, q, :], in_=hf[:, k, tt * 128:(tt + 1) * 128], identity=ident)
                if half == 0:
                    P.act.copy(out=ot[:, 0:512], in_=ps.re("p a b -> p (a b)"))
                else:
                    P.dve.tensor_copy(out=ot[:, 512:1024], in_=ps.re("p a b -> p (a b)"))
            r0 = j * TB + tt * 128
            P.act.dma_start(out=C.out[r0:r0 + 128, :], in_=ot)
    P.phase_end()


def build(opts):
    nc = bass.Bass("TRN2", target_bir_lowering=False)
    outer = ExitStack()
    with outer:
        outer.enter_context(nc.allow_non_contiguous_dma(reason="small parameter vectors"))
        P = Prog(nc, outer)
        C = Ctx()
        C.opts = opts
        C.x = P.dram("x", [T, D], F32, kind="ExternalInput")
        C.mem = P.dram("mem", [256, D], F32, kind="ExternalInput")
        C.pos = P.dram("pos", [1, T], I32, kind="ExternalInput")
        C.w = {}
        for name, shape in opts["wshapes"].items():
            C.w[name] = P.dram(name, shape, F32, kind="ExternalInput")
        C.ident_d = P.dram("ident", [128, 128], F32, kind="ExternalInput")
        dbg = opts.get("debug", ())
        kd = lambda n: ("ExternalOutput" if n in dbg else "Internal")
        C.xT = P.dram("xT", [8, 128, T], F32, kind=kd("xT"))
        C.pT = P.dram("pT", [NMIX + 32, T], F32, kind=kd("pT"))
        C.gT = P.dram("gT", [4096, T], BF16, kind=kd("gT"))
        C.out = P.dram("out", [T, D], F32, kind="ExternalOutput")
        C.oT = P.dram("oT", [4, 512, T], BF16, kind=kd("oT"))
        C.cosT = P.dram("cosT", [128, T], F32, kind=kd("cosT"))
        C.sinT = P.dram("sinT", [128, T], F32, kind=kd("sinT"))
        C.tri_d = P.dram("tri", [128, 128], F32, kind="ExternalInput")
        C.ropec_d = P.dram("ropec", [128, 2], F32, kind="ExternalInput")
        C.bones_d = P.dram("bones", [128, 128], F32, kind="ExternalInput")
        C.mask4_d = P.dram("mask4", [64, 256], F32, kind="ExternalInput")
        C.mls_d = P.dram("mls", [64, 64], F32, kind="ExternalInput")
        C.vfT = P.dram("vfT", [512, T], F32, kind=kd("vfT"))
        C.maskB_d = P.dram("maskB", [4, 128, 128], F32, kind="ExternalInput")
        C.maskC_d = P.dram("maskC", [4, 128, 128], F32, kind="ExternalInput")
        P.ph = outer
        C.eps_t = P.tile([128, 1], F32)
        P.ph = ExitStack()
        P.dve.memset(out=C.eps_t, value=EPS)
        if not opts.get("skip_pre"):
            stage_x_in(P, C)
            stage_rope(P, C)
        nl = opts.get("layers", DEPTH)
        for l in range(nl):
            if not opts.get("skip_pre"):
                stage_sweep(P, C, l)
            if opts.get("stop") == f"sweep{l}":
                break
            if "mla" in opts.get("mixers", "mla,hgrn,s5,rwkv"):
                stage_mla(P, C, l)
            if "hgrn" in opts.get("mixers", "mla,hgrn,s5,rwkv"):
                stage_hgrn(P, C, l)
            if "s5" in opts.get("mixers", "mla,hgrn,s5,rwkv"):
                stage_s5(P, C, l)
            if "rwkv" in opts.get("mixers", "mla,hgrn,s5,rwkv"):
                stage_rwkv(P, C, l)
            if opts.get("stop") == f"mix{l}":
                break
            stage_merge(P, C, l)
            if opts.get("stop") == f"merge{l}":
                break
            stage_xattn(P, C, l)
            if opts.get("stop") == f"xattn{l}":
                break
            stage_ffn(P, C, l)
            if opts.get("stop") == f"ffn{l}":
                break
        if not opts.get("stop"):
            stage_final(P, C)
        P.phase_end()
    return nc


_CACHE = {}


def make_inputs(inputs=None):
    consts = {"ident": np.eye(128, dtype=np.float32)}
    consts["tri"] = np.triu(np.ones((128, 128), np.float32))
    invf = (10000.0 ** (-np.arange(32, dtype=np.float32) / 32)).astype(np.float32)
    rc = np.zeros((128, 2), np.float32)
    for p in range(128):
        rc[p, 0] = invf[p % 32]
        rc[p, 1] = -1.0 if (p // 32) % 2 == 0 else 1.0
    consts["ropec"] = rc
    bo = np.zeros((128, 128), np.float32)
    bo[:64, :64] = 1.0
    bo[64:, 64:] = 1.0
    consts["bones"] = bo
    st_ = np.triu(np.ones((64, 64), np.float32), 1)
    in_ = np.triu(np.ones((64, 64), np.float32), 0)
    consts["mask4"] = np.concatenate([st_, in_, st_, in_], axis=1)
    consts["mls"] = np.tril(np.ones((64, 64), np.float32), -1)
    mB = np.zeros((4, 128, 128), np.float32)
    for pr in range(4):
        for g2 in range(2):
            g8 = 2 * pr + g2
            mB[pr, g2 * 64:(g2 + 1) * 64, g8 * 16:(g8 + 1) * 16] = 1.0
    consts["maskB"] = mB
    consts["maskC"] = np.ascontiguousarray(mB.transpose(0, 2, 1))
    return consts


def kernel(**inputs):
    wnames = [k for k in inputs if k not in ("x", "mem", "positions")]
    wshapes = {k: list(np.asarray(inputs[k]).shape) for k in wnames}
    if "nc" not in _CACHE:
        _CACHE["nc"] = build({"wshapes": wshapes})
    nc = _CACHE["nc"]
    consts = make_inputs()
    wts = {k: np.ascontiguousarray(np.asarray(inputs[k], dtype=np.float32)) for k in wnames}
    x = np.asarray(inputs["x"], dtype=np.float32)
    mem = np.asarray(inputs["mem"], dtype=np.float32)
    pos = np.asarray(inputs["positions"]).astype(np.int32)
    n = 8
    in_maps = []
    for c in range(n):
        b = c % 4
        m = {"x": np.ascontiguousarray(x[b]), "mem": np.ascontiguousarray(mem[b]),
             "pos": np.ascontiguousarray(pos[b][None, :])}
        m.update(wts)
        m.update(consts)
        in_maps.append(m)
    res = run_bass_kernel_spmd(nc, in_maps, core_ids=list(range(n)))
    out = np.stack([np.asarray(res.results[b]["out"], dtype=np.float32) for b in range(4)], axis=0)
    return out
```

```python
import numpy as np
import concourse.bass as bass
import concourse.mybir as mybir
from concourse.bass_utils import run_bass_kernel_spmd
from contextlib import ExitStack

F32 = mybir.dt.float32
BF16 = mybir.dt.bfloat16
I32 = mybir.dt.int32
AF = mybir.ActivationFunctionType
ALU = mybir.AluOpType

T = 4096
D = 1024
TB = 512
NB = T // TB
DEPTH = 2
EPS = 1e-6
P_IN = 8896
NMIX = 4800
D_FF = 2816


class Buf:
    __slots__ = ("lw", "rd")

    def __init__(self):
        self.lw = None
        self.rd = {}


class V:
    def __init__(self, ap, buf):
        self.ap = ap
        self.buf = buf

    def __getitem__(self, k):
        return V(self.ap[k], self.buf)

    def re(self, pat, **kw):
        return V(self.ap.rearrange(pat, **kw), self.buf)

    def bc(self, shape):
        return V(self.ap.to_broadcast(list(shape)), self.buf)

    def pbc(self, n):
        return V(self.ap.partition_broadcast(n), self.buf)

    def us(self, axis):
        return V(self.ap.unsqueeze(axis), self.buf)

    def bitcast(self, dt):
        return V(self.ap.bitcast(dt), self.buf)

    def fresh(self):
        return V(self.ap, Buf())

    @property
    def shape(self):
        return self.ap.shape


OUT_KEYS = ("out", "accum_out")
ENGS = ("pe", "act", "dve", "pool", "sp")
EPOCH = 30000
NDMA = 56
SAME_ENGINE_SYNC = True
NDMA_HW = 40


class EngProxy:
    def __init__(self, prog, eng):
        self.prog = prog
        self.eng = eng

    def __getattr__(self, name):
        prog, eng = self.prog, self.eng

        def call(**kw):
            reads, writes = [], []
            kk = {}
            for k, v in kw.items():
                if isinstance(v, V):
                    (writes if k in OUT_KEYS else reads).append(v.buf)
                    kk[k] = v.ap
                else:
                    kk[k] = v
            if name == "dma_start":
                return prog.dma(eng, kk, reads, writes)
            if name == "memset":
                ap = kk.pop("out")
                val = kk.pop("value")
                return prog.op(eng, lambda e: e.memset(ap, val), reads, writes)
            if name == "transpose":
                o, i, idn = kk["out"], kk["in_"], kk["identity"]
                return prog.op(eng, lambda e: e.transpose(o, i, idn), reads, writes)
            return prog.op(eng, lambda e: getattr(e, name)(**kk), reads, writes)

        return call


class Prog:
    def __init__(self, nc, outer):
        self.nc = nc
        self.outer = outer
        self.ph = ExitStack()
        self.q = {e: [] for e in ENGS}
        self.seq = {e: 0 for e in ENGS}
        self.wm = {e: {} for e in ENGS}
        self.needed = {e: set() for e in ENGS}
        self.inc = {e: 0 for e in ENGS}
        self.rank = {e: {} for e in ENGS}
        self.esems = {e: [] for e in ENGS}
        self.dsems = [outer.enter_context(nc.semaphore(f"sd{j}")) for j in range(NDMA)]
        self.dma_cnt = [0] * NDMA
        self.dma_rr = 0
        self.dma_rr_sw = 0
        self.pe = EngProxy(self, "pe")
        self.act = EngProxy(self, "act")
        self.dve = EngProxy(self, "dve")
        self.pool = EngProxy(self, "pool")
        self.sp = EngProxy(self, "sp")
        self.same_engine_sync = SAME_ENGINE_SYNC
        self.nt = 0
        self.nphase = 0

    def tile(self, shape, dtype, name=None):
        self.nt += 1
        t = self.ph.enter_context(self.nc.sbuf_tensor(f"t{self.nt}", list(shape), dtype))
        return V(t[:], Buf())

    def psum(self, shape, dtype=F32):
        self.nt += 1
        t = self.ph.enter_context(self.nc.psum_tensor(f"p{self.nt}", list(shape), dtype))
        return V(t[:], Buf())

    def dram(self, name, shape, dtype, kind="Internal"):
        t = self.nc.dram_tensor(name, list(shape), dtype, kind=kind)
        return V(t.ap(), Buf())

    def _deps(self, reads, writes):
        deps = {}

        def add(k, v):
            if deps.get(k, 0) < v:
                deps[k] = v

        for b in reads:
            if b.lw is not None:
                add(*b.lw)
        for b in writes:
            if b.lw is not None:
                add(*b.lw)
            for k, v in b.rd.items():
                add(k, v)
        return deps

    def _waits(self, eng, deps, force=False):
        waits = []
        wm = self.wm[eng]
        for k, v in deps.items():
            if k == eng and not force and (eng == "pe" or not self.same_engine_sync):
                continue
            if wm.get(k, 0) >= v:
                continue
            wm[k] = v
            waits.append((k, v))
            if not isinstance(k, tuple):
                self.needed[k].add(v)
        return waits

    def _mark(self, tok, reads, writes):
        k, v = tok
        for b in reads:
            if b.rd.get(k, 0) < v:
                b.rd[k] = v
        for b in writes:
            b.lw = tok
            b.rd = {}

    def op(self, eng, fn, reads, writes):
        deps = self._deps(reads, writes)
        waits = self._waits(eng, deps)
        self.seq[eng] += 1
        tok = (eng, self.seq[eng])
        self.q[eng].append([waits, fn, tok, None])
        self._mark(tok, reads, writes)
        return tok

    def dma(self, eng, kk, reads, writes):
        deps = self._deps(reads, writes)
        if eng == "pool":
            i = NDMA_HW + self.dma_rr_sw
            self.dma_rr_sw = (self.dma_rr_sw + 1) % (NDMA - NDMA_HW)
        else:
            i = self.dma_rr
            self.dma_rr = (self.dma_rr + 1) % NDMA_HW
        key = ("d", i)
        prev = self.dma_cnt[i]
        if prev > 0 and deps.get(key, 0) < prev:
            deps[key] = prev
        waits = self._waits(eng, deps)
        self.dma_cnt[i] = prev + 16
        tok = (key, prev + 16)
        self.q[eng].append([waits, lambda e: e.dma_start(**kk), None, i])
        self._mark(tok, reads, writes)
        return tok

    def _semval(self, k, v):
        if isinstance(k, tuple):
            return self.dsems[k[1]], v
        r = self.rank[k][v] - 1
        j = r // EPOCH
        while len(self.esems[k]) <= j:
            self.esems[k].append(self.outer.enter_context(self.nc.semaphore(f"s{k}{len(self.esems[k])}")))
        return self.esems[k][j], (r % EPOCH) + 1

    def phase_end(self):
        deps = {e: self.seq[e] for e in ENGS if self.seq[e] > 0}
        for i in range(NDMA):
            if self.dma_cnt[i] > 0:
                deps[("d", i)] = self.dma_cnt[i]
        for e in ENGS:
            waits = self._waits(e, dict(deps), force=True)
            self.q[e].append([waits, None, None, None])
        for e in ENGS:
            for s in sorted(self.needed[e]):
                self.inc[e] += 1
                self.rank[e][s] = self.inc[e]
            self.needed[e] = set()
        with self.nc.Block() as block:
            engmap = {"pe": block.tensor, "act": block.scalar, "dve": block.vector,
                      "pool": block.gpsimd, "sp": block.sync}
            for e in ENGS:
                q = self.q[e]
                rk = self.rank[e]

                def body(eng, q=q, rk=rk, e=e):
                    for waits, fn, tok, di in q:
                        for k, v in waits:
                            s, val = self._semval(k, v)
                            eng.wait_ge(s, val)
                        if fn is None:
                            continue
                        ins = fn(eng)
                        if di is not None:
                            ins.then_inc(self.dsems[di], 16)
                        elif tok[1] in rk:
                            s, val = self._semval(e, tok[1])
                            ins.then_inc(s, 1)

                engmap[e](body)
        for e in ENGS:
            self.q[e] = []
            self.rank[e] = {}
        self.ph.close()
        self.ph = ExitStack()
        self.nphase += 1


class RR:
    def __init__(self, tiles):
        self.t = tiles
        self.i = 0

    def next(self):
        t = self.t[self.i % len(self.t)]
        self.i += 1
        return t


def tiles(P, n, shape, dtype):
    return RR([P.tile(shape, dtype) for _ in range(n)])


def psums(P, n, shape, dtype=F32):
    return RR([P.psum(shape, dtype) for _ in range(n)])


class Ctx:
    pass


def load_consts(P, C):
    ident = P.tile([128, 128], F32)
    P.sp.dma_start(out=ident, in_=C.ident_d.fresh())
    ones = P.tile([128, 128], F32)
    P.dve.memset(out=ones, value=1.0)
    return ident, ones


def rms_block(P, C, ones, xblk, nk, gain, out_bf, ps_pool, sq_pool, tmp_pool, n=TB, dim=None):
    dim = dim or nk * 128
    ps = ps_pool.next()
    for k in range(nk):
        sq = sq_pool.next()
        P.act.activation(out=sq[:, :n], in_=xblk[:, k, :], func=AF.Square)
        P.pe.matmul(out=ps[:, :n], lhsT=ones, rhs=sq[:, :n], start=(k == 0), stop=(k == nk - 1))
    rstd = tmp_pool.next()
    P.act.activation(out=rstd[:, :n], in_=ps[:, :n], func=AF.Ln, scale=1.0 / dim, bias=C.eps_t[:, 0:1])
    P.act.activation(out=rstd[:, :n], in_=rstd[:, :n], func=AF.Exp, scale=-0.5)
    for k in range(nk):
        P.dve.scalar_tensor_tensor(out=out_bf[:, k, :], in0=xblk[:, k, :], scalar=gain[:, k:k + 1],
                                   in1=rstd[:, :n], op0=ALU.mult, op1=ALU.mult)
    return rstd


def load_w(P, wt, W, r0, nrows, c0, ncols):
    nk = (nrows + 127) // 128
    if nrows % 128 == 0:
        P.pool.dma_start(out=wt[:, :nk, :ncols],
                         in_=W[r0:r0 + nrows, c0:c0 + ncols].re("(k p) c -> p k c", p=128).fresh())
    else:
        assert nk == 1
        P.pool.dma_start(out=wt[:nrows, 0, :ncols], in_=W[r0:r0 + nrows, c0:c0 + ncols].fresh())


def linear(P, W, K, c0, ncols, rhs, blocks, evac, wpool, pspool, n=TB, r0=0):
    nk = (K + 127) // 128
    for g0 in range(0, ncols, 512):
        gw = min(512, ncols - g0)
        wt = wpool.next()
        load_w(P, wt, W, r0, K, c0 + g0, gw)
        for j in blocks:
            for cb in range(0, gw, 128):
                cs = min(128, gw - cb)
                ps = pspool.next()
                for kc in range(nk):
                    ksz = min(128, K - kc * 128)
                    P.pe.matmul(out=ps[:cs, :n], lhsT=wt[:ksz, kc, cb:cb + cs], rhs=rhs(kc, j),
                                start=(kc == 0), stop=(kc == nk - 1))
                evac(g0 + cb, cs, j, ps)


def stage_x_in(P, C):
    ident, ones = load_consts(P, C)
    xin = tiles(P, 2, [128, D], F32)
    st = tiles(P, 2, [128, 8, 128], F32)
    pp = psums(P, 4, [128, 4, 128], F32)
    for i in range(T // 128):
        xt = xin.next()
        P.sp.dma_start(out=xt, in_=C.x[i * 128:(i + 1) * 128, :].fresh())
        s = st.next()
        for half in range(2):
            ps = pp.next()
            for q in range(4):
                k = half * 4 + q
                P.pe.transpose(out=ps[:, q, :], in_=xt[:, k * 128:(k + 1) * 128], identity=ident)
            if half == 0:
                P.act.copy(out=s[:, 0:4, :], in_=ps)
            else:
                P.dve.tensor_copy(out=s[:, 4:8, :], in_=ps)
        P.act.dma_start(out=C.xT.re("k p t -> p k t")[:, :, i * 128:(i + 1) * 128].fresh(), in_=s)
    P.phase_end()


def stage_sweep(P, C, l):
    ident, ones = load_consts(P, C)
    hT = [P.tile([128, 8, TB], BF16) for _ in range(NB)]
    g = P.tile([128, 8], F32)
    P.sp.dma_start(out=g, in_=C.w["norm_mix"][l].re("(k p) -> p k", p=128).fresh())
    xb = tiles(P, 2, [128, 8, TB], F32)
    sq = tiles(P, 2, [128, TB], F32)
    tmp = tiles(P, 2, [128, TB], F32)
    pp = psums(P, 2, [128, TB])
    for j in range(NB):
        x = xb.next()
        P.sp.dma_start(out=x, in_=C.xT.re("k p t -> p k t")[:, :, j * TB:(j + 1) * TB].fresh())
        rms_block(P, C, ones, x, 8, g, hT[j], pp, sq, tmp)
    wpool = tiles(P, 3, [128, 8, 512], BF16)
    pp2 = psums(P, 4, [128, TB])
    stf = tiles(P, 3, [128, TB], F32)
    stb = tiles(P, 3, [128, TB], BF16)
    W = C.w["w_in"][l]
    cnt = [0]

    def evac_mix(coff, cs, j, ps):
        s = stf.next()
        if cnt[0] % 2 == 0:
            P.act.copy(out=s[:cs], in_=ps[:cs])
        else:
            P.dve.tensor_copy(out=s[:cs], in_=ps[:cs])
        cnt[0] += 1
        P.act.dma_start(out=C.pT[coff:coff + cs, j * TB:(j + 1) * TB].fresh(), in_=s[:cs])

    def evac_gate(coff, cs, j, ps):
        s = stb.next()
        P.act.activation(out=s[:cs], in_=ps[:cs], func=AF.Sigmoid)
        P.act.dma_start(out=C.gT[coff:coff + cs, j * TB:(j + 1) * TB].fresh(), in_=s[:cs])

    rhs = lambda kc, j: hT[j][:, kc, :]
    linear(P, W, D, 0, NMIX, rhs, range(NB), evac_mix, wpool, pp2)
    linear(P, W, D, NMIX, 4096, rhs, range(NB), evac_gate, wpool, pp2)
    if l >= 1:
        def evac_vd(coff, cs, j, ps):
            s = stf.next()
            P.act.copy(out=s[:cs], in_=ps[:cs])
            P.act.dma_start(out=C.pT[NMIX + coff:NMIX + coff + cs, j * TB:(j + 1) * TB].fresh(), in_=s[:cs])
        linear(P, C.w["rwkv_vres_down"][l - 1], D, 0, 32, rhs, range(NB), evac_vd, wpool, pp2)
    P.phase_end()


def stage_rope(P, C):
    posi = P.tile([128, T], I32)
    P.sp.dma_start(out=posi, in_=C.pos[0:1, :].pbc(128).fresh())
    posf = P.tile([128, T], F32)
    P.dve.tensor_copy(out=posf, in_=posi)
    cv = P.tile([128, 2], F32)
    P.sp.dma_start(out=cv, in_=C.ropec_d.fresh())
    tp = tiles(P, 2, [128, TB], F32)
    ti = tiles(P, 2, [128, TB], I32)
    tq = tiles(P, 2, [128, TB], F32)
    to = tiles(P, 2, [128, TB], F32)
    for j in range(NB):
        blk = slice(j * TB, (j + 1) * TB)
        for which in range(2):
            u = tp.next()
            P.dve.tensor_scalar(out=u, in0=posf[:, blk], scalar1=cv[:, 0:1], scalar2=None, op0=ALU.mult)
            P.dve.tensor_scalar(out=u, in0=u, scalar1=0.15915494309189535, scalar2=(0.25 if which == 0 else 0.0),
                                op0=ALU.mult, op1=ALU.add)
            ui = ti.next()
            P.dve.tensor_copy(out=ui, in_=u)
            uf = tq.next()
            P.dve.tensor_copy(out=uf, in_=ui)
            P.dve.tensor_tensor(out=u, in0=u, in1=uf, op=ALU.subtract)
            P.dve.tensor_scalar(out=uf, in0=u, scalar1=0.5, scalar2=None, op0=ALU.is_gt)
            P.dve.tensor_tensor(out=u, in0=u, in1=uf, op=ALU.subtract)
            P.dve.tensor_scalar(out=uf, in0=u, scalar1=-0.5, scalar2=None, op0=ALU.is_lt)
            P.dve.tensor_tensor(out=u, in0=u, in1=uf, op=ALU.add)
            o = to.next()
            P.act.activation(out=o, in_=u, func=AF.Sin, scale=6.28318)
            if which == 1:
                P.dve.tensor_scalar(out=o, in0=o, scalar1=cv[:, 1:2], scalar2=None, op0=ALU.mult)
            P.act.dma_start(out=(C.cosT if which == 0 else C.sinT)[:, blk].fresh(), in_=o)
    P.phase_end()


def stage_mla(P, C, l):
    ident, ones = load_consts(P, C)
    onesb = P.tile([128, 128], BF16)
    P.dve.memset(out=onesb, value=1.0)
    tri = P.tile([128, 128], BF16)
    P.pool.dma_start(out=tri, in_=C.tri_d.fresh())
    gq = P.tile([128, 2], F32)
    P.sp.dma_start(out=gq, in_=C.w["mla_q_norm"][l].re("(k p) -> p k", p=128).fresh())
    gkv = P.tile([128, 1], F32)
    P.sp.dma_start(out=gkv, in_=C.w["mla_kv_norm"][l].re("(k p) -> p k", p=128).fresh())
    cqn = P.tile([128, 2, T], BF16)
    ckvn = P.tile([128, 1, T], BF16)
    Krot = P.tile([128, T], BF16)
    Qn = P.tile([128, 4, T], BF16)
    Kn = P.tile([128, 4, T], BF16)
    Vt = P.tile([128, 32, 512], BF16)
    Qrot = P.tile([128, 2, T], BF16)
    ps_all = [P.psum([128, TB]) for _ in range(8)]
    gen = RR(ps_all[0:4])
    opool = RR(ps_all[4:6])
    dpool = RR(ps_all[6:8])
    xq = tiles(P, 2, [128, 2, TB], F32)
    xkv = tiles(P, 2, [128, 1, TB], F32)
    sq = tiles(P, 2, [128, TB], F32)
    tmp = tiles(P, 4, [128, TB], F32)
    csp = tiles(P, 4, [128, TB], F32)
    krp = tiles(P, 4, [128, TB], F32)
    wuq = C.w["mla_w_uq"][l]
    wukv = C.w["mla_w_ukv"][l]
    wqn = P.tile([128, 2, 512], BF16)
    wqr = P.tile([128, 2, 2, 128], BF16)
    wqs = P.tile([128, 2, 2, 128], BF16)
    wk = P.tile([128, 512], BF16)
    wv = P.tile([128, 512], BF16)
    for h in range(4):
        P.pool.dma_start(out=wqn[:, :, h * 128:(h + 1) * 128],
                         in_=wuq[:, h * 192:h * 192 + 128].re("(k p) c -> p k c", p=128).fresh())
        hp, h2 = h // 2, h % 2
        P.pool.dma_start(out=wqr[:, :, hp, h2 * 64:h2 * 64 + 64],
                         in_=wuq[:, h * 192 + 128:h * 192 + 192].re("(k p) c -> p k c", p=128).fresh())
        P.pool.dma_start(out=wqs[:, :, hp, h2 * 64:h2 * 64 + 32],
                         in_=wuq[:, h * 192 + 160:h * 192 + 192].re("(k p) c -> p k c", p=128).fresh())
        P.pool.dma_start(out=wqs[:, :, hp, h2 * 64 + 32:h2 * 64 + 64],
                         in_=wuq[:, h * 192 + 128:h * 192 + 160].re("(k p) c -> p k c", p=128).fresh())
        P.pool.dma_start(out=wk[:, h * 128:(h + 1) * 128], in_=wukv[:, h * 256:h * 256 + 128].fresh())
        P.pool.dma_start(out=wv[:, h * 128:(h + 1) * 128], in_=wukv[:, h * 256 + 128:h * 256 + 256].fresh())
    for j in range(NB):
        blk = slice(j * TB, (j + 1) * TB)
        cq = xq.next()
        P.sp.dma_start(out=cq, in_=C.pT[0:256, blk].re("(k p) t -> p k t", p=128).fresh())
        ckv = xkv.next()
        P.sp.dma_start(out=ckv[:, 0, :], in_=C.pT[256:384, blk].fresh())
        rms_block(P, C, ones, cq, 2, gq, cqn[:, :, blk], gen, sq, tmp)
        rms_block(P, C, ones, ckv, 1, gkv, ckvn[:, :, blk], gen, sq, tmp)
        cos = csp.next()
        P.sp.dma_start(out=cos, in_=C.cosT[:, blk].fresh())
        sin = csp.next()
        P.sp.dma_start(out=sin, in_=C.sinT[:, blk].fresh())
        kr = krp.next()
        ks = krp.next()
        for h2 in range(2):
            P.sp.dma_start(out=kr[h2 * 64:h2 * 64 + 64], in_=C.pT[384:448, blk].fresh())
            P.sp.dma_start(out=ks[h2 * 64:h2 * 64 + 32], in_=C.pT[416:448, blk].fresh())
            P.sp.dma_start(out=ks[h2 * 64 + 32:h2 * 64 + 64], in_=C.pT[384:416, blk].fresh())
        P.dve.tensor_tensor(out=kr, in0=kr, in1=cos, op=ALU.mult)
        P.pool.tensor_tensor(out=ks, in0=ks, in1=sin, op=ALU.mult)
        P.dve.tensor_tensor(out=Krot[:, blk], in0=kr, in1=ks, op=ALU.add)
        for h in range(4):
            ps = gen.next()
            for kc in range(2):
                P.pe.matmul(out=ps, lhsT=wqn[:, kc, h * 128:(h + 1) * 128], rhs=cqn[:, kc, blk],
                            start=(kc == 0), stop=(kc == 1))
            P.act.copy(out=Qn[:, h, blk], in_=ps)
            ps = gen.next()
            P.pe.matmul(out=ps, lhsT=wk[:, h * 128:(h + 1) * 128], rhs=ckvn[:, 0, blk], start=True, stop=True)
            P.dve.tensor_copy(out=Kn[:, h, blk], in_=ps)
        for hp in range(2):
            ps = gen.next()
            ps2 = gen.next()
            for kc in range(2):
                P.pe.matmul(out=ps, lhsT=wqr[:, kc, hp, :], rhs=cqn[:, kc, blk], start=(kc == 0), stop=(kc == 1))
            for kc in range(2):
                P.pe.matmul(out=ps2, lhsT=wqs[:, kc, hp, :], rhs=cqn[:, kc, blk], start=(kc == 0), stop=(kc == 1))
            t1 = tmp.next()
            t2 = tmp.next()
            P.dve.tensor_tensor(out=t1, in0=ps, in1=cos, op=ALU.mult)
            P.dve.tensor_tensor(out=t2, in0=ps2, in1=sin, op=ALU.mult)
            P.pool.tensor_tensor(out=Qrot[:, hp, blk], in0=t1, in1=t2, op=ALU.add)
        for i in range(4):
            tt = j * 4 + i
            ps = gen.next()
            P.pe.matmul(out=ps, lhsT=ckvn[:, 0, tt * 128:(tt + 1) * 128], rhs=wv, start=True, stop=True)
            if i % 2 == 0:
                P.act.copy(out=Vt[:, tt, :], in_=ps)
            else:
                P.dve.tensor_copy(out=Vt[:, tt, :], in_=ps)
    scale = 192.0 ** -0.5
    epool = tiles(P, 5, [128, TB], BF16)
    stb = tiles(P, 2, [128, TB], BF16)
    for h in range(4):
        hp, sub = h // 2, h % 2
        for j in range(NB):
            ops_ = opool.next()
            dps = dpool.next()
            nkt = 4 * j + 4

            def score(kt):
                r = kt - 4 * j
                q0 = 128 * r if r > 0 else 0
                qs = slice(j * TB + q0, (j + 1) * TB)
                ks_ = slice(kt * 128, (kt + 1) * 128)
                sp_ = gen.next()
                P.pe.matmul(out=sp_[:, q0:], lhsT=Kn[:, h, ks_], rhs=Qn[:, h, qs], start=True, stop=False)
                P.pe.matmul(out=sp_[:, q0:], lhsT=Krot[64 * sub:64 * sub + 64, ks_],
                            rhs=Qrot[64 * sub:64 * sub + 64, hp, qs], start=False, stop=True)
                e = epool.next()
                P.act.activation(out=e[:, q0:], in_=sp_[:, q0:], func=AF.Exp, scale=scale)
                if r >= 0:
                    P.dve.tensor_tensor(out=e[:, q0:q0 + 128], in0=e[:, q0:q0 + 128], in1=tri, op=ALU.mult)
                return (kt, e, q0)

            def pv(item):
                kt, e, q0 = item
                P.pe.matmul(out=ops_[:, q0:], lhsT=Vt[:, kt, h * 128:(h + 1) * 128], rhs=e[:, q0:],
                            start=(kt == 0), stop=(kt == nkt - 1))
                P.pe.matmul(out=dps[:, q0:], lhsT=onesb, rhs=e[:, q0:], start=(kt == 0), stop=(kt == nkt - 1))

            pend = []
            for kt in range(nkt):
                pend.append(score(kt))
                if len(pend) > 2:
                    pv(pend.pop(0))
            while pend:
                pv(pend.pop(0))
            rden = tmp.next()
            P.dve.reciprocal(out=rden, in_=dps)
            ob = stb.next()
            P.dve.tensor_tensor(out=ob, in0=ops_, in1=rden, op=ALU.mult)
            P.act.dma_start(out=C.oT[0, h * 128:(h + 1) * 128, j * TB:(j + 1) * TB].fresh(), in_=ob)
    P.phase_end()


def stage_hgrn(P, C, l):
    ident, ones = load_consts(P, C)
    identb = P.tile([128, 128], BF16)
    P.dve.tensor_copy(out=identb, in_=ident)
    cm = P.tile([64, 64], F32)
    P.sp.dma_start(out=cm, in_=C.tri_d[0:64, 0:64].fresh())
    mask = P.tile([128, T], BF16)
    P.dve.memset(out=mask, value=1.0)
    P.dve.memset(out=mask.re("p (c s) -> p c s", s=64)[:, :, 0:1], value=0.0)
    lb = P.tile([128, 4], F32)
    oml = P.tile([128, 4], F32)
    if l == 0:
        P.dve.memset(out=lb, value=0.0)
    else:
        z0 = P.tile([128, 4], F32)
        P.sp.dma_start(out=z0, in_=C.w["hgrn_lb_logits"][0].re("(h p) -> p h", p=128).fresh())
        P.sp.dma_start(out=lb, in_=C.w["hgrn_lb_logits"][1].re("(h p) -> p h", p=128).fresh())
        P.dve.tensor_tensor(out=lb, in0=lb, in1=z0, op=ALU.subtract)
        P.act.activation(out=lb, in_=lb, func=AF.Sigmoid)
    P.dve.tensor_scalar(out=oml, in0=lb, scalar1=-1.0, scalar2=1.0, op0=ALU.mult, op1=ALU.add)
    onorm = P.tile([128, 4], F32)
    P.sp.dma_start(out=onorm, in_=C.w["hgrn_o_norm"][l].re("(h p) -> p h", p=128).fresh())
    qf = P.tile([128, T], F32)
    ff = P.tile([128, T], F32)
    bb = P.tile([128, T], F32)
    eb = P.tile([128, T], F32)
    ktf = qf
    NH = 2
    Qt = [P.tile([128, T], BF16) for _ in range(NH)]
    Kt = [P.tile([128, T], BF16) for _ in range(NH)]
    Kh = [P.tile([128, T], BF16) for _ in range(NH)]
    Ib = [P.tile([128, T], BF16) for _ in range(NH)]
    oacc = [P.tile([128, T], F32) for _ in range(NH)]
    ebl = [P.tile([128, 64], F32) for _ in range(NH)]
    S = [P.tile([128, 128], F32) for _ in range(NH)]
    Sb = [P.tile([128, 128], BF16) for _ in range(NH)]
    p_tr = psums(P, 2, [64, 2, 128], BF16)
    p_at = psums(P, 2, [64, 64])
    p_o = psums(P, 2, [128, 64])
    p_kv = psums(P, 1, [128, 128])
    p_gen = psums(P, 1, [128, TB])
    vk = tiles(P, 8, [64, 2, 128], BF16)
    att = tiles(P, 8, [64, 64], BF16)
    sq = tiles(P, 2, [128, TB], F32)
    tmp = tiles(P, 2, [128, TB], F32)
    stb = tiles(P, 2, [128, TB], BF16)
    for h0 in range(0, 4, NH):
        for a in range(NH):
            h = h0 + a
            P.sp.dma_start(out=qf, in_=C.pT[1472 + h * 128:1472 + (h + 1) * 128, :].fresh())
            P.act.copy(out=Ib[a], in_=qf)
            P.sp.dma_start(out=ff, in_=C.pT[960 + h * 128:960 + (h + 1) * 128, :].fresh())
            P.act.activation(out=ff, in_=ff, func=AF.Sigmoid)
            P.dve.tensor_scalar(out=ff, in0=ff, scalar1=oml[:, h:h + 1], scalar2=lb[:, h:h + 1],
                                op0=ALU.mult, op1=ALU.add)
            P.act.activation(out=eb, in_=ff, func=AF.Ln)
            P.dve.tensor_tensor_scan(out=bb, data0=mask, data1=eb, initial=0.0, op0=ALU.mult, op1=ALU.add)
            P.dve.tensor_scalar(out=ff, in0=ff, scalar1=-1.0, scalar2=1.0, op0=ALU.mult, op1=ALU.add)
            P.act.activation(out=eb, in_=bb, func=AF.Exp)
            P.sp.dma_start(out=qf, in_=C.pT[448 + h * 128:448 + (h + 1) * 128, :].fresh())
            P.dve.tensor_tensor(out=Qt[a], in0=qf, in1=eb, op=ALU.mult)
            P.act.activation(out=ktf, in_=bb, func=AF.Exp, scale=-1.0)
            P.pool.tensor_tensor(out=ktf, in0=ktf, in1=ff, op=ALU.mult)
            P.act.copy(out=Kt[a], in_=ktf)
            P.act.copy(out=ebl[a], in_=eb.re("p (c s) -> p c s", s=64)[:, :, 63])
            P.dve.tensor_tensor(out=Kh[a].re("p (c s) -> p c s", s=64), in0=ktf.re("p (c s) -> p c s", s=64),
                                in1=ebl[a].us(2).bc([128, 64, 64]), op=ALU.mult)
        def hg_I(c):
            ch = slice(c * 64, (c + 1) * 64)
            vks, ats = [], []
            for a in range(NH):
                pt = p_tr.next()
                P.pe.transpose(out=pt[:, 0, :], in_=Ib[a][:, ch], identity=identb)
                P.pe.transpose(out=pt[:, 1, :], in_=Kh[a][:, ch], identity=identb)
                v = vk.next()
                if a % 2 == 0:
                    P.act.copy(out=v, in_=pt)
                else:
                    P.dve.tensor_copy(out=v, in_=pt)
                vks.append(v)
            for a in range(NH):
                pa = p_at.next()
                P.pe.matmul(out=pa, lhsT=Kt[a][:, ch], rhs=Qt[a][:, ch], start=True, stop=True)
                at = att.next()
                P.dve.tensor_tensor(out=at, in0=pa, in1=cm, op=ALU.mult)
                ats.append(at)
            return (c, vks, ats)

        def hg_II(item):
            c, vks, ats = item
            ch = slice(c * 64, (c + 1) * 64)
            for a in range(NH):
                po = p_o.next()
                P.pe.matmul(out=po, lhsT=vks[a][:, 0, :], rhs=ats[a], start=True, stop=(c == 0))
                if c > 0:
                    P.pe.matmul(out=po, lhsT=Sb[a], rhs=Qt[a][:, ch], start=False, stop=True)
                P.act.copy(out=oacc[a][:, ch], in_=po)
            for a in range(NH):
                pk = p_kv.next()
                P.pe.matmul(out=pk, lhsT=vks[a][:, 1, :], rhs=vks[a][:, 0, :], start=True, stop=True)
                if c == 0:
                    P.dve.tensor_copy(out=Sb[a], in_=pk)
                    P.dve.tensor_copy(out=S[a], in_=pk)
                else:
                    P.dve.scalar_tensor_tensor(out=Sb[a], in0=S[a], scalar=ebl[a][:, c:c + 1], in1=pk,
                                               op0=ALU.mult, op1=ALU.add)
                    P.dve.scalar_tensor_tensor(out=S[a], in0=S[a], scalar=ebl[a][:, c:c + 1], in1=pk,
                                               op0=ALU.mult, op1=ALU.add)

        pend = []
        for c in range(T // 64):
            pend.append(hg_I(c))
            if len(pend) > 2:
                hg_II(pend.pop(0))
        while pend:
            hg_II(pend.pop(0))
        for a in range(NH):
            h = h0 + a
            P.sp.dma_start(out=qf, in_=C.pT[1984 + h * 128:1984 + (h + 1) * 128, :].fresh())
            P.act.activation(out=qf, in_=qf, func=AF.Sigmoid)
            for j in range(NB):
                blk = slice(j * TB, (j + 1) * TB)
                s_ = sq.next()
                P.act.activation(out=s_, in_=oacc[a][:, blk], func=AF.Square)
                ps = p_gen.next()
                P.pe.matmul(out=ps, lhsT=ones, rhs=s_, start=True, stop=True)
                r = tmp.next()
                P.act.activation(out=r, in_=ps, func=AF.Ln, scale=1.0 / 128, bias=C.eps_t[:, 0:1])
                P.act.activation(out=r, in_=r, func=AF.Exp, scale=-0.5)
                P.dve.scalar_tensor_tensor(out=r, in0=oacc[a][:, blk], scalar=onorm[:, h:h + 1], in1=r,
                                           op0=ALU.mult, op1=ALU.mult)
                ob = stb.next()
                P.pool.tensor_tensor(out=ob, in0=r, in1=qf[:, blk], op=ALU.mult)
                P.act.dma_start(out=C.oT[1, h * 128:(h + 1) * 128, blk].fresh(), in_=ob)
    P.phase_end()


def frac_sin(P, u, o, ui, uf):
    P.dve.tensor_copy(out=ui, in_=u)
    P.dve.tensor_copy(out=uf, in_=ui)
    P.dve.tensor_tensor(out=u, in0=u, in1=uf, op=ALU.subtract)
    P.dve.tensor_scalar(out=uf, in0=u, scalar1=0.5, scalar2=None, op0=ALU.is_gt)
    P.dve.tensor_tensor(out=u, in0=u, in1=uf, op=ALU.subtract)
    P.dve.tensor_scalar(out=uf, in0=u, scalar1=-0.5, scalar2=None, op0=ALU.is_lt)
    P.dve.tensor_tensor(out=u, in0=u, in1=uf, op=ALU.add)
    P.act.activation(out=o, in_=u, func=AF.Sin, scale=6.28318)


def stage_s5(P, C, l):
    ident, ones = load_consts(P, C)
    w = C.w
    def ld(shape, src):
        t = P.tile(shape, F32)
        P.sp.dma_start(out=t, in_=src.fresh())
        return t
    Are = ld([128, 16], w["s5_A_re"][l].re("(t g) n -> (g n) t", g=2))
    Aim = ld([128, 16], w["s5_A_im"][l].re("(t g) n -> (g n) t", g=2))
    ls = P.tile([128, 16], F32)
    lsv = w["s5_log_step"][l].re("(t g) -> g t", g=2)
    for g2 in range(2):
        P.sp.dma_start(out=ls[g2 * 64:(g2 + 1) * 64, :], in_=lsv[g2:g2 + 1, :].bc([64, 16]).fresh())
    Bre = ld([128, 16, 16], w["s5_B_re"][l].re("(t g) n c -> (g n) t c", g=2))
    Bim = ld([128, 16, 16], w["s5_B_im"][l].re("(t g) n c -> (g n) t c", g=2))
    Cre = ld([128, 4, 64], w["s5_C_re"][l].re("(u g) c n -> (g c) u n", g=8))
    Cim = ld([128, 4, 64], w["s5_C_im"][l].re("(u g) c n -> (g c) u n", g=8))
    Dv = ld([128, 4], w["s5_D"][l].re("(u p) -> p u", p=128))
    bglu = ld([128, 4], w["s5_b_glu"][l].re("(u p) -> p u", p=128))
    mB = ld([128, 4, 128], C.maskB_d.re("r p c -> p r c"))
    mC = ld([128, 4, 128], C.maskC_d.re("r p c -> p r c"))
    wglu = P.tile([128, 4, 512], BF16)
    load_w(P, wglu, w["s5_w_glu"][l], 0, 512, 0, 512)
    sm = [P.tile([128, 16], F32) for _ in range(12)]
    smi = P.tile([128, 16], I32)
    dt, mag, th, cs, sn, abre, abim, zre, zim, t0, t1, t2 = sm
    P.act.activation(out=dt, in_=ls, func=AF.Exp)
    P.dve.tensor_scalar(out=Are, in0=Are, scalar1=-1e-4, scalar2=None, op0=ALU.min)
    P.dve.tensor_tensor(out=t0, in0=dt, in1=Are, op=ALU.mult)
    P.act.activation(out=mag, in_=t0, func=AF.Exp)
    P.dve.tensor_tensor(out=th, in0=dt, in1=Aim, op=ALU.mult)
    P.dve.tensor_scalar(out=t0, in0=th, scalar1=0.15915494309189535, scalar2=0.25, op0=ALU.mult, op1=ALU.add)
    frac_sin(P, t0, cs, smi, t1)
    P.dve.tensor_scalar(out=t0, in0=th, scalar1=0.15915494309189535, scalar2=None, op0=ALU.mult)
    frac_sin(P, t0, sn, smi, t1)
    P.dve.tensor_tensor(out=abre, in0=mag, in1=cs, op=ALU.mult)
    P.dve.tensor_tensor(out=abim, in0=mag, in1=sn, op=ALU.mult)
    P.dve.tensor_tensor(out=t0, in0=Are, in1=Are, op=ALU.mult)
    P.dve.tensor_tensor(out=t1, in0=Aim, in1=Aim, op=ALU.mult)
    P.dve.tensor_tensor(out=t0, in0=t0, in1=t1, op=ALU.add)
    P.dve.reciprocal(out=t0, in_=t0)
    P.dve.tensor_scalar(out=t1, in0=abre, scalar1=-1.0, scalar2=None, op0=ALU.add)
    P.dve.tensor_tensor(out=zre, in0=t1, in1=Are, op=ALU.mult)
    P.dve.tensor_tensor(out=t2, in0=abim, in1=Aim, op=ALU.mult)
    P.dve.tensor_tensor(out=zre, in0=zre, in1=t2, op=ALU.add)
    P.dve.tensor_tensor(out=zre, in0=zre, in1=t0, op=ALU.mult)
    P.dve.tensor_tensor(out=zim, in0=abim, in1=Are, op=ALU.mult)
    P.dve.tensor_tensor(out=t2, in0=t1, in1=Aim, op=ALU.mult)
    P.dve.tensor_tensor(out=zim, in0=zim, in1=t2, op=ALU.subtract)
    P.dve.tensor_tensor(out=zim, in0=zim, in1=t0, op=ALU.mult)
    bbre = P.tile([128, 16, 16], F32)
    bbim = P.tile([128, 16, 16], F32)
    tb = P.tile([128, 16, 16], F32)
    zr_b = zre.us(2).bc([128, 16, 16])
    zi_b = zim.us(2).bc([128, 16, 16])
    P.dve.tensor_tensor(out=bbre, in0=Bre, in1=zr_b, op=ALU.mult)
    P.dve.tensor_tensor(out=tb, in0=Bim, in1=zi_b, op=ALU.mult)
    P.dve.tensor_tensor(out=bbre, in0=bbre, in1=tb, op=ALU.subtract)
    P.dve.tensor_tensor(out=bbim, in0=Bim, in1=zr_b, op=ALU.mult)
    P.dve.tensor_tensor(out=tb, in0=Bre, in1=zi_b, op=ALU.mult)
    P.dve.tensor_tensor(out=bbim, in0=bbim, in1=tb, op=ALU.add)
    LB = P.tile([128, 16, 2, 128], BF16)
    LC = P.tile([128, 16, 2, 128], BF16)
    xt = tiles(P, 3, [128, 128], F32)
    ptr = psums(P, 1, [128, 128])
    for st in range(16):
        pr, ut = st % 4, st // 4
        for ri, src in ((0, bbre), (1, bbim)):
            x = xt.next()
            P.dve.tensor_tensor(out=x.re("p (g c) -> p g c", c=16), in0=mB[:, pr, :].re("p (g c) -> p g c", c=16),
                                in1=src[:, st, :].us(1).bc([128, 8, 16]), op=ALU.mult)
            ps = ptr.next()
            P.pe.transpose(out=ps, in_=x, identity=ident)
            P.act.copy(out=LB[:, st, ri, :], in_=ps)
        for ri, src in ((0, Cre), (1, Cim)):
            x = xt.next()
            P.dve.tensor_tensor(out=x.re("p (g n) -> p g n", n=64), in0=mC[:, pr, :].re("p (g n) -> p g n", n=64),
                                in1=src[:, ut, :].us(1).bc([128, 2, 64]), op=ALU.mult)
            ps = ptr.next()
            P.pe.transpose(out=ps, in_=x, identity=ident)
            P.act.activation(out=LC[:, st, ri, :], in_=ps, func=AF.Copy, scale=(1.0 if ri == 0 else -1.0))
    Ec = P.tile([128, 16, TB], F32)
    Es = P.tile([128, 16, TB], F32)
    tt = tiles(P, 4, [128, 256], F32)
    for st in range(16):
        P.act.copy(out=Ec[:, st, 0:1], in_=cs[:, st:st + 1])
        P.act.copy(out=Es[:, st, 0:1], in_=sn[:, st:st + 1])
        n = 1
        while n < TB:
            cre = Ec[:, st, n - 1:n]
            cim = Es[:, st, n - 1:n]
            a = tt.next()
            b = tt.next()
            P.dve.tensor_scalar(out=a[:, :n], in0=Es[:, st, 0:n], scalar1=cim, scalar2=None, op0=ALU.mult)
            P.dve.tensor_scalar(out=b[:, :n], in0=Ec[:, st, 0:n], scalar1=cim, scalar2=None, op0=ALU.mult)
            P.dve.scalar_tensor_tensor(out=Ec[:, st, n:2 * n], in0=Ec[:, st, 0:n], scalar=cre, in1=a[:, :n],
                                       op0=ALU.mult, op1=ALU.subtract)
            P.dve.scalar_tensor_tensor(out=Es[:, st, n:2 * n], in0=Es[:, st, 0:n], scalar=cre, in1=b[:, :n],
                                       op0=ALU.mult, op1=ALU.add)
            n *= 2
    car = P.tile([128, 16, 2], F32)
    P.dve.memset(out=car, value=0.0)
    ufp = tiles(P, 2, [128, 4, TB], F32)
    ubp = tiles(P, 2, [128, 4, TB], BF16)
    pb = psums(P, 4, [128, TB])
    py = psums(P, 2, [128, TB])
    pg = psums(P, 1, [128, TB])
    wk = tiles(P, 20, [128, TB], F32)
    wk2 = tiles(P, 5, [128, TB], F32)
    xb = tiles(P, 6, [128, TB], BF16)
    zfp = tiles(P, 1, [128, 4, TB], F32)
    zbp = tiles(P, 1, [128, 4, TB], BF16)
    stb = tiles(P, 2, [128, TB], BF16)
    for j in range(NB):
        blk = slice(j * TB, (j + 1) * TB)
        uf = ufp.next()
        P.sp.dma_start(out=uf, in_=C.pT[2496:3008, blk].re("(u p) t -> p u t", p=128).fresh())
        ub = ubp.next()
        P.act.copy(out=ub, in_=uf)
        zf = zfp.next()
        zb = zbp.next()
        ypss = {}

        def stA(st):
            ut = st // 4
            bre = pb.next()
            bim = pb.next()
            P.pe.matmul(out=bre, lhsT=LB[:, st, 0, :], rhs=ub[:, ut, :], start=True, stop=True)
            P.pe.matmul(out=bim, lhsT=LB[:, st, 1, :], rhs=ub[:, ut, :], start=True, stop=True)
            cos = Ec[:, st, :]
            sin = Es[:, st, :]
            t1_, t2_, t3_, t4_, wre, wim = [wk.next() for _ in range(6)]
            P.dve.tensor_tensor(out=t1_, in0=bre, in1=cos, op=ALU.mult)
            P.dve.tensor_tensor(out=t2_, in0=bim, in1=sin, op=ALU.mult)
            P.pool.tensor_tensor(out=wre, in0=t1_, in1=t2_, op=ALU.add)
            P.dve.tensor_tensor(out=t3_, in0=bim, in1=cos, op=ALU.mult)
            P.dve.tensor_tensor(out=t4_, in0=bre, in1=sin, op=ALU.mult)
            P.pool.tensor_tensor(out=wim, in0=t3_, in1=t4_, op=ALU.subtract)
            return dict(st=st, wre=wre, wim=wim, cos=cos, sin=sin)

        def stB(d):
            st = d["st"]
            zr, zi, a1, a2, a3, a4 = [wk.next() for _ in range(6)]
            rho = mag[:, st:st + 1].bc([128, TB])
            P.dve.tensor_tensor_scan(out=zr, data0=rho, data1=d["wre"], initial=car[:, st, 0:1], op0=ALU.mult, op1=ALU.add)
            P.dve.tensor_tensor_scan(out=zi, data0=rho, data1=d["wim"], initial=car[:, st, 1:2], op0=ALU.mult, op1=ALU.add)
            P.pool.tensor_tensor(out=a1, in0=zr, in1=d["cos"], op=ALU.mult)
            P.pool.tensor_tensor(out=a2, in0=zi, in1=d["sin"], op=ALU.mult)
            P.pool.tensor_tensor(out=a3, in0=zr, in1=d["sin"], op=ALU.mult)
            P.pool.tensor_tensor(out=a4, in0=zi, in1=d["cos"], op=ALU.mult)
            d.update(a1=a1, a2=a2, a3=a3, a4=a4)
            return d

        def stC(d):
            st = d["st"]
            ut, pr = st // 4, st % 4
            xre = xb.next()
            xim = xb.next()
            P.dve.tensor_tensor(out=xre, in0=d["a1"], in1=d["a2"], op=ALU.subtract)
            P.dve.tensor_tensor(out=xim, in0=d["a3"], in1=d["a4"], op=ALU.add)
            P.dve.tensor_tensor(out=car[:, st, 0:1], in0=d["a1"][:, TB - 1:TB], in1=d["a2"][:, TB - 1:TB], op=ALU.subtract)
            P.dve.tensor_tensor(out=car[:, st, 1:2], in0=d["a3"][:, TB - 1:TB], in1=d["a4"][:, TB - 1:TB], op=ALU.add)
            if pr == 0:
                ypss[ut] = py.next()
            yps = ypss[ut]
            P.pe.matmul(out=yps, lhsT=LC[:, st, 0, :], rhs=xre, start=(pr == 0), stop=False)
            P.pe.matmul(out=yps, lhsT=LC[:, st, 1, :], rhs=xim, start=False, stop=(pr == 3))
            if pr == 3:
                y, sq_, a_, in_, sg_ = [wk2.next() for _ in range(5)]
                P.dve.scalar_tensor_tensor(out=y, in0=uf[:, ut, :], scalar=Dv[:, ut:ut + 1], in1=yps, op0=ALU.mult, op1=ALU.add)
                P.act.activation(out=sq_, in_=y, func=AF.Square)
                P.dve.tensor_scalar(out=a_, in0=sq_, scalar1=0.044715, scalar2=1.0, op0=ALU.mult, op1=ALU.add)
                P.pool.tensor_tensor(out=in_, in0=a_, in1=y, op=ALU.mult)
                P.act.activation(out=sg_, in_=in_, func=AF.Sigmoid, scale=1.5957691216057308)
                P.dve.tensor_tensor(out=zf[:, ut, :], in0=y, in1=sg_, op=ALU.mult)
                P.act.copy(out=zb[:, ut, :], in_=zf[:, ut, :])

        qa, qb = [], []
        for st in range(16):
            qa.append(stA(st))
            if len(qa) > 1:
                qb.append(stB(qa.pop(0)))
            if len(qb) > 1:
                stC(qb.pop(0))
        while qa:
            qb.append(stB(qa.pop(0)))
            if len(qb) > 1:
                stC(qb.pop(0))
        while qb:
            stC(qb.pop(0))
        for cb in range(4):
            ps = pg.next()
            for kc in range(4):
                P.pe.matmul(out=ps, lhsT=wglu[:, kc, cb * 128:(cb + 1) * 128], rhs=zb[:, kc, :], start=(kc == 0), stop=(kc == 3))
            sg_ = wk2.next()
            P.act.activation(out=sg_, in_=ps, func=AF.Sigmoid, bias=bglu[:, cb:cb + 1])
            ob = stb.next()
            P.dve.tensor_tensor(out=ob, in0=zf[:, cb, :], in1=sg_, op=ALU.mult)
            P.act.dma_start(out=C.oT[2, cb * 128:(cb + 1) * 128, blk].fresh(), in_=ob)
    P.phase_end()


def stage_rwkv(P, C, l):
    ident, ones = load_consts(P, C)
    w = C.w
    identb = P.tile([128, 128], BF16)
    P.dve.tensor_copy(out=identb, in_=ident)

    def ld(shape, src, dt=F32):
        t = P.tile(shape, dt)
        P.sp.dma_start(out=t, in_=src.fresh())
        return t

    bones = ld([128, 128], C.bones_d)
    mask4 = ld([64, 256], C.mask4_d)
    mls = ld([64, 64], C.mls_d)
    tiny = P.tile([128, 1], F32)
    P.dve.memset(out=tiny, value=1e-12)
    gneps = P.tile([128, 1], F32)
    P.dve.memset(out=gneps, value=64e-5)
    rmask = P.tile([128, TB], BF16)
    P.dve.memset(out=rmask, value=1.0)
    P.dve.memset(out=rmask.re("p (c s) -> p c s", s=64)[:, :, 0:1], value=0.0)
    mu = w["rwkv_mu"][l]
    v4 = lambda a: a.re("(u p) -> p u", p=128)
    mu_r, mu_k, mu_v = ld([128, 4], v4(mu[0:512])), ld([128, 4], v4(mu[512:1024])), ld([128, 4], v4(mu[1024:1536]))
    mu_s = P.tile([128, 3], F32)
    P.sp.dma_start(out=mu_s[0:64, 0:1], in_=mu[1536:1600].re("(p o) -> p o", o=1).fresh())
    P.sp.dma_start(out=mu_s[0:64, 1:2], in_=mu[1600:1664].re("(p o) -> p o", o=1).fresh())
    P.sp.dma_start(out=mu_s[:, 2:3], in_=mu[1664:1792].re("(p o) -> p o", o=1).fresh())
    om = {}
    for nm, t in (("r", mu_r), ("k", mu_k), ("v", mu_v)):
        o = P.tile([128, 4], F32)
        P.dve.tensor_scalar(out=o, in0=t, scalar1=-1.0, scalar2=1.0, op0=ALU.mult, op1=ALU.add)
        om[nm] = o
    omu_s = P.tile([128, 3], F32)
    P.dve.memset(out=omu_s, value=0.0)
    P.dve.tensor_scalar(out=omu_s[0:64, 0:2], in0=mu_s[0:64, 0:2], scalar1=-1.0, scalar2=1.0, op0=ALU.mult, op1=ALU.add)
    P.dve.tensor_scalar(out=omu_s[:, 2:3], in0=mu_s[:, 2:3], scalar1=-1.0, scalar2=1.0, op0=ALU.mult, op1=ALU.add)
    w0 = ld([128, 4], v4(w["rwkv_w0"][l]))
    a0 = ld([128, 4], v4(w["rwkv_a0"][l]))
    kkp = ld([128, 4], v4(w["rwkv_k_k"][l]))
    kap = ld([128, 4], v4(w["rwkv_k_a"][l]))
    omka = P.tile([128, 4], F32)
    P.dve.tensor_scalar(out=omka, in0=kap, scalar1=-1.0, scalar2=1.0, op0=ALU.mult, op1=ALU.add)
    rkp = ld([128, 4], w["rwkv_r_k"][l].re("(u s) k -> (s k) u", s=2))
    lnw = ld([128, 4], v4(w["rwkv_ln_w"][l]))
    lnb = ld([128, 4], v4(w["rwkv_ln_b"][l]))
    wup = P.tile([128, 1, 512], BF16)
    load_w(P, wup, w["rwkv_w_up"][l], 0, 64, 0, 512)
    aup = P.tile([128, 1, 512], BF16)
    load_w(P, aup, w["rwkv_a_up"][l], 0, 64, 0, 512)
    gup = P.tile([128, 1, 512], BF16)
    load_w(P, gup, w["rwkv_g_up"][l], 0, 128, 0, 512)
    if l >= 1:
        vup = P.tile([128, 1, 512], BF16)
        load_w(P, vup, w["rwkv_vres_up"][l - 1], 0, 32, 0, 512)
        vbias = ld([128, 4], v4(w["rwkv_vres_bias"][l - 1]))
    big = lambda: P.tile([128, 4, TB], F32)
    r_, k_, v_, lw, lP, a_, kkn, kh, eP, t1 = [big() for _ in range(10)]
    enP = k_
    g_ = P.tile([128, 4, TB], BF16)
    t2 = lP
    raw = tiles(P, 1, [128, 4, TB + 1], F32)
    raws = tiles(P, 2, [128, TB + 1], F32)
    sm_f = tiles(P, 3, [128, TB], F32)
    sm_b = tiles(P, 3, [128, TB], BF16)
    QRY = P.tile([128, 4, 8, 2, 64], BF16)
    KEYz = P.tile([128, 4, 8, 2, 2, 64], BF16)
    P.dve.memset(out=KEYz, value=0.0)
    Az = P.tile([128, 4, 8, 2, 64], BF16)
    P.dve.memset(out=Az, value=0.0)
    KB = P.tile([128, 4, 8, 64], BF16)
    HAT = P.tile([128, 4, 8, 2, 64], BF16)
    Vb = P.tile([128, 4, TB], BF16)
    yacc = a_
    PC = P.tile([128, 4, 8], F32)
    S = [P.tile([128, 64], F32) for _ in range(4)]
    Sb = [P.tile([128, 64], BF16) for _ in range(4)]
    Sbz = [P.tile([128, 2, 64], BF16) for _ in range(4)]
    for u in range(4):
        P.dve.memset(out=Sbz[u], value=0.0)
    banks = psums(P, 8, [128, TB])
    gen = banks
    tmp_tm = tiles(P, 5, [64, 2, 3, 128], BF16)
    amp = tiles(P, 5, [64, 4, 256], BF16)
    lbp = tiles(P, 4, [64, 4, 64], BF16)
    rbp = tiles(P, 6, [64, 4, 64], BF16)
    xp = tiles(P, 8, [64, 4, 64], BF16)
    sqp = tiles(P, 6, [64, 2, 4, 64], BF16)
    id64 = P.tile([64, 64], BF16)
    P.dve.tensor_copy(out=id64, in_=ident[0:64, 0:64])
    stb = tiles(P, 2, [128, TB], BF16)
    stf = tiles(P, 2, [128, TB], F32)
    c4 = lambda t: t.re("p u (c s) -> p u c s", s=64)

    def load_raw(t, rows, j, np_=128, multi=True):
        lo = j * TB
        if multi:
            src = C.pT[rows[0]:rows[1], :].re("(u p) t -> p u t", p=128)
            if j == 0:
                P.dve.memset(out=t[:, :, 0:1], value=0.0)
                P.sp.dma_start(out=t[:, :, 1:], in_=src[:, :, 0:TB].fresh())
            else:
                P.sp.dma_start(out=t, in_=src[:, :, lo - 1:lo + TB].fresh())
        else:
            src = C.pT[rows[0]:rows[1], :]
            if j == 0:
                P.dve.memset(out=t[:np_, 0:1], value=0.0)
                P.sp.dma_start(out=t[:np_, 1:], in_=src[:, 0:TB].fresh())
            else:
                P.sp.dma_start(out=t[:np_], in_=src[:, lo - 1:lo + TB].fresh())

    RW0 = 3008
    for j in range(C.opts.get("rw_blocks", NB)):
        blk = slice(j * TB, (j + 1) * TB)
        for (dst, roff, m_, o_) in ((r_, 0, mu_r, om["r"]), (k_, 512, mu_k, om["k"]), (v_, 1024, mu_v, om["v"])):
            x = raw.next()
            load_raw(x, (RW0 + roff, RW0 + roff + 512), j)
            for u in range(4):
                P.pool.tensor_scalar(out=dst[:, u, :], in0=x[:, u, 1:], scalar1=o_[:, u:u + 1], scalar2=0.0, op0=ALU.mult, op1=ALU.add)
                P.dve.scalar_tensor_tensor(out=dst[:, u, :], in0=x[:, u, 0:TB], scalar=m_[:, u:u + 1], in1=dst[:, u, :],
                                           op0=ALU.mult, op1=ALU.add)
        sm = []
        for ci, (roff, np_) in enumerate(((1536, 64), (1600, 64), (1664, 128))):
            x = raws.next()
            load_raw(x, (RW0 + roff, RW0 + roff + np_), j, np_, multi=False)
            o = sm_f.next()
            P.pool.tensor_scalar(out=o[:np_], in0=x[:np_, 1:], scalar1=omu_s[:np_, ci:ci + 1], scalar2=0.0, op0=ALU.mult, op1=ALU.add)
            P.dve.scalar_tensor_tensor(out=o[:np_], in0=x[:np_, 0:TB], scalar=mu_s[:np_, ci:ci + 1], in1=o[:np_],
                                       op0=ALU.mult, op1=ALU.add)
            sm.append(o)
        wdm, adm, gdm = sm
        tw = sm_b.next()
        P.act.activation(out=tw[:64], in_=wdm[:64], func=AF.Tanh)
        adb = sm_b.next()
        P.act.copy(out=adb[:64], in_=adm[:64])
        sgd = sm_b.next()
        P.act.activation(out=sgd, in_=gdm, func=AF.Sigmoid)
        for u in range(4):
            cs_ = slice(u * 128, (u + 1) * 128)
            ps = gen.next()
            P.pe.matmul(out=ps, lhsT=wup[:64, 0, cs_], rhs=tw[:64], start=True, stop=True)
            P.act.activation(out=lw[:, u, :], in_=ps, func=AF.Sigmoid, bias=w0[:, u:u + 1])
            ps = gen.next()
            P.pe.matmul(out=ps, lhsT=aup[:64, 0, cs_], rhs=adb[:64], start=True, stop=True)
            P.act.activation(out=a_[:, u, :], in_=ps, func=AF.Sigmoid, bias=a0[:, u:u + 1])
            ps = gen.next()
            P.pe.matmul(out=ps, lhsT=gup[:, 0, cs_], rhs=sgd, start=True, stop=True)
            P.act.copy(out=g_[:, u, :], in_=ps)
        P.dve.tensor_scalar(out=lw, in0=lw, scalar1=-0.6065306597126334, scalar2=None, op0=ALU.mult)
        if l >= 1:
            hv = sm_f.next()
            P.sp.dma_start(out=hv[:32], in_=C.pT[NMIX:NMIX + 32, blk].fresh())
            hvb = sm_b.next()
            P.act.copy(out=hvb[:32], in_=hv[:32])
            vf = raw.next()
            P.sp.dma_start(out=vf[:, :, 0:TB], in_=C.vfT[:, blk].re("(u p) t -> p u t", p=128).fresh())
            for u in range(4):
                ps = gen.next()
                P.pe.matmul(out=ps, lhsT=vup[:32, 0, u * 128:(u + 1) * 128], rhs=hvb[:32], start=True, stop=True)
                sv = sm_f.next()
                P.act.activation(out=sv, in_=ps, func=AF.Sigmoid, bias=vbias[:, u:u + 1])
                P.pool.tensor_tensor(out=vf[:, u, 0:TB], in0=vf[:, u, 0:TB], in1=v_[:, u, :], op=ALU.subtract)
                P.dve.tensor_tensor(out=vf[:, u, 0:TB], in0=vf[:, u, 0:TB], in1=sv, op=ALU.mult)
                P.pool.tensor_tensor(out=v_[:, u, :], in0=v_[:, u, :], in1=vf[:, u, 0:TB], op=ALU.add)
        else:
            P.act.dma_start(out=C.vfT[:, blk].re("(u p) t -> p u t", p=128).fresh(), in_=v_)
        for u in range(4):
            P.dve.tensor_scalar(out=kkn[:, u, :], in0=k_[:, u, :], scalar1=kkp[:, u:u + 1], scalar2=None, op0=ALU.mult)
            sq_ = sm_f.next()
            P.act.activation(out=sq_, in_=kkn[:, u, :], func=AF.Square)
            ps = gen.next()
            P.pe.matmul(out=ps, lhsT=bones, rhs=sq_, start=True, stop=True)
            rs = sm_f.next()
            P.act.activation(out=rs, in_=ps, func=AF.Ln, bias=tiny[:, 0:1])
            P.act.activation(out=rs, in_=rs, func=AF.Exp, scale=-0.5)
            P.pool.tensor_tensor(out=kkn[:, u, :], in0=kkn[:, u, :], in1=rs, op=ALU.mult)
            P.dve.tensor_scalar(out=t1[:, u, :], in0=a_[:, u, :], scalar1=kap[:, u:u + 1], scalar2=omka[:, u:u + 1],
                                op0=ALU.mult, op1=ALU.add)
        P.dve.tensor_tensor(out=kh, in0=k_, in1=t1, op=ALU.mult)
        for u in range(4):
            P.dve.tensor_tensor_scan(out=lP[:, u, :], data0=rmask, data1=lw[:, u, :], initial=0.0, op0=ALU.mult, op1=ALU.add)
        P.act.activation(out=eP, in_=lP, func=AF.Exp)
        P.act.activation(out=enP, in_=lP, func=AF.Exp, scale=-1.0)
        P.dve.tensor_tensor(out=lw, in0=lP, in1=lw, op=ALU.subtract)
        P.act.activation(out=lw, in_=lw, func=AF.Exp)
        P.act.copy(out=PC, in_=c4(eP)[:, :, :, 63])
        P.dve.tensor_tensor(out=t1, in0=kkn, in1=a_, op=ALU.mult)
        P.dve.tensor_tensor(out=t1, in0=t1, in1=enP, op=ALU.mult)
        P.dve.tensor_tensor(out=t2, in0=kh, in1=enP, op=ALU.mult)
        for u in range(4):
            c3 = lambda t: t[:, u, :].re("p (c s) -> p c s", s=64)
            P.dve.scalar_tensor_tensor(out=QRY[:, u, :, 0, :], in0=c3(kkn), scalar=-1.0, in1=c3(lw), op0=ALU.mult, op1=ALU.mult)
            P.pool.tensor_tensor(out=QRY[:, u, :, 1, :], in0=c3(r_), in1=c3(eP), op=ALU.mult)
            P.act.copy(out=KB[:, u, :, :], in_=c3(t1))
            for sub in range(2):
                hs = slice(64 * sub, 64 * sub + 64)
                P.act.copy(out=KEYz[hs, u, :, sub, 0, :], in_=c3(t1)[hs])
                P.act.copy(out=KEYz[hs, u, :, sub, 1, :], in_=c3(t2)[hs])
                P.pool.tensor_copy(out=Az[hs, u, :, sub, :], in_=QRY[hs, u, :, 0, :])
            pcb = PC[:, u, :].us(2).bc([128, 8, 64])
            P.dve.tensor_tensor(out=HAT[:, u, :, 0, :], in0=c3(t1), in1=pcb, op=ALU.mult)
            P.pool.tensor_tensor(out=HAT[:, u, :, 1, :], in0=c3(t2), in1=pcb, op=ALU.mult)
        P.act.copy(out=Vb, in_=v_)
        hd = lambda g, hl: (2 * g + hl // 2, hl // 2, hl % 2)
        v4 = lambda bk, lo=0: bk[:64, lo:lo + 256].re("p (h c) -> p h c", h=4)
        store = {}

        def square(N, L):
            bk = banks.next()
            pn, pl2 = v4(bk, 0), v4(bk, 256)
            for hl in range(4):
                P.pe.matmul(out=pn[:, hl, :], lhsT=L[:, hl, :], rhs=N[:, hl, :], start=True, stop=True)
            for hl in range(4):
                P.pe.matmul(out=pl2[:, hl, :], lhsT=N[:, hl, :], rhs=L[:, hl, :], start=True, stop=True)
            NL = sqp.next()
            P.act.copy(out=NL, in_=bk[:64, :].re("p (a h c) -> p a h c", a=2, h=4))
            return NL

        def gen_I(cc, g):
            c0 = cc * 64
            bk = banks.next()
            pt = bk.bitcast(BF16)[:64, 0:768].re("p (a b c) -> p a b c", a=2, b=3)
            for ul in range(2):
                u = 2 * g + ul
                P.pe.transpose(out=pt[:, ul, 0, :], in_=HAT[:, u, cc, 0, :], identity=identb)
                P.pe.transpose(out=pt[:, ul, 1, :], in_=HAT[:, u, cc, 1, :], identity=identb)
                P.pe.transpose(out=pt[:, ul, 2, :], in_=Vb[:, u, c0:c0 + 64], identity=identb)
            tm = tmp_tm.next()
            P.act.copy(out=tm, in_=pt)
            b1, b2, bl = banks.next(), banks.next(), banks.next()
            v1 = b1[:64].re("p (h c) -> p h c", h=4)
            v2 = b2[:64].re("p (h c) -> p h c", h=4)
            vl = v4(bl)
            for hl in range(4):
                u, ul, sub = hd(g, hl)
                qry = QRY[:, u, cc, :, :].re("p a s -> p (a s)")
                P.pe.matmul(out=v1[:, hl, :], lhsT=KEYz[:, u, cc, sub, 0, :], rhs=qry, start=True, stop=True)
                P.pe.matmul(out=v2[:, hl, :], lhsT=KEYz[:, u, cc, sub, 1, :], rhs=qry, start=True, stop=True)
                P.pe.matmul(out=vl[:, hl, :], lhsT=Az[:, u, cc, sub, :], rhs=KB[:, u, cc, :], start=True, stop=True)
            am = amp.next()
            P.dve.tensor_tensor(out=am[:, :, 0:128], in0=v1, in1=mask4[:, 0:128].us(1).bc([64, 4, 128]), op=ALU.mult)
            P.dve.tensor_tensor(out=am[:, :, 128:256], in0=v2, in1=mask4[:, 128:256].us(1).bc([64, 4, 128]), op=ALU.mult)
            Lb = lbp.next()
            P.dve.tensor_tensor(out=Lb, in0=vl, in1=mls.us(1).bc([64, 4, 64]), op=ALU.mult)
            yield
            X = xp.next()
            P.pool.tensor_tensor(out=X, in0=am[:, :, 0:64], in1=id64.us(1).bc([64, 4, 64]), op=ALU.add)
            NL = square(am[:, :, 0:64], Lb)
            yield
            for k in range(5):
                N, L = NL[:, 0], NL[:, 1]
                bk = banks.next()
                pa = v4(bk)
                for hl in range(4):
                    P.pe.matmul(out=pa[:, hl, :], lhsT=L[:, hl, :], rhs=X[:, hl, :], start=True, stop=True)
                Xn = xp.next()
                P.dve.tensor_tensor(out=Xn, in0=pa, in1=X, op=ALU.add)
                X = Xn
                if k < 4:
                    NL = square(N, L)
                yield
            store[(cc, g)] = (tm, am, X)

        def gen_II(cc, g):
            c0 = cc * 64
            first_chunk = (j == 0 and cc == 0)
            tm, am, WT = store.pop((cc, g))
            bk = banks.next()
            pr_ = v4(bk)
            for hl in range(4):
                u, ul, sub = hd(g, hl)
                p0 = 64 * sub
                if not first_chunk:
                    P.pe.matmul(out=pr_[:, hl, :], lhsT=Az[:, u, cc, sub, :], rhs=Sb[u], start=True, stop=False)
                P.pe.matmul(out=pr_[:, hl, :], lhsT=am[:, hl, 128:192], rhs=tm[:, ul, 2, p0:p0 + 64],
                            start=first_chunk, stop=True)
            Rb = rbp.next()
            P.act.copy(out=Rb, in_=pr_)
            yield
            bk = banks.next()
            pu = v4(bk)
            for hl in range(4):
                P.pe.matmul(out=pu[:, hl, :], lhsT=WT[:, hl, :], rhs=Rb[:, hl, :], start=True, stop=True)
            Ub = rbp.next()
            P.dve.tensor_copy(out=Ub, in_=pu)
            yield
            bk = banks.next()
            py = bk[:, 0:128].re("p (u c) -> p u c", u=2)
            for hl in range(4):
                u, ul, sub = hd(g, hl)
                p0 = 64 * sub
                if not first_chunk:
                    P.pe.matmul(out=py[p0:p0 + 64, ul, :], lhsT=Sbz[u][:, sub, :], rhs=QRY[:, u, cc, 1, :],
                                start=True, stop=False)
                P.pe.matmul(out=py[p0:p0 + 64, ul, :], lhsT=Ub[:, hl, :], rhs=am[:, hl, 64:128],
                            start=first_chunk, stop=False)
                P.pe.matmul(out=py[p0:p0 + 64, ul, :], lhsT=tm[:, ul, 2, p0:p0 + 64], rhs=am[:, hl, 192:256],
                            start=False, stop=True)
            P.act.copy(out=yacc[:, 2 * g:2 * g + 2, c0:c0 + 64], in_=py)
            bk2 = banks.next()
            ps_ = bk2[:, 0:128].re("p (u c) -> p u c", u=2)
            for hl in range(4):
                u, ul, sub = hd(g, hl)
                p0 = 64 * sub
                P.pe.matmul(out=ps_[p0:p0 + 64, ul, :], lhsT=tm[:, ul, 0, p0:p0 + 64], rhs=Ub[:, hl, :], start=True, stop=False)
                P.pe.matmul(out=ps_[p0:p0 + 64, ul, :], lhsT=tm[:, ul, 1, p0:p0 + 64], rhs=tm[:, ul, 2, p0:p0 + 64],
                            start=False, stop=True)
            for ul in range(2):
                u = 2 * g + ul
                if first_chunk:
                    P.dve.tensor_copy(out=S[u], in_=ps_[:, ul, :])
                else:
                    P.dve.scalar_tensor_tensor(out=S[u], in0=S[u], scalar=PC[:, u, cc:cc + 1], in1=ps_[:, ul, :],
                                               op0=ALU.mult, op1=ALU.add)
                P.pool.tensor_copy(out=Sb[u], in_=S[u])
                P.pool.tensor_copy(out=Sbz[u][0:64, 0, :], in_=S[u][0:64])
                P.pool.tensor_copy(out=Sbz[u][64:128, 1, :], in_=S[u][64:128])
            yield

        NCH = C.opts.get("rw_chunks", 8)
        LA = 1
        pendI = [(cc, g) for cc in range(NCH) for g in range(2)]
        nextII = {0: 0, 1: 0}
        actI, actII = [], {}
        doneI = set()
        while pendI or actI or actII or nextII[0] < NCH or nextII[1] < NCH:
            while pendI and len(actI) < 2 * (LA + 1) and pendI[0][0] <= min(nextII[0], nextII[1]) + LA:
                key = pendI.pop(0)
                actI.append((key, gen_I(*key)))
            for g in range(2):
                if g not in actII and nextII[g] < NCH and (nextII[g], g) in doneI:
                    actII[g] = gen_II(nextII[g], g)
            for g in list(actII.keys()):
                try:
                    next(actII[g])
                except StopIteration:
                    del actII[g]
                    nextII[g] += 1
            for item in list(actI):
                try:
                    next(item[1])
                except StopIteration:
                    actI.remove(item)
                    doneI.add(item[0])
        for u in range(4):
            ps = gen.next()
            P.pe.matmul(out=ps, lhsT=bones, rhs=yacc[:, u, :], start=True, stop=True)
            yc = sm_f.next()
            P.dve.scalar_tensor_tensor(out=yc, in0=ps, scalar=-1.0 / 64, in1=yacc[:, u, :], op0=ALU.mult, op1=ALU.add)
            sq_ = sm_f.next()
            P.act.activation(out=sq_, in_=yc, func=AF.Square)
            ps = gen.next()
            P.pe.matmul(out=ps, lhsT=bones, rhs=sq_, start=True, stop=True)
            rs = sm_f.next()
            P.act.activation(out=rs, in_=ps, func=AF.Ln, scale=1.0 / 64, bias=gneps[:, 0:1])
            P.act.activation(out=rs, in_=rs, func=AF.Exp, scale=-0.5)
            P.pool.tensor_tensor(out=yc, in0=yc, in1=rs, op=ALU.mult)
            P.dve.tensor_scalar(out=yc, in0=yc, scalar1=lnw[:, u:u + 1], scalar2=lnb[:, u:u + 1], op0=ALU.mult, op1=ALU.add)
            rk = stf.next()
            P.dve.scalar_tensor_tensor(out=rk, in0=r_[:, u, :], scalar=rkp[:, u:u + 1], in1=kh[:, u, :], op0=ALU.mult, op1=ALU.mult)
            ps = gen.next()
            P.pe.matmul(out=ps, lhsT=bones, rhs=rk, start=True, stop=True)
            P.dve.tensor_tensor(out=rk, in0=ps, in1=v_[:, u, :], op=ALU.mult)
            P.pool.tensor_tensor(out=yc, in0=yc, in1=rk, op=ALU.add)
            ob = stb.next()
            P.dve.tensor_tensor(out=ob, in0=yc, in1=g_[:, u, :], op=ALU.mult)
            P.act.dma_start(out=C.oT[3, u * 128:(u + 1) * 128, blk].fresh(), in_=ob)
    P.phase_end()


def load_w_big(P, wt, W, nrows, ncols):
    for c0 in range(0, ncols, 512):
        cw = min(512, ncols - c0)
        P.pool.dma_start(out=wt[:, :, c0:c0 + cw], in_=W[0:nrows, c0:c0 + cw].re("(k p) c -> p k c", p=128).fresh())


class WBig:
    def __init__(self, P, W, nrows, ncols, order=None):
        self.nk = nrows // 128
        self.tiles = {}
        pieces = list(range((ncols + 511) // 512))
        for pi in (order or pieces):
            c0 = pi * 512
            cw = min(512, ncols - c0)
            t = P.tile([128, self.nk, 512], BF16)
            P.pool.dma_start(out=t[:, :, :cw], in_=W[0:nrows, c0:c0 + cw].re("(k p) c -> p k c", p=128).fresh())
            self.tiles[pi] = t

    def sl(self, kc, c0, cs):
        return self.tiles[c0 // 512][:, kc, (c0 % 512):(c0 % 512) + cs]


def stage_merge(P, C, l):
    ident, ones = load_consts(P, C)
    w = C.w
    Pm = []
    for nm in ("w_branch_mla", "w_branch_hgrn", "w_branch_s5", "w_branch_rwkv"):
        Pm.append(WBig(P, w[nm][l], 512, 1024))
    wout = WBig(P, w["w_out"][l], 1024, 1024)
    xbp = tiles(P, 2, [128, 8, TB], F32)
    omp = tiles(P, 8, [128, 4, TB], BF16)
    ybp = tiles(P, 2, [128, 8, TB], BF16)
    gp = tiles(P, 6, [128, TB], BF16)
    accp = tiles(P, 2, [128, TB], F32)
    tmpp = tiles(P, 3, [128, TB], F32)
    pp = psums(P, 6, [128, TB])
    xTv = C.xT.re("k p t -> p k t")
    for j in range(NB):
        blk = slice(j * TB, (j + 1) * TB)
        xb = xbp.next()
        P.sp.dma_start(out=xb, in_=xTv[:, :, blk].fresh())
        om = []
        for m in range(4):
            t = omp.next()
            P.sp.dma_start(out=t, in_=C.oT[m, :, blk].re("(k p) t -> p k t", p=128).fresh())
            om.append(t)
        yb = ybp.next()
        for cb in range(8):
            acc = accp.next()
            for m in range(4):
                ps = pp.next()
                for kc in range(4):
                    P.pe.matmul(out=ps, lhsT=Pm[m].sl(kc, cb * 128, 128), rhs=om[m][:, kc, :],
                                start=(kc == 0), stop=(kc == 3))
                gt = gp.next()
                P.sp.dma_start(out=gt, in_=C.gT[m * 1024 + cb * 128:m * 1024 + (cb + 1) * 128, blk].fresh())
                if m == 0:
                    P.dve.tensor_tensor(out=acc, in0=ps, in1=gt, op=ALU.mult)
                else:
                    tm_ = tmpp.next()
                    P.dve.tensor_tensor(out=tm_, in0=ps, in1=gt, op=ALU.mult)
                    if m < 3:
                        P.pool.tensor_tensor(out=acc, in0=acc, in1=tm_, op=ALU.add)
                    else:
                        P.pool.tensor_tensor(out=yb[:, cb, :], in0=acc, in1=tm_, op=ALU.add)
        for cb2 in range(8):
            ps = pp.next()
            for kc in range(8):
                P.pe.matmul(out=ps, lhsT=wout.sl(kc, cb2 * 128, 128), rhs=yb[:, kc, :],
                            start=(kc == 0), stop=(kc == 7))
            P.dve.tensor_tensor(out=xb[:, cb2, :], in0=ps, in1=xb[:, cb2, :], op=ALU.add)
        P.act.dma_start(out=xTv[:, :, blk].fresh(), in_=xb)
    P.phase_end()


def stage_xattn(P, C, l):
    ident, ones = load_consts(P, C)
    w = C.w
    onesb = P.tile([128, 128], BF16)
    P.dve.memset(out=onesb, value=1.0)
    gm = P.tile([128, 8], F32)
    P.sp.dma_start(out=gm, in_=w["norm_xm"][l].re("(k p) -> p k", p=128).fresh())
    gq = P.tile([128, 8], F32)
    P.sp.dma_start(out=gq, in_=w["norm_xq"][l].re("(k p) -> p k", p=128).fresh())
    pp = psums(P, 7, [128, TB])
    sq = tiles(P, 2, [128, TB], F32)
    tmp = tiles(P, 3, [128, TB], F32)
    memT = P.tile([128, 8, 256], F32)
    mt_in = tiles(P, 2, [128, D], F32)
    for mt in range(2):
        t = mt_in.next()
        P.sp.dma_start(out=t, in_=C.mem[mt * 128:(mt + 1) * 128, :].fresh())
        for k in range(8):
            ps = pp.next()
            P.pe.transpose(out=ps[:, 0:128], in_=t[:, k * 128:(k + 1) * 128], identity=ident)
            P.act.copy(out=memT[:, k, mt * 128:(mt + 1) * 128], in_=ps[:, 0:128])
    hm = P.tile([128, 8, 256], BF16)
    rms_block(P, C, ones, memT, 8, gm, hm, pp, sq, tmp, n=256)
    wkv = w["xattn_w_kv"][l]
    kT = P.tile([128, 8, 256], BF16)
    vt = P.tile([128, 2, 1024], BF16)
    wpool = tiles(P, 2, [128, 8, 512], BF16)

    def evac_k(coff, cs, j, ps):
        P.act.copy(out=kT[:, coff // 128, :], in_=ps[:, :256])

    linear(P, wkv, D, 0, 1024, lambda kc, j: hm[:, kc, :], [0], evac_k, wpool, pp, n=256)
    for half in range(2):
        wt = wpool.next()
        load_w(P, wt, wkv, 0, D, 1024 + half * 512, 512)
        for mt in range(2):
            ps = pp.next()
            for kc in range(8):
                P.pe.matmul(out=ps, lhsT=hm[:, kc, mt * 128:(mt + 1) * 128], rhs=wt[:, kc, :], start=(kc == 0), stop=(kc == 7))
            P.dve.tensor_copy(out=vt[:, mt, half * 512:(half + 1) * 512], in_=ps)
    wq = WBig(P, w["xattn_w_q"][l], 1024, 1024)
    wo = WBig(P, w["xattn_w_o"][l], 1024, 1024)
    xbp = tiles(P, 2, [128, 8, TB], F32)
    hqp = tiles(P, 2, [128, 8, TB], BF16)
    qp = tiles(P, 2, [128, 8, TB], BF16)
    op_ = tiles(P, 2, [128, 8, TB], BF16)
    ep = tiles(P, 4, [128, TB], BF16)
    xTv = C.xT.re("k p t -> p k t")
    for j in range(NB):
        blk = slice(j * TB, (j + 1) * TB)
        xb = xbp.next()
        P.sp.dma_start(out=xb, in_=xTv[:, :, blk].fresh())
        hq = hqp.next()
        rms_block(P, C, ones, xb, 8, gq, hq, pp, sq, tmp)
        q = qp.next()
        for cb in range(8):
            ps = pp.next()
            for kc in range(8):
                P.pe.matmul(out=ps, lhsT=wq.sl(kc, cb * 128, 128), rhs=hq[:, kc, :], start=(kc == 0), stop=(kc == 7))
            if cb % 2 == 0:
                P.act.copy(out=q[:, cb, :], in_=ps)
            else:
                P.dve.tensor_copy(out=q[:, cb, :], in_=ps)
        o = op_.next()
        for h in range(4):
            es = []
            for mt in range(2):
                sps = pp.next()
                for c2 in range(2):
                    P.pe.matmul(out=sps, lhsT=kT[:, 2 * h + c2, mt * 128:(mt + 1) * 128], rhs=q[:, 2 * h + c2, :],
                                start=(c2 == 0), stop=(c2 == 1))
                e = ep.next()
                P.act.activation(out=e, in_=sps, func=AF.Exp, scale=1.0 / 16)
                es.append(e)
            dps = pp.next()
            for mt in range(2):
                P.pe.matmul(out=dps, lhsT=onesb, rhs=es[mt], start=(mt == 0), stop=(mt == 1))
            rden = tmp.next()
            P.dve.reciprocal(out=rden, in_=dps)
            for dv in range(2):
                ops_ = pp.next()
                for mt in range(2):
                    P.pe.matmul(out=ops_, lhsT=vt[:, mt, h * 256 + dv * 128:h * 256 + (dv + 1) * 128], rhs=es[mt],
                                start=(mt == 0), stop=(mt == 1))
                P.dve.tensor_tensor(out=o[:, 2 * h + dv, :], in0=ops_, in1=rden, op=ALU.mult)
        for cb2 in range(8):
            ps = pp.next()
            for kc in range(8):
                P.pe.matmul(out=ps, lhsT=wo.sl(kc, cb2 * 128, 128), rhs=o[:, kc, :], start=(kc == 0), stop=(kc == 7))
            P.dve.tensor_tensor(out=xb[:, cb2, :], in0=ps, in1=xb[:, cb2, :], op=ALU.add)
        P.act.dma_start(out=xTv[:, :, blk].fresh(), in_=xb)
    P.phase_end()


def stage_ffn(P, C, l):
    ident, ones = load_consts(P, C)
    w = C.w
    NF = D_FF // 128
    TF = 512
    wgu = WBig(P, w["ffn_w_gate_up"][l], 1024, 2 * D_FF, order=[0, 5, 6, 1, 7, 2, 8, 3, 9, 4, 10])
    wd = WBig(P, w["ffn_w_down"][l], D_FF, 1024)
    cw = []
    for wi in range(3):
        t_ = P.tile([128, NF], F32)
        P.sp.dma_start(out=t_, in_=w["ffn_conv_w"][l][wi].re("(k p) -> p k", p=128).fresh())
        cw.append(t_)
    cbias = P.tile([128, NF], F32)
    P.sp.dma_start(out=cbias, in_=w["ffn_conv_b"][l].re("(k p) -> p k", p=128).fresh())
    gn = P.tile([128, 8], F32)
    P.sp.dma_start(out=gn, in_=w["norm_ffn"][l].re("(k p) -> p k", p=128).fresh())
    halo = P.tile([128, NF, 2], F32)
    P.dve.memset(out=halo, value=0.0)
    xbp = tiles(P, 1, [128, 8, TF], F32)
    hp = tiles(P, 2, [128, 8, TF], BF16)
    ap_ = tiles(P, 1, [128, NF, TF], BF16)
    gxp = tiles(P, 2, [128, TF + 2], F32)
    tp = tiles(P, 3, [128, TF], F32)
    rxp = tiles(P, 3, [128, TF], F32)
    sq = tiles(P, 1, [128, TB], F32)
    tmp = tiles(P, 1, [128, TB], F32)
    pp = psums(P, 7, [128, TB])
    xTv = C.xT.re("k p t -> p k t")
    for j in range(T // TF):
        blk = slice(j * TF, (j + 1) * TF)
        xb = xbp.next()
        P.sp.dma_start(out=xb, in_=xTv[:, :, blk].fresh())
        h = hp.next()
        rms_block(P, C, ones, xb, 8, gn, h, pp, sq, tmp, n=TF)
        a = ap_.next()
        for cb in range(NF):
            gps = pp.next()
            for kc in range(8):
                P.pe.matmul(out=gps[:, :TF], lhsT=wgu.sl(kc, cb * 128, 128), rhs=h[:, kc, :], start=(kc == 0), stop=(kc == 7))
            ups = pp.next()
            for kc in range(8):
                P.pe.matmul(out=ups[:, :TF], lhsT=wgu.sl(kc, D_FF + cb * 128, 128), rhs=h[:, kc, :],
                            start=(kc == 0), stop=(kc == 7))
            gx = gxp.next()
            P.act.copy(out=gx[:, 2:TF + 2], in_=gps[:, :TF])
            P.pool.tensor_copy(out=gx[:, 0:2], in_=halo[:, cb, :])
            P.pool.tensor_copy(out=halo[:, cb, :], in_=gx[:, TF:TF + 2])
            t = tp.next()
            P.dve.tensor_scalar(out=t, in0=gx[:, 2:TF + 2], scalar1=cw[2][:, cb:cb + 1], scalar2=cbias[:, cb:cb + 1],
                                op0=ALU.mult, op1=ALU.add)
            P.dve.scalar_tensor_tensor(out=t, in0=gx[:, 1:TF + 1], scalar=cw[1][:, cb:cb + 1], in1=t, op0=ALU.mult, op1=ALU.add)
            P.dve.scalar_tensor_tensor(out=t, in0=gx[:, 0:TF], scalar=cw[0][:, cb:cb + 1], in1=t, op0=ALU.mult, op1=ALU.add)
            sl = tp.next()
            P.act.activation(out=sl, in_=t, func=AF.Silu)
            P.dve.tensor_tensor(out=a[:, cb, :], in0=ups[:, :TF], in1=sl, op=ALU.mult)
        for cb2 in range(8):
            ps = pp.next()
            for kc in range(NF):
                P.pe.matmul(out=ps[:, :TF], lhsT=wd.sl(kc, cb2 * 128, 128), rhs=a[:, kc, :], start=(kc == 0), stop=(kc == NF - 1))
            rx = rxp.next()
            P.sp.dma_start(out=rx, in_=C.xT[cb2, :, blk].fresh())
            P.dve.tensor_tensor(out=rx, in0=ps[:, :TF], in1=rx, op=ALU.add)
            P.act.dma_start(out=C.xT[cb2, :, blk].fresh(), in_=rx)
    P.phase_end()


def stage_final(P, C):
    ident, ones = load_consts(P, C)
    gf = P.tile([128, 8], F32)
    P.sp.dma_start(out=gf, in_=C.w["norm_final"].re("(k p) -> p k", p=128).fresh())
    xbp = tiles(P, 2, [128, 8, TB], F32)
    hfp = tiles(P, 2, [128, 8, TB], F32)
    otp = tiles(P, 2, [128, D], F32)
    sq = tiles(P, 2, [128, TB], F32)
    tmp = tiles(P, 2, [128, TB], F32)
    pp = psums(P, 2, [128, TB])
    pt = psums(P, 4, [128, 4, 128])
    xTv = C.xT.re("k p t -> p k t")
    for j in range(NB):
        blk = slice(j * TB, (j + 1) * TB)
        xb = xbp.next()
        P.sp.dma_start(out=xb, in_=xTv[:, :, blk].fresh())
        hf = hfp.next()
        rms_block(P, C, ones, xb, 8, gf, hf, pp, sq, tmp)
        for tt in range(4):
            ot = otp.next()
            for half in range(2):
                ps = pt.next()
                for q in range(4):
                    k = half * 4 + q
                    P.pe.transpose(out=ps[:, q, :], in_=hf[:, k, tt * 128:(tt + 1) * 128], identity=ident)
                if half == 0:
                    P.act.copy(out=ot[:, 0:512], in_=ps.re("p a b -> p (a b)"))
                else:
                    P.dve.tensor_copy(out=ot[:, 512:1024], in_=ps.re("p a b -> p (a b)"))
            r0 = j * TB + tt * 128
            P.act.dma_start(out=C.out[r0:r0 + 128, :], in_=ot)
    P.phase_end()


def build(opts):
    nc = bass.Bass("TRN2", target_bir_lowering=False)
    outer = ExitStack()
    with outer:
        outer.enter_context(nc.allow_non_contiguous_dma(reason="small parameter vectors"))
        P = Prog(nc, outer)
        C = Ctx()
        C.opts = opts
        C.x = P.dram("x", [T, D], F32, kind="ExternalInput")
        C.mem = P.dram("mem", [256, D], F32, kind="ExternalInput")
        C.pos = P.dram("pos", [1, T], I32, kind="ExternalInput")
        C.w = {}
        for name, shape in opts["wshapes"].items():
            C.w[name] = P.dram(name, shape, F32, kind="ExternalInput")
        C.ident_d = P.dram("ident", [128, 128], F32, kind="ExternalInput")
        dbg = opts.get("debug", ())
        kd = lambda n: ("ExternalOutput" if n in dbg else "Internal")
        C.xT = P.dram("xT", [8, 128, T], F32, kind=kd("xT"))
        C.pT = P.dram("pT", [NMIX + 32, T], F32, kind=kd("pT"))
        C.gT = P.dram("gT", [4096, T], BF16, kind=kd("gT"))
        C.out = P.dram("out", [T, D], F32, kind="ExternalOutput")
        C.oT = P.dram("oT", [4, 512, T], BF16, kind=kd("oT"))
        C.cosT = P.dram("cosT", [128, T], F32, kind=kd("cosT"))
        C.sinT = P.dram("sinT", [128, T], F32, kind=kd("sinT"))
        C.tri_d = P.dram("tri", [128, 128], F32, kind="ExternalInput")
        C.ropec_d = P.dram("ropec", [128, 2], F32, kind="ExternalInput")
        C.bones_d = P.dram("bones", [128, 128], F32, kind="ExternalInput")
        C.mask4_d = P.dram("mask4", [64, 256], F32, kind="ExternalInput")
        C.mls_d = P.dram("mls", [64, 64], F32, kind="ExternalInput")
        C.vfT = P.dram("vfT", [512, T], F32, kind=kd("vfT"))
        C.maskB_d = P.dram("maskB", [4, 128, 128], F32, kind="ExternalInput")
        C.maskC_d = P.dram("maskC", [4, 128, 128], F32, kind="ExternalInput")
        P.ph = outer
        C.eps_t = P.tile([128, 1], F32)
        P.ph = ExitStack()
        P.dve.memset(out=C.eps_t, value=EPS)
        if not opts.get("skip_pre"):
            stage_x_in(P, C)
            stage_rope(P, C)
        nl = opts.get("layers", DEPTH)
        for l in range(nl):
            if not opts.get("skip_pre"):
                stage_sweep(P, C, l)
            if opts.get("stop") == f"sweep{l}":
                break
            if "mla" in opts.get("mixers", "mla,hgrn,s5,rwkv"):
                stage_mla(P, C, l)
            if "hgrn" in opts.get("mixers", "mla,hgrn,s5,rwkv"):
                stage_hgrn(P, C, l)
            if "s5" in opts.get("mixers", "mla,hgrn,s5,rwkv"):
                stage_s5(P, C, l)
            if "rwkv" in opts.get("mixers", "mla,hgrn,s5,rwkv"):
                stage_rwkv(P, C, l)
            if opts.get("stop") == f"mix{l}":
                break
            stage_merge(P, C, l)
            if opts.get("stop") == f"merge{l}":
                break
            stage_xattn(P, C, l)
            if opts.get("stop") == f"xattn{l}":
                break
            stage_ffn(P, C, l)
            if opts.get("stop") == f"ffn{l}":
                break
        if not opts.get("stop"):
            stage_final(P, C)
        P.phase_end()
    return nc


_CACHE = {}


def make_inputs(inputs=None):
    consts = {"ident": np.eye(128, dtype=np.float32)}
    consts["tri"] = np.triu(np.ones((128, 128), np.float32))
    invf = (10000.0 ** (-np.arange(32, dtype=np.float32) / 32)).astype(np.float32)
    rc = np.zeros((128, 2), np.float32)
    for p in range(128):
        rc[p, 0] = invf[p % 32]
        rc[p, 1] = -1.0 if (p // 32) % 2 == 0 else 1.0
    consts["ropec"] = rc
    bo = np.zeros((128, 128), np.float32)
    bo[:64, :64] = 1.0
    bo[64:, 64:] = 1.0
    consts["bones"] = bo
    st_ = np.triu(np.ones((64, 64), np.float32), 1)
    in_ = np.triu(np.ones((64, 64), np.float32), 0)
    consts["mask4"] = np.concatenate([st_, in_, st_, in_], axis=1)
    consts["mls"] = np.tril(np.ones((64, 64), np.float32), -1)
    mB = np.zeros((4, 128, 128), np.float32)
    for pr in range(4):
        for g2 in range(2):
            g8 = 2 * pr + g2
            mB[pr, g2 * 64:(g2 + 1) * 64, g8 * 16:(g8 + 1) * 16] = 1.0
    consts["maskB"] = mB
    consts["maskC"] = np.ascontiguousarray(mB.transpose(0, 2, 1))
    return consts


def kernel(**inputs):
    wnames = [k for k in inputs if k not in ("x", "mem", "positions")]
    wshapes = {k: list(np.asarray(inputs[k]).shape) for k in wnames}
    if "nc" not in _CACHE:
        _CACHE["nc"] = build({"wshapes": wshapes})
    nc = _CACHE["nc"]
    consts = make_inputs()
    wts = {k: np.ascontiguousarray(np.asarray(inputs[k], dtype=np.float32)) for k in wnames}
    x = np.asarray(inputs["x"], dtype=np.float32)
    mem = np.asarray(inputs["mem"], dtype=np.float32)
    pos = np.asarray(inputs["positions"]).astype(np.int32)
    n = 8
    in_maps = []
    for c in range(n):
        b = c % 4
        m = {"x": np.ascontiguousarray(x[b]), "mem": np.ascontiguousarray(mem[b]),
             "pos": np.ascontiguousarray(pos[b][None, :])}
        m.update(wts)
        m.update(consts)
        in_maps.append(m)
    res = run_bass_kernel_spmd(nc, in_maps, core_ids=list(range(n)))
    out = np.stack([np.asarray(res.results[b]["out"], dtype=np.float32) for b in range(4)], axis=0)
    return out
```
